# Optimizing a Trainium2 kernel written in Bass

```python
import jax, jax.numpy as jnp
from jax import lax
import numpy as np

D_MODEL = 1024
BATCH = 8
SEQ = 2048
DEPTH = 4
DEC_BATCH = 128
DEC_SEQ = 1
PAST_LEN = 16384
PAGE_SIZE = 128

N_MIXERS = 3
N_RGLRU = len(range(0, DEPTH, N_MIXERS))
N_GDN = len(range(1, DEPTH, N_MIXERS))
N_SSD = len(range(2, DEPTH, N_MIXERS))
CONV_W = 4
EPS = 1e-6
MLP_HIDDEN = 4 * D_MODEL
RG_WIDTH = D_MODEL
RG_BLOCK = 256
RG_BLOCKS = RG_WIDTH // RG_BLOCK
RG_C = 8.0
GDN_DK = 128
GDN_DV = 128
GDN_HEADS = D_MODEL // GDN_DK
GDN_QKV = GDN_HEADS * (2 * GDN_DK + GDN_DV)
GDN_IN = GDN_QKV + GDN_HEADS * GDN_DV + 2 * GDN_HEADS
GDN_CHUNK = 64
SSM_D_INNER = 2 * D_MODEL
SSM_HEADDIM = 64
SSM_HEADS = SSM_D_INNER // SSM_HEADDIM
SSM_STATE = 128
SSM_GROUPS = 4
SSM_HPG = SSM_HEADS // SSM_GROUPS
SSM_CONV_DIM = SSM_D_INNER + 2 * SSM_GROUPS * SSM_STATE
SSM_IN = SSM_D_INNER + SSM_CONV_DIM + SSM_HEADS
SSM_CHUNK = 64

kernel_name = 'hybrid_rglru_gdn_ssd_adaln_step'

F32 = jnp.float32


def rmsnorm(x):
    xf = x.astype(F32)
    return (xf * lax.rsqrt(jnp.mean(xf * xf, axis=-1, keepdims=True) + EPS)).astype(x.dtype)


def l2norm(x):
    xf = x.astype(F32)
    return xf * lax.rsqrt(jnp.sum(xf * xf, axis=-1, keepdims=True) + EPS)


def causal_conv(x, buf, w, b=None):
    L = x.shape[1]
    xp = jnp.concatenate([buf.astype(x.dtype), x], axis=1)
    y = xp[:, 0:L] * w[0]
    for k in range(1, CONV_W):
        y = y + xp[:, k:k + L] * w[k]
    if b is not None:
        y = y + b
    return y, xp[:, L:]


def lin_combine(left, right):
    a_l, b_l = left
    a_r, b_r = right
    return a_l * a_r, a_r * b_l + b_r


def rglru_mixer(u, conv_buf, h0, w_in, conv_w, conv_b, gate_w, gate_b, lam, w_out):
    Bsz, L, _ = u.shape
    proj = u @ w_in
    y_br = jax.nn.gelu(proj[..., :RG_WIDTH])
    x_br, new_buf = causal_conv(proj[..., RG_WIDTH:], conv_buf, conv_w, conv_b)
    xb = x_br.reshape(Bsz, L, RG_BLOCKS, RG_BLOCK)
    gates = jnp.einsum('blnj,gnjk->gblnk', xb, gate_w).reshape(2, Bsz, L, RG_WIDTH) + gate_b[:, None, None, :]
    r = jax.nn.sigmoid(gates[0].astype(F32))
    i = jax.nn.sigmoid(gates[1].astype(F32))
    log_a = RG_C * r * jax.nn.log_sigmoid(lam.astype(F32))
    a = jnp.exp(log_a)
    bx = jnp.sqrt(-jnp.expm1(2.0 * log_a)) * i * x_br.astype(F32)
    a_cum, b_cum = lax.associative_scan(lin_combine, (a, bx), axis=1)
    h = a_cum * h0.astype(F32)[:, None, :] + b_cum
    out = (h.astype(u.dtype) * y_br) @ w_out
    return out, new_buf, h[:, -1]


def gated_delta_chunked(q, k, v, g, beta, S0):
    Bsz, L, H, DK = q.shape
    DV = v.shape[-1]
    C = GDN_CHUNK if L % GDN_CHUNK == 0 else L
    n = L // C

    def chunks(t):
        return t.astype(F32).reshape(Bsz, n, C, H, -1).transpose(0, 3, 1, 2, 4)

    q, k, v = chunks(q), chunks(k), chunks(v)
    g = g.reshape(Bsz, n, C, H).transpose(0, 3, 1, 2)
    beta = beta.reshape(Bsz, n, C, H).transpose(0, 3, 1, 2)
    G = jnp.cumsum(g, axis=-1)
    causal = jnp.tril(jnp.ones((C, C), dtype=bool))
    strict = jnp.tril(jnp.ones((C, C), dtype=bool), -1)
    diff = G[..., :, None] - G[..., None, :]
    decay = jnp.where(causal, jnp.exp(jnp.where(causal, diff, 0.0)), 0.0)
    kb = k * beta[..., None]
    A = jnp.where(strict, jnp.einsum('bhnik,bhnjk->bhnij', kb, k) * decay, 0.0)
    lhs = A + jnp.eye(C, dtype=F32)
    rhs = jnp.concatenate([v * beta[..., None], kb * jnp.exp(G)[..., None]], axis=-1)
    sol = lax.linalg.triangular_solve(lhs, rhs, left_side=True, lower=True)
    U, W = sol[..., :DV], sol[..., DV:]
    qk = jnp.einsum('bhnik,bhnjk->bhnij', q, k) * decay
    q_dec = q * jnp.exp(G)[..., None]
    k_dec = k * jnp.exp(G[..., -1:] - G)[..., None]
    g_last = jnp.exp(G[..., -1])

    def step(S, inp):
        U_c, W_c, qk_c, qd_c, kd_c, gl_c = inp
        v_new = U_c - jnp.einsum('bhck,bhkv->bhcv', W_c, S)
        o_c = jnp.einsum('bhck,bhkv->bhcv', qd_c, S) + jnp.einsum('bhij,bhjv->bhiv', qk_c, v_new)
        S = gl_c[..., None, None] * S + jnp.einsum('bhck,bhcv->bhkv', kd_c, v_new)
        return S, o_c

    mv = lambda t: jnp.moveaxis(t, 2, 0)
    S_fin, o = lax.scan(step, S0.astype(F32), (mv(U), mv(W), mv(qk), mv(q_dec), mv(k_dec), mv(g_last)))
    o = o.transpose(1, 0, 3, 2, 4).reshape(Bsz, L, H, DV)
    return o, S_fin


def gdn_mixer(u, conv_buf, S0, w_in, conv_w, A_log, dt_bias, norm_g, w_out):
    Bsz, L, _ = u.shape
    HK = GDN_HEADS * GDN_DK
    HV = GDN_HEADS * GDN_DV
    proj = u @ w_in
    qkv = proj[..., :GDN_QKV]
    gate = proj[..., GDN_QKV:GDN_QKV + HV]
    a_in = proj[..., GDN_QKV + HV:GDN_QKV + HV + GDN_HEADS]
    b_in = proj[..., GDN_QKV + HV + GDN_HEADS:]
    qkv, new_buf = causal_conv(qkv, conv_buf, conv_w)
    qkv = jax.nn.silu(qkv)
    q = l2norm(qkv[..., :HK].reshape(Bsz, L, GDN_HEADS, GDN_DK)) * (GDN_DK ** -0.5)
    k = l2norm(qkv[..., HK:2 * HK].reshape(Bsz, L, GDN_HEADS, GDN_DK))
    v = qkv[..., 2 * HK:].reshape(Bsz, L, GDN_HEADS, GDN_DV)
    beta = jax.nn.sigmoid(b_in.astype(F32))
    g = -jnp.exp(A_log.astype(F32)) * jax.nn.softplus(a_in.astype(F32) + dt_bias)
    o, S_fin = gated_delta_chunked(q, k, v, g, beta, S0)
    o = rmsnorm(o.astype(u.dtype)) * norm_g * jax.nn.silu(gate.reshape(Bsz, L, GDN_HEADS, GDN_DV))
    return o.reshape(Bsz, L, HV) @ w_out, new_buf, S_fin


def ssd_chunked(x, dt, A, Bm, Cm, h0):
    Bsz, L, H, P = x.shape
    C = SSM_CHUNK if L % SSM_CHUNK == 0 else L
    n = L // C
    G, Hg, N = SSM_GROUPS, SSM_HPG, SSM_STATE
    x = x.astype(F32).reshape(Bsz, n, C, G, Hg, P)
    dt = dt.reshape(Bsz, n, C, G, Hg)
    Bm = Bm.astype(F32).reshape(Bsz, n, C, G, N)
    Cm = Cm.astype(F32).reshape(Bsz, n, C, G, N)
    a = (dt * A.reshape(G, Hg)).transpose(0, 1, 3, 4, 2)
    Acs = jnp.cumsum(a, axis=-1)
    causal = jnp.tril(jnp.ones((C, C), dtype=bool))
    seg = Acs[..., :, None] - Acs[..., None, :]
    Lmat = jnp.where(causal, jnp.exp(jnp.where(causal, seg, 0.0)), 0.0)
    xdt = x * dt[..., None]
    CB = jnp.einsum('bnlgd,bnsgd->bngls', Cm, Bm)
    y_diag = jnp.einsum('bnghls,bnsghp->bnlghp', CB[:, :, :, None] * Lmat, xdt)
    decay_states = jnp.exp(Acs[..., -1:] - Acs)
    states = jnp.einsum('bnsgd,bnghs,bnsghp->bnghpd', Bm, decay_states, xdt)
    chunk_decay = jnp.exp(Acs[..., -1])

    def step(h, inp):
        dec_c, st_c = inp
        return dec_c[..., None, None] * h + st_c, h

    h_fin, h_prev = lax.scan(step, h0.astype(F32).reshape(Bsz, G, Hg, P, N),
                             (jnp.moveaxis(chunk_decay, 1, 0), jnp.moveaxis(states, 1, 0)))
    h_prev = jnp.moveaxis(h_prev, 0, 1)
    y_off = jnp.einsum('bnlgd,bnghpd,bnghl->bnlghp', Cm, h_prev, jnp.exp(Acs))
    y = (y_diag + y_off).reshape(Bsz, L, H, P)
    return y, h_fin.reshape(Bsz, H, P, N)


def ssd_mixer(u, conv_buf, h0, w_in, conv_w, conv_b, A_log, dt_bias, D_skip, norm_g, w_out):
    Bsz, L, _ = u.shape
    GN = SSM_GROUPS * SSM_STATE
    proj = u @ w_in
    z = proj[..., :SSM_D_INNER]
    xBC = proj[..., SSM_D_INNER:SSM_D_INNER + SSM_CONV_DIM]
    dt_in = proj[..., SSM_D_INNER + SSM_CONV_DIM:]
    xBC, new_buf = causal_conv(xBC, conv_buf, conv_w, conv_b)
    xBC = jax.nn.silu(xBC)
    xs = xBC[..., :SSM_D_INNER].reshape(Bsz, L, SSM_HEADS, SSM_HEADDIM)
    Bm = xBC[..., SSM_D_INNER:SSM_D_INNER + GN].reshape(Bsz, L, SSM_GROUPS, SSM_STATE)
    Cm = xBC[..., SSM_D_INNER + GN:].reshape(Bsz, L, SSM_GROUPS, SSM_STATE)
    dt = jax.nn.softplus(dt_in.astype(F32) + dt_bias)
    A = -jnp.exp(A_log.astype(F32))
    y, h_fin = ssd_chunked(xs, dt, A, Bm, Cm, h0)
    y = y + D_skip[:, None] * xs.astype(F32)
    y = (y.reshape(Bsz, L, SSM_D_INNER) * jax.nn.silu(z.astype(F32))).astype(u.dtype)
    y = rmsnorm(y.reshape(Bsz, L, SSM_GROUPS, SSM_D_INNER // SSM_GROUPS)).reshape(Bsz, L, SSM_D_INNER) * norm_g
    return y @ w_out, new_buf, h_fin


def sq_relu_mlp(u, w_up, w_down):
    h = jax.nn.relu(u @ w_up)
    return (h * h) @ w_down


def trunk(x, c, rg_conv, rg_h, gdn_conv, gdn_S, ssd_conv, ssd_h, p):
    new_rg_conv, new_rg_h, new_gdn_conv, new_gdn_S, new_ssd_conv, new_ssd_h = [], [], [], [], [], []
    for i in range(DEPTH):
        j = i // N_MIXERS
        mod = jax.nn.silu(c) @ p['w_mod'][i] + p['b_mod'][i]
        sh1, sc1, g1, sh2, sc2, g2 = jnp.split(mod[:, None, :], 6, axis=-1)
        u = rmsnorm(x) * (1.0 + sc1) + sh1
        kind = i % N_MIXERS
        if kind == 0:
            out, cb, st = rglru_mixer(u, rg_conv[j], rg_h[j], p['rg_w_in'][j], p['rg_conv_w'][j], p['rg_conv_b'][j],
                                      p['rg_gate_w'][j], p['rg_gate_b'][j], p['rg_lambda'][j], p['rg_w_out'][j])
            new_rg_conv.append(cb)
            new_rg_h.append(st)
        elif kind == 1:
            out, cb, st = gdn_mixer(u, gdn_conv[j], gdn_S[j], p['gdn_w_in'][j], p['gdn_conv_w'][j], p['gdn_A_log'][j],
                                    p['gdn_dt_bias'][j], p['gdn_norm_g'][j], p['gdn_w_out'][j])
            new_gdn_conv.append(cb)
            new_gdn_S.append(st)
        else:
            out, cb, st = ssd_mixer(u, ssd_conv[j], ssd_h[j], p['ssd_w_in'][j], p['ssd_conv_w'][j], p['ssd_conv_b'][j],
                                    p['ssd_A_log'][j], p['ssd_dt_bias'][j], p['ssd_D'][j], p['ssd_norm_g'][j], p['ssd_w_out'][j])
            new_ssd_conv.append(cb)
            new_ssd_h.append(st)
        x = x + g1 * out
        u = rmsnorm(x) * (1.0 + sc2) + sh2
        x = x + g2 * sq_relu_mlp(u, p['w_mlp_up'][i], p['w_mlp_down'][i])
    y = rmsnorm(x) * p['final_norm_g']
    return (y, jnp.stack(new_rg_conv), jnp.stack(new_rg_h), jnp.stack(new_gdn_conv), jnp.stack(new_gdn_S),
            jnp.stack(new_ssd_conv), jnp.stack(new_ssd_h))


def setup_inputs(seed: int = 0) -> dict:
    key = jax.random.key(seed)
    ks = iter(jax.random.split(key, 48))

    def nrm(shape, scale):
        return jax.random.normal(next(ks), shape, F32) * scale

    def unif(shape, lo, hi):
        return jax.random.uniform(next(ks), shape, F32, lo, hi)

    def dt_bias_init(shape):
        dt = jnp.exp(unif(shape, float(np.log(1e-3)), float(np.log(1e-1))))
        return dt + jnp.log(-jnp.expm1(-dt))

    D = D_MODEL
    s_lam = unif((N_RGLRU, RG_WIDTH), 0.9, 0.999)
    return {
        'x_prompt': nrm((BATCH, SEQ, D), 1.0),
        'x_sample': nrm((DEC_BATCH, DEC_SEQ, D), 1.0),
        'state_rglru_conv': nrm((N_RGLRU, DEC_BATCH, CONV_W - 1, RG_WIDTH), 1.0),
        'state_rglru_h': nrm((N_RGLRU, DEC_BATCH, RG_WIDTH), 0.5),
        'state_gdn_conv': nrm((N_GDN, DEC_BATCH, CONV_W - 1, GDN_QKV), 1.0),
        'state_gdn_S': nrm((N_GDN, DEC_BATCH, GDN_HEADS, GDN_DK, GDN_DV), 0.1),
        'state_ssd_conv': nrm((N_SSD, DEC_BATCH, CONV_W - 1, SSM_CONV_DIM), 1.0),
        'state_ssd_h': nrm((N_SSD, DEC_BATCH, SSM_HEADS, SSM_HEADDIM, SSM_STATE), 0.1),
        'c_prompt': nrm((BATCH, D), 1.0),
        'c_sample': nrm((DEC_BATCH, D), 1.0),
        'w_mod': nrm((DEPTH, D, 6 * D), D ** -0.5),
        'b_mod': nrm((DEPTH, 6 * D), 0.02),
        'w_mlp_up': nrm((DEPTH, D, MLP_HIDDEN), D ** -0.5),
        'w_mlp_down': nrm((DEPTH, MLP_HIDDEN, D), MLP_HIDDEN ** -0.5),
        'final_norm_g': 1.0 + nrm((D,), 0.02),
        'rg_w_in': nrm((N_RGLRU, D, 2 * RG_WIDTH), D ** -0.5),
        'rg_conv_w': nrm((N_RGLRU, CONV_W, RG_WIDTH), CONV_W ** -0.5),
        'rg_conv_b': nrm((N_RGLRU, RG_WIDTH), 0.02),
        'rg_gate_w': nrm((N_RGLRU, 2, RG_BLOCKS, RG_BLOCK, RG_BLOCK), RG_BLOCK ** -0.5),
        'rg_gate_b': nrm((N_RGLRU, 2, RG_WIDTH), 0.02),
        'rg_lambda': jnp.log(s_lam) - jnp.log1p(-s_lam),
        'rg_w_out': nrm((N_RGLRU, RG_WIDTH, D), RG_WIDTH ** -0.5),
        'gdn_w_in': nrm((N_GDN, D, GDN_IN), D ** -0.5),
        'gdn_conv_w': nrm((N_GDN, CONV_W, GDN_QKV), CONV_W ** -0.5),
        'gdn_A_log': jnp.log(unif((N_GDN, GDN_HEADS), 1.0, 16.0)),
        'gdn_dt_bias': dt_bias_init((N_GDN, GDN_HEADS)),
        'gdn_norm_g': 1.0 + nrm((N_GDN, GDN_DV), 0.02),
        'gdn_w_out': nrm((N_GDN, GDN_HEADS * GDN_DV, D), (GDN_HEADS * GDN_DV) ** -0.5),
        'ssd_w_in': nrm((N_SSD, D, SSM_IN), D ** -0.5),
        'ssd_conv_w': nrm((N_SSD, CONV_W, SSM_CONV_DIM), CONV_W ** -0.5),
        'ssd_conv_b': nrm((N_SSD, SSM_CONV_DIM), 0.02),
        'ssd_A_log': jnp.log(unif((N_SSD, SSM_HEADS), 1.0, 16.0)),
        'ssd_dt_bias': dt_bias_init((N_SSD, SSM_HEADS)),
        'ssd_D': 1.0 + nrm((N_SSD, SSM_HEADS), 0.02),
        'ssd_norm_g': 1.0 + nrm((N_SSD, SSM_D_INNER), 0.02),
        'ssd_w_out': nrm((N_SSD, SSM_D_INNER, D), SSM_D_INNER ** -0.5),
    }


def reference(x_prompt, x_sample, state_rglru_conv, state_rglru_h, state_gdn_conv, state_gdn_S,
              state_ssd_conv, state_ssd_h, c_prompt, c_sample, w_mod, b_mod, w_mlp_up, w_mlp_down,
              final_norm_g, rg_w_in, rg_conv_w, rg_conv_b, rg_gate_w, rg_gate_b, rg_lambda, rg_w_out,
              gdn_w_in, gdn_conv_w, gdn_A_log, gdn_dt_bias, gdn_norm_g, gdn_w_out,
              ssd_w_in, ssd_conv_w, ssd_conv_b, ssd_A_log, ssd_dt_bias, ssd_D, ssd_norm_g, ssd_w_out):
    p = dict(w_mod=w_mod, b_mod=b_mod, w_mlp_up=w_mlp_up, w_mlp_down=w_mlp_down, final_norm_g=final_norm_g,
             rg_w_in=rg_w_in, rg_conv_w=rg_conv_w, rg_conv_b=rg_conv_b, rg_gate_w=rg_gate_w, rg_gate_b=rg_gate_b,
             rg_lambda=rg_lambda, rg_w_out=rg_w_out, gdn_w_in=gdn_w_in, gdn_conv_w=gdn_conv_w, gdn_A_log=gdn_A_log,
             gdn_dt_bias=gdn_dt_bias, gdn_norm_g=gdn_norm_g, gdn_w_out=gdn_w_out, ssd_w_in=ssd_w_in,
             ssd_conv_w=ssd_conv_w, ssd_conv_b=ssd_conv_b, ssd_A_log=ssd_A_log, ssd_dt_bias=ssd_dt_bias,
             ssd_D=ssd_D, ssd_norm_g=ssd_norm_g, ssd_w_out=ssd_w_out)
    Bp = x_prompt.shape[0]
    dtp = x_prompt.dtype
    z_rg_conv = jnp.zeros((N_RGLRU, Bp, CONV_W - 1, RG_WIDTH), dtp)
    z_rg_h = jnp.zeros((N_RGLRU, Bp, RG_WIDTH), dtp)
    z_gdn_conv = jnp.zeros((N_GDN, Bp, CONV_W - 1, GDN_QKV), dtp)
    z_gdn_S = jnp.zeros((N_GDN, Bp, GDN_HEADS, GDN_DK, GDN_DV), dtp)
    z_ssd_conv = jnp.zeros((N_SSD, Bp, CONV_W - 1, SSM_CONV_DIM), dtp)
    z_ssd_h = jnp.zeros((N_SSD, Bp, SSM_HEADS, SSM_HEADDIM, SSM_STATE), dtp)
    (y_prompt, p_rg_conv, p_rg_h, p_gdn_conv, p_gdn_S, p_ssd_conv, p_ssd_h) = trunk(
        x_prompt, c_prompt, z_rg_conv, z_rg_h, z_gdn_conv, z_gdn_S, z_ssd_conv, z_ssd_h, p)
    (y_sample, s_rg_conv, s_rg_h, s_gdn_conv, s_gdn_S, s_ssd_conv, s_ssd_h) = trunk(
        x_sample, c_sample, state_rglru_conv, state_rglru_h, state_gdn_conv, state_gdn_S,
        state_ssd_conv, state_ssd_h, p)
    return (y_prompt, y_sample, p_rg_conv, p_rg_h, p_gdn_conv, p_gdn_S, p_ssd_conv, p_ssd_h,
            s_rg_conv, s_rg_h, s_gdn_conv, s_gdn_S, s_ssd_conv, s_ssd_h)
```

```python
import os
import math
import numpy as np
import concourse.bass as bass
import concourse.mybir as mybir
from concourse.bass_utils import run_bass_kernel_spmd

F32 = mybir.dt.float32
BF16 = mybir.dt.bfloat16
AF = mybir.ActivationFunctionType
ALU = mybir.AluOpType
AX = mybir.AxisListType

NCORES = 8
D = 1024
KT = 8
LP = 2048
NS = 16
NT = LP + NS
EPS = 1e-6


class Reg:
    __slots__ = ("name", "lw", "rd", "excl")

    def __init__(self, name, excl=False):
        self.name = name
        self.lw = None
        self.rd = {}
        self.excl = excl


class KB:
    ENGS = ("pe", "dve", "act", "pool", "sp")
    QUEUES = ("sp", "pool")

    def __init__(self, nc):
        self.nc = nc
        self.prog = {e: [] for e in self.ENGS}
        self.sems = []
        self.cur = {}
        self.cnt = {}
        for e in self.ENGS:
            self._new_sem(e)
        self.seen = {e: {} for e in self.ENGS}
        self.dslots = {}
        self.dnext = {}
        for q, n in (("sp", 12), ("pool", 6)):
            self.dslots[q] = [[self._alloc_sem(f"d_{q}{i}"), 0] for i in range(n)]
            self.dnext[q] = 0
        self.nops = 0
        self.needed = set()

    def _alloc_sem(self, name):
        self.sems.append(self.nc.alloc_semaphore(name))
        return len(self.sems) - 1

    def _new_sem(self, e):
        self.cur[e] = self._alloc_sem(f"e_{e}{len(self.sems)}")
        self.cnt[e] = 0

    def _deps(self, e, r, w, pe_skip=True):
        deps = []
        for x in r:
            if x.lw is not None:
                deps.append(x.lw)
            if x.excl:
                for (oe, t) in x.rd.items():
                    if oe != e:
                        deps.append(t)
        for x in w:
            if x.lw is not None and x.lw[0] != e:
                deps.append(x.lw)
            for (oe, t) in x.rd.items():
                if oe != e:
                    deps.append(t)
        waits = []
        sn = self.seen[e]
        for (oe, sk, val) in deps:
            if oe == e and e == "pe":
                continue
            if sn.get(sk, 0) >= val:
                continue
            sn[sk] = val
            waits.append((sk, val))
            if oe is not None:
                self.needed.add((sk, val))
        return waits

    def op(self, e, fn, r=(), w=()):
        waits = self._deps(e, r, w)
        if self.cnt[e] >= 30000:
            self._new_sem(e)
        self.cnt[e] += 1
        tok = (e, self.cur[e], self.cnt[e])
        self.prog[e].append((waits, fn, (self.cur[e], 1)))
        for x in r:
            x.rd[e] = tok
        for x in w:
            x.lw = tok
            x.rd = {}
        self.nops += 1
        return tok

    def dma(self, q, fns, r=(), w=()):
        if not isinstance(fns, (list, tuple)):
            fns = [fns]
        waits = self._deps(q + "_dma", r, w) if False else None
        deps = []
        for x in r:
            if x.lw is not None:
                deps.append(x.lw)
        for x in w:
            if x.lw is not None:
                deps.append(x.lw)
            deps.extend(x.rd.values())
        slot = self.dslots[q][self.dnext[q]]
        self.dnext[q] = (self.dnext[q] + 1) % len(self.dslots[q])
        if slot[1] > 0:
            deps.append((None, slot[0], slot[1]))
        waits = []
        sn = self.seen[q]
        for (oe, sk, val) in deps:
            if sn.get(sk, 0) >= val:
                continue
            sn[sk] = val
            waits.append((sk, val))
            if oe is not None:
                self.needed.add((sk, val))
        for i, fn in enumerate(fns):
            slot[1] += 16
            self.prog[q].append((waits if i == 0 else [], fn, (slot[0], 16)))
        tok = (None, slot[0], slot[1])
        for x in r:
            x.rd["dma%d" % slot[0]] = tok
        for x in w:
            x.lw = tok
            x.rd = {}
        return tok

    def barrier(self):
        toks = []
        for e in self.ENGS:
            if self.cnt[e] > 0:
                toks.append((e, self.cur[e], self.cnt[e]))
        for q in self.QUEUES:
            for s in self.dslots[q]:
                if s[1] > 0:
                    toks.append((None, s[0], s[1]))
        for e in self.ENGS:
            waits = []
            sn = self.seen[e]
            for (oe, sk, val) in toks:
                if oe == e:
                    continue
                if sn.get(sk, 0) >= val:
                    continue
                sn[sk] = val
                waits.append((sk, val))
                if oe is not None:
                    self.needed.add((sk, val))
            if waits:
                self.prog[e].append((waits, None, None))

    def finish(self):
        self.barrier()

    def emit(self):
        nc = self.nc
        sems = self.sems
        prog = self.prog

        eng_sems = set()
        for e in self.ENGS:
            for (waits, fn, inc) in prog[e]:
                if fn is not None and inc[1] == 1:
                    eng_sems.add(inc[0])
        rank = {}
        by_sk = {}
        for (sk, idx) in self.needed:
            by_sk.setdefault(sk, []).append(idx)
        for sk, lst_ in by_sk.items():
            rank[sk] = {idx: r + 1 for r, idx in enumerate(sorted(lst_))}
        needed = self.needed
        opidx = {}

        def run(eng, lst):
            for (waits, fn, inc) in lst:
                for (sk, val) in waits:
                    if sk in eng_sems:
                        eng.wait_ge(sems[sk], rank[sk][val])
                    else:
                        eng.wait_ge(sems[sk], val)
                if fn is not None:
                    ins = fn(eng)
                    if inc[1] == 1:
                        opidx[inc[0]] = opidx.get(inc[0], 0) + 1
                        if (inc[0], opidx[inc[0]]) in needed:
                            ins.then_inc(sems[inc[0]], 1)
                    else:
                        ins.then_inc(sems[inc[0]], inc[1])

        with nc.Block() as block:
            @block.sync
            def _(e):
                run(e, prog["sp"])

            @block.gpsimd
            def _(e):
                run(e, prog["pool"])

            @block.vector
            def _(e):
                run(e, prog["dve"])

            @block.scalar
            def _(e):
                run(e, prog["act"])

            @block.tensor
            def _(e):
                run(e, prog["pe"])


class Mem:
    def __init__(self, nc):
        self.nc = nc
        self.off = (nc.sbuf_base + 63) // 64 * 64
        self.top = nc.sbuf_top
        self.n = 0

    def alloc(self, shape, dtype, name=None):
        nbytes = int(np.prod(shape[1:])) * (4 if dtype == F32 else 2)
        nbytes = (nbytes + 63) // 64 * 64
        off = self.off
        assert off + nbytes <= self.top, f"SBUF overflow {name} {off + nbytes} > {self.top}"
        self.off += nbytes
        self.n += 1
        return self.nc.alloc_sbuf_tensor_at(f"{name or 't'}_{self.n}", list(shape), dtype, offset=off)

    def mark(self):
        return self.off

    def release(self, m):
        self.off = m


def mk_tiles(step):
    t = [(c0, step, False, c0 // 512) for c0 in range(0, LP, step)]
    t.append((LP, NS, True, 4))
    return t


TILES = mk_tiles(512)
TILES256 = mk_tiles(256)
TILES128 = mk_tiles(128)
LAYERS = [("rg", 0), ("gdn", 0), ("ssd", 0), ("rg", 1)]


def build_program(layers=(0, 1, 2, 3), final=True):
    nc = bass.Bass("TRN2", target_bir_lowering=False)
    k = KB(nc)
    mem = Mem(nc)
    have = {LAYERS[i][0] for i in layers}

    def din(name, shape):
        return nc.dram_tensor(name, list(shape), F32, kind="ExternalInput").ap()

    def dout(name, shape):
        return nc.dram_tensor(name, list(shape), F32, kind="ExternalOutput").ap()

    xT = din("xT", [D, NT])
    cT = din("cT", [D, 17])
    w_mod = din("w_mod", [4, D, 6 * D])
    b_mod = din("b_mod_t", [128, 4 * 48])
    w_up = din("w_up", [4, D, 4 * D])
    w_down = din("w_down", [4, 4 * D, D])
    fng = din("fng", [128, 8])
    yT = dout("yT", [D, NT])
    if "rg" in have:
        rg_w_in = din("rg_w_in", [2, D, 2 * D])
        rg_w_out = din("rg_w_out", [2, D, D])
        rg_gate_w = din("rg_gate_w", [2, 2, 4, 256, 256])
        rg_vec = din("rg_vec", [128, 2 * 8 * 8])
        rg_conv_in = din("rg_conv_in", [2, 128, 8 * 3 * 16])
        rg_h_in = din("rg_h_in", [2, 128, 8 * 16])
        p_rg_conv = dout("p_rg_conv", [2, 128, 8 * 3])
        s_rg_conv = dout("s_rg_conv", [2, 128, 8 * 3 * 16])
        p_rg_h = dout("p_rg_h", [2, 128, 8])
        s_rg_h = dout("s_rg_h", [2, 128, 8 * 16])
    if "ssd" in have:
        ssd_w_in = din("ssd_w_in", [D, 5152])
        ssd_w_out = din("ssd_w_out", [2048, D])
        ssd_vec = din("ssd_vec", [128, 5 * 24])
        ssd_hvec = din("ssd_hvec", [32, 3])
        ssd_rows = din("ssd_rows", [128, 64])
        ssd_cols = din("ssd_cols", [128, 32])
        ssd_conv_in = din("ssd_conv_in", [128, 24 * 3 * 16])
        ssd_h_in = din("ssd_h_in", [NS, 2048, 128])
        p_ssd_conv = dout("p_ssd_conv", [128, 24 * 3])
        s_ssd_conv = dout("s_ssd_conv", [128, 24 * 3 * 16])
        p_ssd_hT = dout("p_ssd_hT", [128, 2048])
        s_ssd_h = dout("s_ssd_h", [NS, 2048, 128])
    if "gdn" in have:
        gdn_w_in = din("gdn_w_in", [D, 4112])
        gdn_w_out = din("gdn_w_out", [D, D])
        gdn_vec = din("gdn_vec", [128, 4 * 24])
        gdn_rows = din("gdn_rows", [128, 16 + 128])
        gdn_masks = din("gdn_masks", [128, 7 * 128])
        gdn_conv_in = din("gdn_conv_in", [128, 24 * 3 * 16])
        gdn_S_in = din("gdn_S_in", [NS, 128, 8 * 128])
        p_gdn_conv = dout("p_gdn_conv", [128, 24 * 3])
        s_gdn_conv = dout("s_gdn_conv", [128, 24 * 3 * 16])
        p_gdn_S = dout("p_gdn_S", [128, 8 * 128])
        s_gdn_S = dout("s_gdn_S", [NS, 128, 8 * 128])

    X = mem.alloc([128, KT, NT], F32, "X")
    XR = [[Reg(f"X{kk}_{t}") for t in range(5)] for kk in range(KT)]
    MODS = [mem.alloc([128, 48, 17], F32, "MOD") for _ in range(2)]
    MODRS = [Reg("MOD0"), Reg("MOD1")]
    mod_cur = {"i": 0}
    ONESB = mem.alloc([128, 128], BF16, "onesb")
    ONESF = mem.alloc([128, 128], F32, "onesf")
    IDF = mem.alloc([128, 128], F32, "idf")
    TRI = mem.alloc([128, 128], F32, "tri")
    CONST = Reg("const")
    KC = mem.alloc([128, 2], F32, "kc")
    BMOD = mem.alloc([128, 4 * 48], F32, "bmod")
    FNG = mem.alloc([128, 8], F32, "fng")
    CT_ = mem.alloc([128, KT, 17], F32, "ct")
    SC = mem.alloc([128, KT, 17], BF16, "sc")
    CTR = Reg("ct")
    SMALL = Reg("small")
    WP = [mem.alloc([128, 8, 1024], BF16, f"wp{i}") for i in range(3)]
    WPR = [Reg(f"wp{i}") for i in range(3)]
    PS = nc.alloc_psum_tensor("ps", [128, 8, 512], F32)
    PSR = [Reg(f"ps{i}", excl=True) for i in range(8)]
    ps_state = {"next": 0, "lim": 8}

    def psum():
        b = ps_state.get("base", 0) + ps_state["next"] % ps_state["lim"]
        ps_state["next"] += 1
        return b

    class QSlot:
        def __init__(self, bank, q):
            self.ap = PS[:, bank, q * 128:(q + 1) * 128]
            self.reg = PSR[bank]

    QSLOTS = [[QSlot(bank, q) for q in range(4)] for bank in range(4)]
    qs_state = [0, 0, 0, 0]

    def psq(lane):
        sl = QSLOTS[lane][qs_state[lane] % 4]
        qs_state[lane] += 1
        return sl

    class WPool:
        def __init__(self):
            self.sched = []
            self.issued = 0
            self.taken = 0
            self.released = 0

        def add(self, src_ap):
            self.sched.append(src_ap)

        def _issue(self, i):
            b = i % 3
            src = self.sched[i]
            k.dma("pool", lambda e, b=b, src=src: e.dma_start(out=WP[b][:], in_=src), w=[WPR[b]])

        def get(self):
            i = self.taken
            self.pump()
            assert self.issued > i, "weight pool deadlock: release blocks before getting more"
            self.taken += 1
            return WP[i % 3], WPR[i % 3]

        def pump(self):
            while self.issued < len(self.sched) and self.issued < self.released + 3:
                self._issue(self.issued)
                self.issued += 1

        def rel(self, n=1):
            self.released += n
            assert self.released <= self.taken
            self.pump()

    wpool = WPool()

    def blk(w_ap2d, r0, c0):
        return w_ap2d[r0:r0 + 1024, c0:c0 + 1024].rearrange("(k p) n -> p k n", p=128)

    def act(out, in_, func, r, w, **kw):
        k.op("act", lambda e: e.activation(out=out, in_=in_, func=func, **kw), r=r, w=w)

    def vtt(out, a, b_, op, r, w, eng="dve"):
        k.op(eng, lambda e: e.tensor_tensor(out=out, in0=a, in1=b_, op=op), r=r, w=w)

    def vts(out, a, s1, s2, op0, op1, r, w, eng="dve"):
        if s2 is None:
            k.op(eng, lambda e: e.tensor_scalar(out=out, in0=a, scalar1=s1, scalar2=None, op0=op0), r=r, w=w)
        else:
            k.op(eng, lambda e: e.tensor_scalar(out=out, in0=a, scalar1=s1, scalar2=s2, op0=op0, op1=op1), r=r, w=w)

    def vstt(out, a, s, b_, op0, op1, r, w, eng="dve"):
        k.op(eng, lambda e: e.scalar_tensor_tensor(out=out, in0=a, scalar=s, in1=b_, op0=op0, op1=op1), r=r, w=w)

    def vcopy(out, a, r, w, eng="dve"):
        k.op(eng, lambda e: e.tensor_copy(out=out, in_=a), r=r, w=w)

    def vmemset(out, val, w, eng="dve"):
        k.op(eng, lambda e: e.memset(out, val), w=w)

    def vscan(out, d0, d1, init, r, w):
        k.op("dve", lambda e: e.tensor_tensor_scan(out=out, data0=d0, data1=d1, initial=init, op0=ALU.mult, op1=ALU.add), r=r, w=w)

    def vrecip(out, a, r, w):
        k.op("dve", lambda e: e.reciprocal(out=out, in_=a), r=r, w=w)

    def sigmoid_L(out, in_, r, w, scale=1.0, nbias=None):
        kw = {"scale": -scale}
        if nbias is not None:
            kw["bias"] = nbias
        act(out, in_, AF.Exp, r, w, **kw)
        act(out, out, AF.Ln, w, w, bias=1.0, scale=1.0)
        act(out, out, AF.Exp, w, w, scale=-1.0)

    def rstd_L(out, in_, r, w, inv, rows=128, lnpost=None):
        act(out, in_, AF.Ln, r, w, scale=inv, bias=KC[:rows, 0:1])
        if lnpost is None:
            act(out, out, AF.Exp, w, w, scale=-0.5)
        else:
            act(out, out, AF.Exp, w, w, scale=-0.5, bias=lnpost)

    def vreduce(out, a, r, w):
        k.op("dve", lambda e: e.tensor_reduce(out=out, in_=a, axis=AX.X, op=ALU.add), r=r, w=w)

    def dma(q, out, in_, r=(), w=()):
        k.dma(q, lambda e: e.dma_start(out=out, in_=in_), r=r, w=w)

    def mm_ap(out_ap, preg, pairs, r, first=True, last=True):
        np_ = len(pairs)
        for i_, (l, rh) in enumerate(pairs):
            k.op("pe", (lambda l, rh, st, sp: (lambda e: e.matmul(out_ap, lhsT=l, rhs=rh, start=st, stop=sp)))(
                l, rh, first and i_ == 0, last and i_ == np_ - 1), r=r, w=[preg])

    def mm(b, n, pairs, r, rows=128):
        mm_ap(PS[:rows, b, :n], PSR[b], pairs, r)

    def transpose(b, in_ap, in_parts, in_free, r):
        k.op("pe", lambda e: e.transpose(PS[:in_free, b, :in_parts], in_ap, IDF[:in_parts, :in_parts]),
             r=list(r) + [CONST], w=[PSR[b]])

    def transpose_q(slot, in_ap, r):
        k.op("pe", lambda e: e.transpose(slot.ap, in_ap, IDF[:]), r=list(r) + [CONST], w=[slot.reg])

    def run_interleaved(gens):
        live = list(gens)
        while live:
            nxt = []
            for gn in live:
                try:
                    next(gn)
                    nxt.append(gn)
                except StopIteration:
                    pass
            live = nxt

    evac_flip = [0]

    def evac(out, in_, r, w):
        evac_flip[0] ^= 1
        if evac_flip[0]:
            act(out, in_, AF.Copy, r, w)
        else:
            vcopy(out, in_, r, w)

    vmemset(ONESB[:], 1.0, [CONST], eng="pool")
    vmemset(ONESF[:], 1.0, [CONST], eng="pool")
    vmemset(IDF[:], 0.0, [CONST], eng="pool")
    k.op("pool", lambda e: e.affine_select(out=IDF[:], in_=IDF[:], pattern=[[-1, 128]], compare_op=ALU.not_equal,
                                           fill=1.0, base=0, channel_multiplier=1), r=[CONST], w=[CONST])
    k.op("pool", lambda e: e.affine_select(out=TRI[:], in_=ONESF[:], pattern=[[1, 128]], compare_op=ALU.is_ge,
                                           fill=0.0, base=0, channel_multiplier=-1), r=[CONST], w=[CONST])
    vmemset(KC[:, 0:1], EPS, [CONST], eng="pool")
    vmemset(KC[:, 1:2], math.log(128.0 ** -0.5), [CONST], eng="pool")
    dma("sp", BMOD[:], b_mod, w=[SMALL])
    dma("sp", FNG[:], fng, w=[SMALL])
    for kk in range(KT):
        dma("sp", X[:, kk, :], xT[kk * 128:(kk + 1) * 128, :], w=XR[kk])
    dma("sp", CT_[:], cT.rearrange("(k p) n -> p k n", p=128), w=[CTR])
    SGC = mem.alloc([128, KT, 17], F32, "sgc")
    sigmoid_L(SGC[:], CT_[:], [CTR], [CTR])
    vtt(SC[:], CT_[:], SGC[:], ALU.mult, [CTR], [CTR])

    def modcol(v, kk):
        return MODS[mod_cur["i"]][:, v * 8 + kk, 0:1]

    def modmat(v, kk):
        return MODS[mod_cur["i"]][:, v * 8 + kk, 1:17]

    class _ModReg:
        def __getattr__(self, a):
            return getattr(MODRS[mod_cur["i"]], a)

        def __setattr__(self, a, v):
            setattr(MODRS[mod_cur["i"]], a, v)

    MODR = _ModReg()

    def sumsq(T, SQ, SQR):
        c0, n, sample, ri = T
        b = psum()
        for kk in range(KT):
            s = kk % 2
            act(SQ[s][:, :n], X[:, kk, c0:c0 + n], AF.Square, [XR[kk][ri]], [SQR[s]])
            mm_ap(PS[:, b, :n], PSR[b], [(ONESB[:], SQ[s][:, :n])], [SQR[s], CONST], first=(kk == 0), last=(kk == KT - 1))
        return b

    def rstd(b, n, inv, RS, RSR, rows=128):
        rstd_L(RS[:rows, :n], PS[:rows, b, :n], [PSR[b], CONST], [RSR], inv, rows=rows)

    def norm_mod(v_sh, v_sc, T, U, UR, scratch):
        c0, n, sample, ri = T
        SQ, SQR, RS, RSR, TMP, TMPR = scratch
        b = sumsq(T, SQ, SQR)
        rstd(b, n, 1.0 / D, RS, RSR)
        for kk in range(KT):
            s = kk % 2
            vtt(TMP[s][:, :n], X[:, kk, c0:c0 + n], RS[:, :n], ALU.mult, [XR[kk][ri], RSR], [TMPR[s]])
            if not sample:
                act(U[:, kk, :n], TMP[s][:, :n], AF.Identity, [TMPR[s], MODR], [UR],
                    scale=modcol(v_sc, kk), bias=modcol(v_sh, kk))
            else:
                vtt(TMP[s][:, :n], TMP[s][:, :n], modmat(v_sc, kk), ALU.mult, [TMPR[s], MODR], [TMPR[s]])
                vtt(U[:, kk, :n], TMP[s][:, :n], modmat(v_sh, kk), ALU.add, [TMPR[s], MODR], [UR])

    def resid_update(v_g, T, m, b, T2, T2R):
        c0, n, sample, ri = T
        xs = X[:, m, c0:c0 + n]
        if not sample:
            vstt(xs, PS[:, b, :n], modcol(v_g, m), xs, ALU.mult, ALU.add, [PSR[b], MODR, XR[m][ri]], [XR[m][ri]])
        else:
            vtt(T2[:, :n], PS[:, b, :n], modmat(v_g, m), ALU.mult, [PSR[b], MODR], [T2R])
            vtt(xs, xs, T2[:, :n], ALU.add, [T2R, XR[m][ri]], [XR[m][ri]])

    def std_scratch(w=512, nsq=2):
        SQ = [mem.alloc([128, w], BF16, "sq") for _ in range(nsq)] * (2 // nsq)
        SQR = [Reg("sq") for _ in range(nsq)] * (2 // nsq)
        RS = mem.alloc([128, w], F32, "rs")
        RSR = Reg("rs")
        TMP = [mem.alloc([128, w], F32, "tmp") for _ in range(2)]
        TMPR = [Reg("tmp") for _ in range(2)]
        return (SQ, SQR, RS, RSR, TMP, TMPR)

    def conv_tile(b, T, m, XP, XPR, CBUF, CBR, XPS, XPSR, wv, bv, out, outr, act_first=False):
        c0, n, sample, ri = T
        if not sample:
            s = m % len(XP)
            xp = XP[s]
            act(xp[:, 3:3 + n], PS[:, b, :n], AF.Copy, [PSR[b]], [XPR[s]])
            if act_first:
                act(xp[:, 0:3], CBUF[:, m, :], AF.Copy, [CBR], [XPR[s]])
            else:
                vcopy(xp[:, 0:3], CBUF[:, m, :], [CBR], [XPR[s]])
            vcopy(CBUF[:, m, :], xp[:, n:n + 3], [XPR[s]], [CBR])
            srcs = [xp[:, kq:kq + n] for kq in range(4)]
            rr = [XPR[s]]
        else:
            act(XPS[:, m, 3, :], PS[:, b, :n], AF.Copy, [PSR[b]], [XPSR])
            srcs = [XPS[:, m, kq, :] for kq in range(4)]
            rr = [XPSR]
        if act_first:
            if bv is not None:
                act(out, srcs[0], AF.Identity, rr + [SMALL], [outr], scale=wv(0, m), bias=bv(m))
            else:
                act(out, srcs[0], AF.Identity, rr + [SMALL], [outr], scale=wv(0, m))
        elif bv is not None:
            vts(out, srcs[0], wv(0, m), bv(m), ALU.mult, ALU.add, rr + [SMALL], [outr])
        else:
            vts(out, srcs[0], wv(0, m), None, ALU.mult, None, rr + [SMALL], [outr])
        for kq in range(1, 4):
            vstt(out, srcs[kq], wv(kq, m), out, ALU.mult, ALU.add, rr + [SMALL, outr], [outr])

    MOD_SPLIT = [(0, 1), (2, 3), (4,), (5,)]

    def sched_mod(i, js=range(6)):
        for j in js:
            wpool.add(blk(w_mod[i], 0, j * 1024))

    def do_mod(i, js=range(6), slot=None):
        slot = (i % 2) if slot is None else slot
        MODt, MODtr = MODS[slot], MODRS[slot]
        for j in js:
            W, WR = wpool.get()
            for m in range(8):
                b = psum()
                mm(b, 17, [(W[:, kk, m * 128:(m + 1) * 128], SC[:, kk, :]) for kk in range(KT)], r=[WR, CTR])
                col = i * 48 + j * 8 + m
                act(MODt[:, j * 8 + m, :], PS[:, b, :17], AF.Identity, [PSR[b], SMALL], [MODtr],
                    bias=BMOD[:, col:col + 1], scale=1.0)
            wpool.rel()
            if j in (1, 4):
                sl = MODt[:, j * 8:(j + 1) * 8, :]
                vts(sl, sl, 1.0, None, ALU.add, None, [MODtr], [MODtr])

    def sched_rg(j):
        for ti in range(5):
            wpool.add(blk(rg_w_in[j], 0, 0))
            wpool.add(blk(rg_w_in[j], 0, 1024))
            wpool.add(blk(rg_w_out[j], 0, 0))

    def do_rg(i, j):
        mk = mem.mark()
        RGV = mem.alloc([128, 10, 8], F32, "rgv")
        LS8 = mem.alloc([128, 8], F32, "ls8")
        GW = mem.alloc([128, 2, 4, 2, 256], BF16, "gw")
        GWR = Reg("gw")
        U = mem.alloc([128, KT, 512], BF16, "U")
        UR = Reg("U")
        HY = mem.alloc([128, KT, 512], BF16, "HY")
        HYR = Reg("HY")
        scratch = std_scratch()
        XP = [mem.alloc([128, 3 + 512], F32, "xp") for _ in range(2)]
        XPR = [Reg("xp") for _ in range(2)]
        XPS = mem.alloc([128, KT, 4, 16], F32, "xps")
        XPSR = Reg("xps")
        H0 = mem.alloc([128, KT, 16], F32, "h0")
        H0R = Reg("h0")
        HS = mem.alloc([128, KT, 16], F32, "hs")
        HSR = Reg("hs")
        CBUF = mem.alloc([128, KT, 3], F32, "cbuf")
        CBR = Reg("cbuf")
        HLAST = mem.alloc([128, KT], F32, "hlast")
        HLR = Reg("hlast")
        XC = [mem.alloc([128, 512], F32, "xc") for _ in range(2)]
        XCR = [Reg("xc") for _ in range(2)]
        XCB = [mem.alloc([128, 512], BF16, "xcb") for _ in range(2)]
        XCBR = [Reg("xcb") for _ in range(2)]
        names = ["R", "I", "A", "S", "BX", "HT", "YB"]
        TT2 = [{nm: mem.alloc([128, 512], F32, nm) for nm in names} for _ in range(2)]
        TR2 = [{nm: Reg(nm) for nm in names} for _ in range(2)]
        TT, TR = TT2[0], TR2[0]

        dma("sp", RGV[:, 0:8, :].rearrange("p b c -> p (b c)"), rg_vec[:, j * 64:(j + 1) * 64], w=[SMALL])
        vts(RGV[:, 8:10, :], RGV[:, 5:7, :], -1.0, None, ALU.mult, None, [SMALL], [SMALL])
        for g in range(2):
            dma("pool", GW[:, g], rg_gate_w[j, g].rearrange("n (jt p) o -> p n jt o", p=128), w=[GWR])
        dma("sp", XPS[:, :, 0:3, :].rearrange("p a b c -> p a (b c)"), rg_conv_in[j].rearrange("p (a bc) -> p a bc", a=8), w=[XPSR])
        dma("sp", H0[:].rearrange("p a b -> p (a b)"), rg_h_in[j], w=[H0R])
        vmemset(CBUF[:], 0.0, [CBR])
        vmemset(HLAST[:], 0.0, [HLR])
        ls = LS8[:, :]
        act(ls, RGV[:, 7, :], AF.Exp, [SMALL], [SMALL], scale=-1.0)
        act(ls, ls, AF.Ln, [SMALL], [SMALL], bias=1.0, scale=1.0)
        vts(ls, ls, -8.0, None, ALU.mult, None, [SMALL], [SMALL])

        def vec(v, m):
            return RGV[:, v, m:m + 1]

        XC4 = [XC, [mem.alloc([128, 512], F32, "xc") for _ in range(2)]]
        XCR4 = [XCR, [Reg("xc") for _ in range(2)]]
        XCB4 = [XCB, [mem.alloc([128, 512], BF16, "xcb") for _ in range(2)]]
        XCBR4 = [XCBR, [Reg("xcb") for _ in range(2)]]
        norm_mod(0, 1, TILES[0], U, UR, scratch)
        for ti, T in enumerate(TILES):
            c0, n, sample, ri = T
            W0, W0R = wpool.get()
            W1, W1R = wpool.get()

            def conv_chain(nb):
                XCn, XCRn, XCBn, XCBRn = XC4[nb % 2], XCR4[nb % 2], XCB4[nb % 2], XCBR4[nb % 2]
                for jt in range(2):
                    m = 2 * nb + jt
                    b = psum()
                    mm(b, n, [(W1[:, kk, m * 128:(m + 1) * 128], U[:, kk, :n]) for kk in range(KT)], r=[W1R, UR])
                    yield
                    conv_tile(b, T, m, XP, XPR, CBUF, CBR, XPS, XPSR, vec, lambda m_: vec(4, m_), XCn[jt][:, :n], XCRn[jt])
                    yield
                    act(XCBn[jt][:, :n], XCn[jt][:, :n], AF.Copy, [XCRn[jt]], [XCBRn[jt]])
                    yield

            def gate_chain(nb, oh):
                XCn, XCRn, XCBn, XCBRn = XC4[nb % 2], XCR4[nb % 2], XCB4[nb % 2], XCBR4[nb % 2]
                mo = 2 * nb + oh
                TT, TR = TT2[mo % 2], TR2[mo % 2]
                Rt, It, At, St, BXt, HTt, YBt = [TT[nm][:, :n] for nm in names]
                bg = []
                for g, nm in ((0, "R"), (1, "I")):
                    b = psum()
                    mm(b, n, [(GW[:, g, nb, jt, oh * 128:(oh + 1) * 128], XCBn[jt][:, :n]) for jt in range(2)],
                       r=[GWR, XCBRn[0], XCBRn[1]])
                    bg.append(b)
                by = psum()
                mm(by, n, [(W0[:, kk, mo * 128:(mo + 1) * 128], U[:, kk, :n]) for kk in range(KT)], r=[W0R, UR])
                yield
                for g, nm in ((0, "R"), (1, "I")):
                    b = bg[g]
                    act(TT[nm][:, :n], PS[:, b, :n], AF.Exp, [PSR[b], SMALL], [TR[nm]], scale=-1.0, bias=vec(8 + g, mo))
                    yield
                act(YBt, PS[:, by, :n], AF.Copy, [PSR[by]], [TR["YB"]])
                act(St, PS[:, by, :n], AF.Square, [PSR[by]], [TR["S"]])
                yield
                for nm in ("R", "I"):
                    act(TT[nm][:, :n], TT[nm][:, :n], AF.Ln, [TR[nm]], [TR[nm]], bias=1.0, scale=1.0)
                    yield
                vts(St, St, 0.044715, 1.0, ALU.mult, ALU.add, [TR["S"]], [TR["S"]])
                yield
                for nm in ("R", "I"):
                    act(TT[nm][:, :n], TT[nm][:, :n], AF.Exp, [TR[nm]], [TR[nm]], scale=-1.0)
                    yield
                vtt(St, St, YBt, ALU.mult, [TR["S"], TR["YB"]], [TR["S"]])
                yield
                act(At, Rt, AF.Exp, [TR["R"], SMALL], [TR["A"]], scale=LS8[:, mo:mo + 1])
                sigmoid_L(St, St, [TR["S"]], [TR["S"]], scale=1.5957691216057308)
                yield
                vtt(YBt, YBt, St, ALU.mult, [TR["YB"], TR["S"]], [TR["YB"]])
                vstt(St, At, -1.0, At, ALU.mult, ALU.mult, [TR["A"]], [TR["S"]])
                yield
                act(St, St, AF.Ln, [TR["S"]], [TR["S"]], bias=1.0, scale=1.0)
                yield
                act(St, St, AF.Exp, [TR["S"]], [TR["S"]], scale=0.5)
                yield
                vtt(BXt, St, It, ALU.mult, [TR["S"], TR["I"]], [TR["BX"]])
                yield
                vtt(BXt, BXt, XCn[oh][:, :n], ALU.mult, [TR["BX"], XCRn[oh]], [TR["BX"]])
                yield
                if not sample:
                    vscan(HTt, At, BXt, HLAST[:, mo:mo + 1], [TR["A"], TR["BX"], HLR], [TR["HT"]])
                    yield
                    vcopy(HLAST[:, mo:mo + 1], TT["HT"][:, n - 1:n], [TR["HT"]], [HLR])
                else:
                    vtt(HTt, At, H0[:, mo, :], ALU.mult, [TR["A"], H0R], [TR["HT"]])
                    yield
                    vtt(HTt, HTt, BXt, ALU.add, [TR["HT"], TR["BX"]], [TR["HT"]])
                    yield
                    act(HS[:, mo, :], HTt, AF.Copy, [TR["HT"]], [HSR])
                yield
                vtt(HY[:, mo, :n], HTt, YBt, ALU.mult, [TR["HT"], TR["YB"]], [HYR])

            run_interleaved([conv_chain(0)])
            for nb in range(4):
                gens = [gate_chain(nb, 0), gate_chain(nb, 1)]
                if nb < 3:
                    gens.append(conv_chain(nb + 1))
                run_interleaved(gens)
            wpool.rel(2)
            if ti + 1 < len(TILES):
                norm_mod(0, 1, TILES[ti + 1], U, UR, scratch)
            W2, W2R = wpool.get()
            for m in range(KT):
                b = psum()
                mm(b, n, [(W2[:, kk, m * 128:(m + 1) * 128], HY[:, kk, :n]) for kk in range(KT)], r=[W2R, HYR])
                resid_update(2, T, m, b, TT2[0]["R"], TR2[0]["R"])
            wpool.rel()
            if c0 + n == LP:
                dma("sp", p_rg_conv[j], CBUF[:].rearrange("p a b -> p (a b)"), r=[CBR])
                dma("sp", p_rg_h[j], HLAST[:], r=[HLR])
            if sample:
                dma("sp", s_rg_conv[j].rearrange("p (a bc) -> p a bc", a=8), XPS[:, :, 1:4, :].rearrange("p a b c -> p a (b c)"), r=[XPSR])
                dma("sp", s_rg_h[j], HS[:].rearrange("p a b -> p (a b)"), r=[HSR])
        k.barrier()
        mem.release(mk)

    def sched_ssd():
        for _ in TILES256:
            for c in range(3):
                wpool.add(blk(ssd_w_in, 0, 2048 + c * 1024))
            for c in range(2):
                wpool.add(blk(ssd_w_in, 0, c * 1024))
            for c in range(2):
                wpool.add(blk(ssd_w_out, c * 1024, 0))

    def do_ssd(i):
        mk = mem.mark()
        NW = 256
        SSV = mem.alloc([128, 5, 24], F32, "ssv")
        SSH = mem.alloc([32, 4], F32, "ssh")
        ROWS = mem.alloc([128, 64], F32, "rows")
        COLS = mem.alloc([128, 32], F32, "cols")
        WDT = mem.alloc([128, KT, 32], BF16, "wdt")
        WDTR = Reg("wdt")
        U = mem.alloc([128, KT, NW], BF16, "U")
        UR = Reg("U")
        scratch = std_scratch(NW, nsq=1)
        CBUF = mem.alloc([128, 24, 3], F32, "cbuf")
        CBR = Reg("cbuf")
        XS = mem.alloc([128, 16, NW], F32, "XS")
        XSR = [Reg(f"xs{m}") for m in range(16)]
        BT = mem.alloc([128, 4, NW], F32, "BT")
        BTR = Reg("bt")
        CTm = mem.alloc([128, 4, NW], F32, "CT")
        CTR2 = Reg("ctm")
        DTF = mem.alloc([32, NW], F32, "dtf")
        DTFR = Reg("dtf")
        XTOK = mem.alloc([128, 2048], F32, "xtok")
        XTR = Reg("xtok")
        DIAG = mem.alloc([128, 4, 128], F32, "diag")
        DIAGR = Reg("diag")
        GBS = mem.alloc([128, 4, 128], F32, "gbs")
        GBSR = Reg("gbs")
        CBM = mem.alloc([128, 4, 128], F32 if False else F32, "cbm")
        CBMR = Reg("cbm")
        HT = mem.alloc([128, 2048], F32, "HT")
        HTR = [Reg(f"ht{h}") for h in range(32)]
        YG = mem.alloc([128, 2048], F32, "YG")
        YGR = [Reg(f"yg{g}") for g in range(4)]
        ZS = DIAG[:].rearrange("p a b -> p (a b)")
        ZSR = DIAGR
        SS = mem.alloc([128, 8], F32, "ss")
        SSR = Reg("ss")
        YT = mem.alloc([128, 16, NW], BF16, "YT")
        YTR = Reg("yt")
        mp = mem.mark()
        XP = [mem.alloc([128, 3 + NW], F32, "xp")]
        XPR = [Reg("xp")]
        BTOK = mem.alloc([128, 512], BF16, "btok")
        BTKR = Reg("btok")
        SM = mem.alloc([128, 6, 32], F32, "sm")
        SMR = [Reg(f"sm{q}") for q in range(6)]
        ME4 = [mem.alloc([128, 4, 128], F32, "me4")] * 2
        ME4R = [Reg("me4")] * 2
        MT4 = [mem.alloc([128, 4, 128], BF16, "mt4")] * 2
        MT4R = [Reg("mt4")] * 2
        XDT4 = [mem.alloc([128, 4, 64], BF16, "xdt4")] * 2
        XDT4R = [Reg("xdt4")] * 2
        CD4 = mem.alloc([128, 4, 128], BF16, "cd4")
        CD4R = Reg("cd4")
        BD4 = mem.alloc([128, 4, 128], BF16, "bd4")
        BD4R = Reg("bd4")
        HTB = mem.alloc([128, 2048], BF16, "htb")
        HTBR = [Reg(f"htb{h}") for h in range(8)]
        XPS, XPSR = None, None

        dma("sp", SSV[:].rearrange("p a b -> p (a b)"), ssd_vec, w=[SMALL])
        dma("sp", SSH[:, 0:3], ssd_hvec, w=[SMALL])
        dma("sp", ROWS[:], ssd_rows, w=[SMALL])
        dma("sp", COLS[:], ssd_cols, w=[SMALL])
        dma("pool", WDT[:], ssd_w_in[:, 5120:5152].rearrange("(k p) n -> p k n", p=128), w=[WDTR])
        act(ROWS[:, 0:32], ROWS[:, 0:32], AF.Exp, [SMALL], [SMALL])
        vts(ROWS[:, 0:32], ROWS[:, 0:32], -1.0, None, ALU.mult, None, [SMALL], [SMALL])
        act(SSH[:, 3:4], SSH[:, 0:1], AF.Exp, [SMALL], [SMALL])
        vts(SSH[:, 3:4], SSH[:, 3:4], -1.0, None, ALU.mult, None, [SMALL], [SMALL])
        ANEG = ROWS[:, 0:32]
        DB = ROWS[:, 32:64]
        vmemset(CBUF[:], 0.0, [CBR])
        vmemset(HT[:], 0.0, HTR)
        vmemset(HTB[:], 0.0, HTBR)
        DTT, ATOK, ACS, DS, CD, TM = [SM[:, q, :] for q in range(6)]

        def wv(kq, m):
            return SSV[:, kq, m:m + 1]

        def bv(m):
            return SSV[:, 4, m:m + 1]

        def post_group(rows, g, ucols, WZ, WZR, have_o, obank):
            yg = YG[:rows, g * 512:(g + 1) * 512]
            if have_o:
                vtt(yg.rearrange("p (h q) -> p h q", q=64), XTOK[:rows, g * 512:(g + 1) * 512].rearrange("p (h q) -> p h q", q=64),
                    DB[:rows, g * 8:(g + 1) * 8].unsqueeze(2).to_broadcast([rows, 8, 64]), ALU.mult, [XTR, SMALL], [YGR[g]])
                vtt(yg, yg, PS[:rows, obank, :512], ALU.add, [YGR[g], PSR[obank]], [YGR[g]])
            bz = psum()
            W, WR = WZ[g // 2], WZR[g // 2]
            mm(bz, 512, [(U[:, kk, ucols[0]:ucols[1]], W[:, kk, (g % 2) * 512:(g % 2) * 512 + 512]) for kk in range(KT)],
               r=[UR, WR], rows=rows)
            sigmoid_L(ZS[:rows, :], PS[:rows, bz, :512], [PSR[bz]], [ZSR])
            vtt(yg, yg, PS[:rows, bz, :512], ALU.mult, [YGR[g], PSR[bz]], [YGR[g]])
            vtt(yg, yg, ZS[:rows, :], ALU.mult, [YGR[g], ZSR], [YGR[g]])
            vtt(ZS[:rows, :], yg, yg, ALU.mult, [YGR[g], ZSR], [ZSR])
            vreduce(SS[:rows, g:g + 1], ZS[:rows, :], [ZSR], [SSR])
            rstd_L(SS[:rows, g:g + 1], SS[:rows, g:g + 1], [SSR, CONST], [SSR], 1.0 / 512, rows=rows)
            vts(yg, yg, SS[:rows, g:g + 1], None, ALU.mult, None, [YGR[g], SSR], [YGR[g]])

        def to_feature_major(rows, ocol):
            for jt in range(16):
                b = psum()
                transpose(b, YG[:rows, jt * 128:(jt + 1) * 128], rows, 128, [YGR[jt // 4]])
                act(YT[:, jt, ocol:ocol + rows], PS[:, b, :rows], AF.Copy, [PSR[b], SMALL], [YTR], scale=COLS[:, jt:jt + 1])

        ps_state["lim"] = 4
        norm_mod(0, 1, TILES256[0], U, UR, scratch)
        for ti, T in enumerate(TILES256):
            c0, n, sample, ri = T
            if sample:
                k.barrier()
                mem.release(mp)
                XPS = mem.alloc([128, 24, 4, 16], F32, "xps")
                XPSR = Reg("xps")
                dma("sp", XPS[:, :, 0:3, :].rearrange("p a b c -> p a (b c)"), ssd_conv_in.rearrange("p (a bc) -> p a bc", a=24), w=[XPSR])
            for cb in range(3):
                W, WR = wpool.get()
                for mm_ in range(8):
                    m = cb * 8 + mm_
                    b = psum()
                    mm(b, n, [(W[:, kk, mm_ * 128:(mm_ + 1) * 128], U[:, kk, :n]) for kk in range(KT)], r=[WR, UR])
                    if m < 16:
                        dst, dr = XS[:, m, :n], XSR[m]
                    elif m < 20:
                        dst, dr = BT[:, m - 16, :n], BTR
                    else:
                        dst, dr = CTm[:, m - 20, :n], CTR2
                    conv_tile(b, T, m, XP, XPR, CBUF, CBR, XPS, XPSR, wv, bv, dst, dr, act_first=True)
                    sg = scratch[4][m % 2][:, :n]
                    sgr = scratch[5][m % 2]
                    sigmoid_L(sg, dst, [dr], [sgr])
                    vtt(dst, dst, sg, ALU.mult, [dr, sgr], [dr])
                wpool.rel()
            b = psum()
            mm(b, n, [(WDT[:, kk, :], U[:, kk, :n]) for kk in range(KT)], r=[WDTR, UR], rows=32)
            act(DTF[:, :n], PS[:32, b, :n], AF.Exp, [PSR[b], SMALL], [DTFR], bias=SSH[:, 1:2], scale=1.0)
            act(DTF[:, :n], DTF[:, :n], AF.Ln, [DTFR], [DTFR], bias=1.0, scale=1.0)
            WZ0, WZ0R = wpool.get()
            WZ1, WZ1R = wpool.get()
            WZ, WZR = [WZ0, WZ1], [WZ0R, WZ1R]
            if not sample:
                for ch in range(n // 128):
                    o = ch * 128
                    for jt in range(16):
                        b = psum()
                        transpose(b, XS[:, jt, o:o + 128], 128, 128, [XSR[jt]])
                        evac(XTOK[:, jt * 128:(jt + 1) * 128], PS[:, b, :128], [PSR[b]], [XTR])
                    for g in range(4):
                        b = psum()
                        transpose(b, BT[:, g, o:o + 128], 128, 128, [BTR])
                        evac(BTOK[:, g * 128:(g + 1) * 128], PS[:, b, :128], [PSR[b]], [BTKR])
                    b = psum()
                    transpose(b, DTF[:, o:o + 128], 32, 128, [DTFR])
                    vcopy(DTT, PS[:, b, :32], [PSR[b]], [SMR[0]])
                    vtt(ATOK, DTT, ANEG, ALU.mult, [SMR[0], SMALL], [SMR[1]])
                    b = psum()
                    mm(b, 32, [(TRI[:], ATOK)], r=[CONST, SMR[1]])
                    vcopy(ACS, PS[:, b, :32], [PSR[b]], [SMR[2]])
                    b = psum()
                    mm(b, 32, [(ONESF[:], ATOK)], r=[CONST, SMR[1]])
                    vtt(TM, PS[:, b, :32], ACS, ALU.subtract, [PSR[b], SMR[2]], [SMR[5]])
                    act(DS, TM, AF.Exp, [SMR[5]], [SMR[3]])
                    act(CD, PS[:, b, :32], AF.Exp, [PSR[b]], [SMR[4]])
                    for g in range(4):
                        b = psum()
                        mm(b, 128, [(BT[:, g, o:o + 128], CTm[:, g, o:o + 128])], r=[BTR, CTR2])
                        vtt(CBM[:, g, :], PS[:, b, :128], TRI[:], ALU.mult, [PSR[b], CONST], [CBMR])
                    def bc(ap2, w):
                        return ap2.unsqueeze(2).to_broadcast([128, 4, w])

                    def emit_gb(hg_):
                        hsl_ = slice(hg_ * 4, hg_ * 4 + 4)
                        vtt(DIAG[:], IDF[:].unsqueeze(1).to_broadcast([128, 4, 128]), bc(ACS[:, hsl_], 128), ALU.mult,
                            [CONST, SMR[2]], [DIAGR])
                        bq = psum()
                        mm(bq, 512, [(ONESF[:], DIAG[:].rearrange("p a b -> p (a b)"))], r=[CONST, DIAGR])
                        return bq

                    bgb = emit_gb(0)
                    act(GBS[:].rearrange("p a b -> p (a b)"), PS[:, bgb, :512], AF.Copy, [PSR[bgb]], [GBSR])
                    for hg in range(8):
                        g = hg // 2
                        s4 = hg % 2
                        hsl = slice(hg * 4, hg * 4 + 4)
                        mt, mtr = MT4[s4], MT4R[s4]
                        me, mer = ME4[s4], ME4R[s4]
                        xd, xdr = XDT4[s4], XDT4R[s4]
                        act(me[:], GBS[:], AF.Exp, [GBSR], [mer])
                        vtt(CD4[:], me[:], CTm[:, g, o:o + 128].unsqueeze(1).to_broadcast([128, 4, 128]), ALU.mult,
                            [mer, CTR2], [CD4R])
                        vtt(me[:], GBS[:], bc(ACS[:, hsl], 128), ALU.subtract, [GBSR, SMR[2]], [mer])
                        vts(me[:], me[:], 0.0, None, ALU.min, None, [mer], [mer])
                        act(me[:], me[:], AF.Exp, [mer], [mer])
                        if hg + 1 < 8:
                            bgb = emit_gb(hg + 1)
                        vtt(xd[:], XTOK[:, hg * 256:(hg + 1) * 256].rearrange("p (h q) -> p h q", q=64), bc(DTT[:, hsl], 64),
                            ALU.mult, [XTR, SMR[0]], [xdr])
                        vtt(BD4[:], BTOK[:, g * 128:(g + 1) * 128].unsqueeze(1).to_broadcast([128, 4, 128]), bc(DS[:, hsl], 128),
                            ALU.mult, [BTKR, SMR[3]], [BD4R])
                        vtt(mt[:], me[:], CBM[:, g, :].unsqueeze(1).to_broadcast([128, 4, 128]), ALU.mult, [mer, CBMR], [mtr])
                        if hg + 1 < 8:
                            act(GBS[:].rearrange("p a b -> p (a b)"), PS[:, bgb, :512], AF.Copy, [PSR[bgb]], [GBSR])
                        ob = 4 + g
                        b2 = psum()
                        for hh in range(4):
                            h = hg * 4 + hh
                            oc = (h % 8) * 64
                            mm_ap(PS[:, ob, oc:oc + 64], PSR[ob], [(mt[:, hh, :], xd[:, hh, :]), (CD4[:, hh, :], HTB[:, h * 64:(h + 1) * 64])],
                                  r=[mtr, xdr, CD4R, HTBR[hg]])
                        for hh in range(4):
                            mm_ap(PS[:, b2, hh * 64:(hh + 1) * 64], PSR[b2], [(BD4[:, hh, :], xd[:, hh, :])], r=[BD4R, xdr])
                        ht4 = HT[:, hg * 256:(hg + 1) * 256].rearrange("p (h q) -> p h q", q=64)
                        hrs = [HTR[hg * 4 + hh] for hh in range(4)]
                        vtt(ht4, ht4, bc(CD[:, hsl], 64), ALU.mult, hrs + [SMR[4]], hrs)
                        vtt(ht4, ht4, PS[:, b2, :256].rearrange("p (h q) -> p h q", q=64), ALU.add, hrs + [PSR[b2]], hrs)
                        act(HTB[:, hg * 256:(hg + 1) * 256], HT[:, hg * 256:(hg + 1) * 256], AF.Copy, hrs, [HTBR[hg]])
                    for g in range(4):
                        post_group(128, g, (o, o + 128), WZ, WZR, True, 4 + g)
                    to_feature_major(128, o)
            else:
                do_ssd_sample(locals())
            wpool.rel(2)
            if ti + 1 < len(TILES256):
                norm_mod(0, 1, TILES256[ti + 1], U, UR, scratch)
            O0, O0R = wpool.get()
            O1, O1R = wpool.get()
            for m in range(KT):
                b = psum()
                mm(b, n, [(O0[:, kk, m * 128:(m + 1) * 128], YT[:, kk, :n]) for kk in range(KT)] +
                   [(O1[:, kk, m * 128:(m + 1) * 128], YT[:, 8 + kk, :n]) for kk in range(KT)], r=[O0R, O1R, YTR])
                resid_update(2, T, m, b, scratch[2], scratch[3])
            wpool.rel(2)
            if c0 + n == LP:
                dma("sp", p_ssd_conv, CBUF[:].rearrange("p a b -> p (a b)"), r=[CBR])
                dma("sp", p_ssd_hT, HT[:], r=HTR)
        ps_state["lim"] = 8
        k.barrier()
        mem.release(mk)

    def do_ssd_sample(L):
        (XS, XSR, BT, BTR, CTm, CTR2, DTF, DTFR, SSH, COLS, XPS, XPSR, YG, YGR, HT, HTR, XTOK, XTR,
         DIAG, DIAGR, GBS, GBSR, CBM, CBMR, WZ, WZR, post_group, to_feature_major) = [L[q] for q in (
            "XS", "XSR", "BT", "BTR", "CTm", "CTR2", "DTF", "DTFR", "SSH", "COLS", "XPS", "XPSR", "YG", "YGR", "HT", "HTR",
            "XTOK", "XTR", "DIAG", "DIAGR", "GBS", "GBSR", "CBM", "CBMR", "WZ", "WZR", "post_group",
            "to_feature_major")]
        mk = mem.mark()
        EXPM = YG[:32, :]
        EXR = YGR[0]
        ADT = mem.alloc([32, 16], F32, "adt")
        ADR = Reg("adt")
        DECX = mem.alloc([128, 16, 16], F32, "decx")
        DTX = mem.alloc([128, 16, 16], F32, "dtx")
        DXR = Reg("dx")
        YS = mem.alloc([128, 16, 16], F32, "ys")
        YSR = Reg("ys")
        HB = [XTOK[:].rearrange("p (a b) -> p a b", a=16), XS[:, :, 128:256]]
        HBR = [XTR, Reg("hb1")]
        TMPB = YG[:].rearrange("p (a b) -> p a b", a=16)
        BBC = GBS
        CBC = CBM
        vmemset(EXPM, 1.0, YGR, eng="pool")
        k.op("pool", lambda e: e.affine_select(out=EXPM, in_=EXPM, pattern=[[1, 2048]], compare_op=ALU.is_ge,
                                               fill=0.0, base=0, channel_multiplier=-64), r=YGR, w=YGR)
        k.op("pool", lambda e: e.affine_select(out=EXPM, in_=EXPM, pattern=[[-1, 2048]], compare_op=ALU.is_ge,
                                               fill=0.0, base=63, channel_multiplier=64), r=YGR, w=YGR)
        vts(ADT[:], DTF[:, :16], SSH[:, 3:4], None, ALU.mult, None, [DTFR, SMALL], [ADR])
        for jt in range(16):
            b = psum()
            mm(b, 16, [(EXPM[:, jt * 128:(jt + 1) * 128], ADT[:])], r=YGR + [ADR])
            act(DECX[:, jt, :], PS[:, b, :16], AF.Exp, [PSR[b]], [DXR])
            b = psum()
            mm(b, 16, [(EXPM[:, jt * 128:(jt + 1) * 128], DTF[:, :16])], r=YGR + [DTFR])
            vtt(DTX[:, jt, :], PS[:, b, :16], XS[:, jt, :16], ALU.mult, [PSR[b], XSR[jt]], [DXR])

        def load_state(t):
            dma("sp", HB[t % 2], ssd_h_in[t].rearrange("(a p) n -> p a n", p=128), w=[HBR[t % 2]])

        load_state(0)
        for t in range(NS):
            if t + 1 < NS:
                load_state(t + 1)
            H0, H0R_ = HB[t % 2], HBR[t % 2]
            for (src, srcr, dst, dstr) in ((BT, BTR, BBC, GBSR), (CTm, CTR2, CBC, CBMR)):
                vtt(DIAG[:], IDF[:].unsqueeze(1).to_broadcast([128, 4, 128]), src[:, :, t:t + 1].to_broadcast([128, 4, 128]),
                    ALU.mult, [CONST, srcr], [DIAGR])
                b = psum()
                mm(b, 512, [(ONESF[:], DIAG[:].rearrange("p a b -> p (a b)"))], r=[CONST, DIAGR])
                act(dst[:].rearrange("p a b -> p (a b)"), PS[:, b, :512], AF.Copy, [PSR[b]], [dstr])
            for jt in range(16):
                g = jt // 4
                vts(TMPB[:, jt, :], BBC[:, g, :], DTX[:, jt, t:t + 1], None, ALU.mult, None, [GBSR, DXR], YGR)
                vstt(H0[:, jt, :], H0[:, jt, :], DECX[:, jt, t:t + 1], TMPB[:, jt, :], ALU.mult, ALU.add, [H0R_, DXR] + YGR, [H0R_])
                vtt(TMPB[:, jt, :], H0[:, jt, :], CBC[:, g, :], ALU.mult, [H0R_, CBMR], YGR)
            vreduce(YS[:, :, t:t + 1].rearrange("p a b -> p (a b)"), TMPB, YGR, [YSR])
            dma("sp", s_ssd_h[t].rearrange("(a p) n -> p a n", p=128), H0, r=[H0R_])
        for jt in range(16):
            vstt(YS[:, jt, :], XS[:, jt, :16], COLS[:, 16 + jt:17 + jt], YS[:, jt, :], ALU.mult, ALU.add, [XSR[jt], SMALL, YSR], [YSR])
        for jt in range(16):
            b = psum()
            transpose(b, YS[:, jt, :], 128, 16, [YSR])
            evac(YG[:16, jt * 128:(jt + 1) * 128], PS[:16, b, :128], [PSR[b]], [YGR[jt // 4]])
        for g in range(4):
            post_group(16, g, (0, 16), WZ, WZR, False, None)
        to_feature_major(16, 0)
        dma("sp", s_ssd_conv.rearrange("p (a bc) -> p a bc", a=24), XPS[:, :, 1:4, :].rearrange("p a b c -> p a (b c)"), r=[XPSR])

    def sched_gdn():
        for _ in TILES128:
            for c in range(3):
                wpool.add(blk(gdn_w_in, 0, c * 1024))
            wpool.add(blk(gdn_w_in, 0, 3072))
            wpool.add(blk(gdn_w_out, 0, 0))

    def do_gdn(i):
        mk = mem.mark()
        NW = 128
        GV = mem.alloc([128, 4, 24], F32, "gv")
        ROWS = mem.alloc([128, 16 + 128], F32, "rows")
        WAB = mem.alloc([128, KT, 16], BF16, "wab")
        WABR = Reg("wab")
        STL = mem.alloc([128, 128], F32, "stl")
        U = mem.alloc([128, KT, NW], BF16, "U")
        UR = Reg("U")
        scratch = std_scratch(NW)
        SQ, SQR, RS, RSR, TMP, TMPR = scratch
        CBUF = mem.alloc([128, 24, 3], F32, "cbuf")
        CBR = Reg("cbuf")
        QKV = mem.alloc([128, 24, NW], F32, "qkv")
        QR = [Reg(f"qkv{m}") for m in range(24)]
        S = mem.alloc([128, 8, 128], F32, "S")
        SR = [Reg(f"S{h}") for h in range(8)]
        OG = mem.alloc([128, 1024], F32, "og")
        OGR = Reg("og")
        ZS = mem.alloc([128, 512], F32, "zs")
        ZSR = Reg("zs")
        SG8 = mem.alloc([128, 8, NW], F32, "sg8")
        SG8R = Reg("sg8")
        YT = mem.alloc([128, 8, NW], BF16, "YT")
        YTR = Reg("yt")
        SM = mem.alloc([128, 10, 8], F32, "sm")
        SMR = Reg("smg")
        AB, BETA, NBETA, GTOK, GC, DSg, CDg, BEG, SSg, TM8 = [SM[:, q, :] for q in range(10)]
        DG = mem.alloc([128, 16, 128], F32, "dg")
        DIAG = DG[:, 0:8, :]
        DIAGR = Reg("diag")
        GBS = DG[:, 8:16, :]
        GBSR = Reg("gbs")
        MSK = mem.alloc([128, 7, 128], F32, "msk")
        MSKR = Reg("msk")
        dma("sp", MSK[:].rearrange("p a b -> p (a b)"), gdn_masks, w=[MSKR])
        mp = mem.mark()
        XP = [mem.alloc([128, 3 + NW], F32, "xp") for _ in range(2)]
        XPR = [Reg("xp") for _ in range(2)]
        NLANE = 2
        slot_names = ["KT", "VT", "D0", "DM", "DTM", "ZT", "DD", "TT", "RB1"]
        LB = [{q: mem.alloc([128, 4, 128], F32, q) for q in slot_names} for _ in range(NLANE)]
        LR = [{q: Reg(q) for q in slot_names} for _ in range(NLANE)]

        dma("sp", GV[:].rearrange("p a b -> p (a b)"), gdn_vec, w=[SMALL])
        dma("sp", ROWS[:], gdn_rows, w=[SMALL])
        dma("pool", WAB[:], gdn_w_in[:, 4096:4112].rearrange("(k p) n -> p k n", p=128), w=[WABR])
        act(ROWS[:, 0:8], ROWS[:, 0:8], AF.Exp, [SMALL], [SMALL])
        vts(ROWS[:, 0:8], ROWS[:, 0:8], -1.0, None, ALU.mult, None, [SMALL], [SMALL])
        ANEG = ROWS[:, 0:8]
        DTB = ROWS[:, 8:16]
        NGR = ROWS[:, 16:144]
        vts(STL[:], TRI[:], -1.0, 1.0, ALU.mult, ALU.add, [CONST], [SMALL])
        vmemset(CBUF[:], 0.0, [CBR])
        vmemset(S[:], 0.0, SR)

        def wv(kq, m):
            return GV[:, kq, m:m + 1]

        def tok_gates(rows, ucols):
            b = psum()
            mm(b, 16, [(U[:, kk, ucols[0]:ucols[1]], WAB[:, kk, :]) for kk in range(KT)], r=[UR, WABR], rows=rows)
            sigmoid_L(BETA[:rows], PS[:rows, b, 8:16], [PSR[b]], [SMR])
            vts(NBETA[:rows], BETA[:rows], -1.0, None, ALU.mult, None, [SMR], [SMR])
            vtt(AB[:rows], PS[:rows, b, 0:8], DTB[:rows], ALU.add, [PSR[b], SMALL], [SMR])
            act(AB[:rows], AB[:rows], AF.Exp, [SMR], [SMR])
            act(AB[:rows], AB[:rows], AF.Ln, [SMR], [SMR], bias=1.0, scale=1.0)
            vtt(GTOK[:rows], AB[:rows], ANEG[:rows], ALU.mult, [SMR, SMALL], [SMR])

        def post(rows, ucols, WG, WGR, ocol):
            og3 = OG[:rows, :].rearrange("p (h d) -> p h d", d=128)
            for half in range(2):
                zz = ZS[:rows, :]
                sl = OG[:rows, half * 512:(half + 1) * 512]
                vtt(zz, sl, sl, ALU.mult, [OGR, ZSR], [ZSR])
                vreduce(SSg[:rows, half * 4:(half + 1) * 4], zz.rearrange("p (h d) -> p h d", d=128), [ZSR], [SMR])
            rstd_L(SSg[:rows], SSg[:rows], [SMR, CONST], [SMR], 1.0 / 128, rows=rows)
            vtt(og3, og3, SSg[:rows].unsqueeze(2).to_broadcast([rows, 8, 128]), ALU.mult, [OGR, SMR], [OGR])
            vtt(og3, og3, NGR[:rows].unsqueeze(1).to_broadcast([rows, 8, 128]), ALU.mult, [OGR, SMALL], [OGR])
            for half in range(2):
                bz = psum()
                mm(bz, 512, [(U[:, kk, ucols[0]:ucols[1]], WG[:, kk, half * 512:(half + 1) * 512]) for kk in range(KT)],
                   r=[UR, WGR], rows=rows)
                sigmoid_L(ZS[:rows, :], PS[:rows, bz, :512], [PSR[bz]], [ZSR])
                sl = OG[:rows, half * 512:(half + 1) * 512]
                vtt(sl, sl, PS[:rows, bz, :512], ALU.mult, [OGR, PSR[bz]], [OGR])
                vtt(sl, sl, ZS[:rows, :], ALU.mult, [OGR, ZSR], [OGR])
            for h in range(8):
                b = psum()
                transpose(b, OG[:rows, h * 128:(h + 1) * 128], rows, 128, [OGR])
                evac(YT[:, h, ocol:ocol + rows], PS[:, b, :rows], [PSR[b]], [YTR])

        ps_state["lim"] = 6
        ps_state["base"] = 0
        norm_mod(0, 1, TILES128[0], U, UR, scratch)
        for ti, T in enumerate(TILES128):
            c0, n, sample, ri = T
            if sample:
                k.barrier()
                mem.release(mp)
                XPS = mem.alloc([128, 24, 4, 16], F32, "xps")
                XPSR = Reg("xps")
                dma("sp", XPS[:, :, 0:3, :].rearrange("p a b c -> p a (b c)"), gdn_conv_in.rearrange("p (a bc) -> p a bc", a=24), w=[XPSR])
            else:
                XPS, XPSR = None, None
            for cb in range(3):
                W, WR = wpool.get()
                for mm_ in range(8):
                    m = cb * 8 + mm_
                    b = psum()
                    mm(b, n, [(W[:, kk, mm_ * 128:(mm_ + 1) * 128], U[:, kk, :n]) for kk in range(KT)], r=[WR, UR])
                    conv_tile(b, T, m, XP, XPR, CBUF, CBR, XPS, XPSR, wv, None, QKV[:, m, :n], QR[m], act_first=True)
                wpool.rel()
                blkv = QKV[:, cb * 8:(cb + 1) * 8, :n]
                brs = QR[cb * 8:(cb + 1) * 8]
                sg = SG8[:, :, :n]
                sigmoid_L(sg, blkv, brs, [SG8R])
                vtt(blkv, blkv, sg, ALU.mult, brs + [SG8R], brs)
                if cb < 2:
                    act(sg, blkv, AF.Square, brs, [SG8R])
                    if not sample:
                        bb = []
                        for half in range(2):
                            b2 = psum()
                            mm(b2, 512, [(ONESF[:], SG8[:, half * 4:(half + 1) * 4, :].rearrange("p a b -> p (a b)"))], r=[CONST, SG8R])
                            bb.append(b2)
                        for half in range(2):
                            rstd_L(SG8[:, half * 4:(half + 1) * 4, :].rearrange("p a b -> p (a b)"), PS[:, bb[half], :512],
                                   [PSR[bb[half]], CONST], [SG8R], 1.0, lnpost=(KC[:, 1:2] if cb == 0 else None))
                    else:
                        b2 = psum()
                        for mm_ in range(8):
                            mm_ap(PS[:, b2, mm_ * 16:(mm_ + 1) * 16], PSR[b2], [(ONESF[:], SG8[:, mm_, :16])], r=[CONST, SG8R])
                        rstd_L(sg, PS[:, b2, :128].rearrange("p (a b) -> p a b", a=8), [PSR[b2], CONST], [SG8R], 1.0,
                               lnpost=(KC[:, 1:2] if cb == 0 else None))
                    vtt(blkv, blkv, sg, ALU.mult, brs + [SG8R], brs)
            WG, WGR = wpool.get()
            if not sample:
                for ch in range(n // 128):
                    o = ch * 128
                    tok_gates(128, (o, o + 128))
                    b = psum()
                    mm(b, 8, [(TRI[:], GTOK)], r=[CONST, SMR])
                    vcopy(GC, PS[:, b, :8], [PSR[b]], [SMR])
                    b = psum()
                    mm(b, 8, [(ONESF[:], GTOK)], r=[CONST, SMR])
                    vtt(TM8, PS[:, b, :8], GC, ALU.subtract, [PSR[b], SMR], [SMR])
                    act(DSg, TM8, AF.Exp, [SMR], [SMR])
                    act(CDg, PS[:, b, :8], AF.Exp, [PSR[b]], [SMR])
                    act(TM8, GC, AF.Exp, [SMR], [SMR])
                    vtt(BEG, TM8, BETA, ALU.mult, [SMR], [SMR])
                    for h in range(8):
                        vts(DIAG[:, h, :], IDF[:], GC[:, h:h + 1], None, ALU.mult, None, [CONST, SMR], [DIAGR])
                    for half in range(2):
                        b = psum()
                        mm(b, 512, [(ONESF[:], DG[:, half * 4:(half + 1) * 4, :].rearrange("p a b -> p (a b)"))], r=[CONST, DIAGR])
                        act(DG[:, 8 + half * 4:8 + (half + 1) * 4, :].rearrange("p a b -> p (a b)"), PS[:, b, :512], AF.Copy, [PSR[b]], [GBSR])
                    def bc4(ap2, w=128):
                        return ap2.unsqueeze(2).to_broadcast([128, 4, w])

                    def cb4(ap2):
                        return ap2.unsqueeze(1).to_broadcast([128, 4, 128])

                    def flat(t):
                        return t[:].rearrange("p a b -> p (a b)")

                    def pe4(b, fn, r):
                        for hh in range(4):
                            mm_ap(PS[:, b, hh * 128:(hh + 1) * 128], PSR[b], fn(hh), r=r)

                    def tr4(b, src4, r):
                        for hh in range(4):
                            k.op("pe", (lambda hh: (lambda e: e.transpose(PS[:, b, hh * 128:(hh + 1) * 128], src4(hh), IDF[:])))(hh),
                                 r=list(r) + [CONST], w=[PSR[b]])

                    def group_chain(gq, B, R):
                        hsl = slice(gq * 4, gq * 4 + 4)
                        hs = [gq * 4 + hh for hh in range(4)]
                        rq = [QR[h] for h in hs]
                        rk = [QR[8 + h] for h in hs]
                        rv = [QR[16 + h] for h in hs]
                        sr = [SR[h] for h in hs]
                        qT4 = QKV[:, gq * 4:gq * 4 + 4, o:o + 128]
                        S4 = S[:, hsl, :]
                        ps4 = lambda b: PS[:, b, :512].rearrange("p (a b) -> p a b", a=4)
                        b = psum()
                        tr4(b, lambda hh: QKV[:, 8 + hs[hh], o:o + 128], rk)
                        evac(flat(B["KT"]), PS[:, b, :512], [PSR[b]], [R["KT"]])
                        b = psum()
                        tr4(b, lambda hh: QKV[:, 16 + hs[hh], o:o + 128], rv)
                        evac(flat(B["VT"]), PS[:, b, :512], [PSR[b]], [R["VT"]])
                        vtt(B["D0"][:], GBS[:, hsl, :], bc4(GC[:, hsl]), ALU.subtract, [GBSR, SMR], [R["D0"]])
                        yield
                        vts(flat(B["DM"]), flat(B["D0"]), 0.0, None, ALU.max, None, [R["D0"]], [R["DM"]])
                        vts(flat(B["DTM"]), flat(B["D0"]), 0.0, None, ALU.min, None, [R["D0"]], [R["DTM"]])
                        bk = psum()
                        pe4(bk, lambda hh: [(QKV[:, 8 + hs[hh], o:o + 128], QKV[:, 8 + hs[hh], o:o + 128])], rk)
                        yield
                        act(flat(B["DM"]), flat(B["DM"]), AF.Exp, [R["DM"]], [R["DM"]], scale=-1.0)
                        act(flat(B["DTM"]), flat(B["DTM"]), AF.Exp, [R["DTM"]], [R["DTM"]])
                        yield
                        vtt(B["DM"][:], B["DM"][:], cb4(STL[:]), ALU.mult, [R["DM"], SMALL], [R["DM"]])
                        vtt(B["DTM"][:], B["DTM"][:], cb4(TRI[:]), ALU.mult, [R["DTM"], CONST], [R["DTM"]])
                        yield
                        vtt(B["DM"][:], ps4(bk), B["DM"][:], ALU.mult, [PSR[bk], R["DM"]], [R["DM"]])
                        vtt(B["DM"][:], B["DM"][:], bc4(NBETA[:, hsl]), ALU.mult, [R["DM"], SMR], [R["DM"]])
                        NA, NAR = B["DM"], R["DM"]
                        yield
                        first = True
                        for lv in range(7):
                            vtt(B["D0"][:], NA[:], cb4(MSK[:, lv, :]), ALU.mult, [NAR, MSKR], [R["D0"]])
                            yield
                            b = psum()
                            if first:
                                pe4(b, lambda hh: [(B["D0"][:, hh, :], IDF[:])], [R["D0"], CONST])
                            else:
                                pe4(b, lambda hh: [(B["D0"][:, hh, :], B["TT"][:, hh, :])], [R["D0"], R["TT"]])
                            yield
                            evac(flat(B["ZT"]), PS[:, b, :512], [PSR[b]], [R["ZT"]])
                            yield
                            b = psum()
                            if first:
                                pe4(b, lambda hh: [(IDF[:], B["ZT"][:, hh, :])], [CONST, R["ZT"]])
                            else:
                                pe4(b, lambda hh: [(B["DD"][:, hh, :], B["ZT"][:, hh, :])], [R["DD"], R["ZT"]])
                            yield
                            if first:
                                vtt(B["TT"][:], cb4(IDF[:]), ps4(b), ALU.add, [CONST, PSR[b]], [R["TT"]])
                            else:
                                vtt(B["TT"][:], B["TT"][:], ps4(b), ALU.add, [R["TT"], PSR[b]], [R["TT"]])
                            first = False
                            yield
                            if lv < 6:
                                b = psum()
                                tr4(b, lambda hh: B["TT"][:, hh, :], [R["TT"]])
                                yield
                                evac(flat(B["DD"]), PS[:, b, :512], [PSR[b]], [R["DD"]])
                                yield
                        vtt(B["VT"][:], B["VT"][:], bc4(BETA[:, hsl]), ALU.mult, [R["VT"], SMR], [R["VT"]])
                        vtt(B["RB1"][:], B["KT"][:], bc4(BEG[:, hsl]), ALU.mult, [R["KT"], SMR], [R["RB1"]])
                        vtt(B["KT"][:], B["KT"][:], bc4(DSg[:, hsl]), ALU.mult, [R["KT"], SMR], [R["KT"]])
                        yield
                        b = psum()
                        pe4(b, lambda hh: [(B["RB1"][:, hh, :], B["TT"][:, hh, :])], [R["RB1"], R["TT"]])
                        yield
                        act(flat(B["ZT"]), PS[:, b, :512], AF.Copy, [PSR[b]], [R["ZT"]], scale=-1.0)
                        act(flat(B["D0"]), GBS[:, hsl, :].rearrange("p a b -> p (a b)"), AF.Exp, [GBSR], [R["D0"]])
                        yield
                        b = psum()
                        pe4(b, lambda hh: [(B["TT"][:, hh, :], B["VT"][:, hh, :]), (B["ZT"][:, hh, :], S[:, hs[hh], :])],
                            [R["TT"], R["VT"], R["ZT"]] + sr)
                        vtt(B["D0"][:], qT4, B["D0"][:], ALU.mult, rq + [R["D0"]], [R["D0"]])
                        yield
                        evac(flat(B["DD"]), PS[:, b, :512], [PSR[b]], [R["DD"]])
                        b2 = psum()
                        pe4(b2, lambda hh: [(QKV[:, 8 + hs[hh], o:o + 128], QKV[:, hs[hh], o:o + 128])], rk + rq)
                        yield
                        vtt(B["DTM"][:], ps4(b2), B["DTM"][:], ALU.mult, [PSR[b2], R["DTM"]], [R["DTM"]])
                        yield
                        ob = 6 + gq
                        pe4(ob, lambda hh: [(B["D0"][:, hh, :], S[:, hs[hh], :]), (B["DTM"][:, hh, :], B["DD"][:, hh, :])],
                            [R["D0"], R["DTM"], R["DD"]] + sr)
                        b = psum()
                        pe4(b, lambda hh: [(B["KT"][:, hh, :], B["DD"][:, hh, :])], [R["KT"], R["DD"]])
                        yield
                        vtt(S4, S4, bc4(CDg[:, hsl]), ALU.mult, sr + [SMR], sr)
                        vtt(S4, S4, ps4(b), ALU.add, sr + [PSR[b]], sr)

                    gens = [group_chain(gq, LB[gq], LR[gq]) for gq in range(2)]
                    for _ in range(3):
                        next(gens[0])
                    live = list(gens)
                    while live:
                        nxt = []
                        for gn in live:
                            try:
                                next(gn)
                                nxt.append(gn)
                            except StopIteration:
                                pass
                        live = nxt
                    for half in range(2):
                        evac(OG[:, half * 512:(half + 1) * 512], PS[:, 6 + half, :512], [PSR[6 + half]], [OGR])
                    if os.environ.get("DBG_GDN") and c0 == 0 and ch == 0:
                        dbg2 = dout("dbg_og", [128, 1024])
                        dma("sp", dbg2, OG[:], r=[OGR])
                    post(128, (o, o + 128), WG, WGR, o)
                    if os.environ.get("DBG_GDN") and c0 == 0 and ch == 0:
                        dbg3 = dout("dbg_og2", [128, 1024])
                        dma("sp", dbg3, OG[:], r=[OGR])
            else:
                gdn_sample(locals())
            wpool.rel()
            if ti + 1 < len(TILES128):
                norm_mod(0, 1, TILES128[ti + 1], U, UR, scratch)
            WO, WOR = wpool.get()
            for m in range(KT):
                b = psum()
                mm(b, n, [(WO[:, kk, m * 128:(m + 1) * 128], YT[:, kk, :n]) for kk in range(KT)], r=[WOR, YTR])
                resid_update(2, T, m, b, RS, RSR)
            wpool.rel()
            if c0 + n == LP:
                dma("sp", p_gdn_conv, CBUF[:].rearrange("p a b -> p (a b)"), r=[CBR])
                dma("sp", p_gdn_S, S[:].rearrange("p a b -> p (a b)"), r=SR)
        ps_state["lim"] = 8
        ps_state["base"] = 0
        k.barrier()
        mem.release(mk)

    def gdn_sample(L):
        (QKV, QR, S, SR, OG, OGR, U, UR, WAB, WABR, ROWS, XPS, XPSR, WG, WGR, post, DG, YT) = [L[q] for q in (
            "QKV", "QR", "S", "SR", "OG", "OGR", "U", "UR", "WAB", "WABR", "ROWS", "XPS", "XPSR", "WG", "WGR", "post", "DG", "YT")]
        ANEG, DTB = ROWS[:, 0:8], ROWS[:, 8:16]
        S0B = [mem.alloc([128, 8, 128], F32, "s0") for _ in range(2)]
        S0RB = [Reg("s0a"), Reg("s0b")]
        SNB = [S, mem.alloc([128, 8, 128], F32, "sn1")]
        SNRB = [SR, [Reg(f"sn1_{h}") for h in range(8)]]
        RW = mem.alloc([1, 12, 8], F32, "rw")
        RWR = Reg("rw")
        KV = DG[0:1].rearrange("p a b -> p (a b)")
        KVR = Reg("kv")
        RQ = mem.alloc([1, 1024], F32, "rq")
        RQR = Reg("rq")
        VNr = mem.alloc([1, 1024], F32, "vnr")
        VNR = Reg("vnr")
        OR_ = mem.alloc([1, 1024], F32, "orow")
        ORR = Reg("orow")
        EGB = mem.alloc([128, 8], F32, "egb")
        EGBR = Reg("egb")
        OH = mem.alloc([1, 16, 16], F32, "oh")
        OHR = Reg("oh")
        vmemset(OH[:], 0.0, [OHR])
        for t in range(NS):
            vmemset(OH[:, t, t:t + 1], 1.0, [OHR])
        ab, beta, g_, eg, neg, qk = [RW[:, q, :] for q in (0, 2, 3, 4, 5, 6)]
        ab16 = RW[:, 0:2, :].rearrange("p a b -> p (a b)")
        def load_S0(t_):
            dma("sp", S0B[t_ % 2][:].rearrange("p a b -> p (a b)"), gdn_S_in[t_], w=[S0RB[t_ % 2]])

        load_S0(0)
        for t in range(NS):
            if t + 1 < NS:
                load_S0(t + 1)
            S0, S0R = S0B[t % 2], S0RB[t % 2]
            SN, SNR = SNB[t % 2], SNRB[t % 2]
            b = psum()
            mm(b, 16, [(U[:, kk, t:t + 1], WAB[:, kk, :]) for kk in range(KT)], r=[UR, WABR], rows=1)
            vcopy(ab16, PS[:1, b, :16], [PSR[b]], [RWR])
            sigmoid_L(beta, RW[:, 1, :], [RWR], [RWR])
            vtt(g_, RW[:, 0, :], DTB[:1], ALU.add, [RWR, SMALL], [RWR])
            act(g_, g_, AF.Exp, [RWR], [RWR])
            act(g_, g_, AF.Ln, [RWR], [RWR], bias=1.0, scale=1.0)
            vtt(g_, g_, ANEG[:1], ALU.mult, [RWR, SMALL], [RWR])
            act(eg, g_, AF.Exp, [RWR], [RWR])
            vts(neg, eg, -1.0, None, ALU.mult, None, [RWR], [RWR])
            for half in range(4):
                b = psum()
                for hh in range(4):
                    m = 8 + half * 4 + hh
                    mm_ap(PS[:1, b, hh * 128:(hh + 1) * 128], PSR[b], [(QKV[:, m, t:t + 1], IDF[:])], r=[QR[m], CONST])
                evac(KV[:, half * 512:(half + 1) * 512], PS[:1, b, :512], [PSR[b]], [KVR])
            def rows_times_S0(mbase):
                for half in range(2):
                    b = psum()
                    for hh in range(4):
                        h = half * 4 + hh
                        mm_ap(PS[:1, b, hh * 128:(hh + 1) * 128], PSR[b], [(QKV[:, mbase + h, t:t + 1], S0[:, h, :])],
                              r=[QR[mbase + h], S0R])
                    evac(RQ[:, half * 512:half * 512 + 512], PS[:1, b, :512], [PSR[b]], [RQR])
            rows_times_S0(8)
            b = psum()
            for h in range(8):
                mm_ap(PS[:1, b, h:h + 1], PSR[b], [(QKV[:, h, t:t + 1], QKV[:, 8 + h, t:t + 1])], r=[QR[h], QR[8 + h]])
            vcopy(qk, PS[:1, b, :8], [PSR[b]], [RWR])
            r3 = RQ[:, 0:1024].rearrange("p (h d) -> p h d", d=128)
            qs3 = r3
            v3 = KV[:, 1024:2048].rearrange("p (h d) -> p h d", d=128)
            vn3 = VNr[:].rearrange("p (h d) -> p h d", d=128)
            o3 = OR_[:].rearrange("p (h d) -> p h d", d=128)

            def bc(row):
                return row.unsqueeze(2).to_broadcast([1, 8, 128])
            vtt(vn3, r3, bc(neg), ALU.mult, [RQR, RWR], [VNR])
            vtt(vn3, vn3, v3, ALU.add, [VNR, KVR], [VNR])
            vtt(vn3, vn3, bc(beta), ALU.mult, [VNR, RWR], [VNR])
            rows_times_S0(0)
            vtt(o3, qs3, bc(eg), ALU.mult, [RQR, RWR], [ORR])
            vtt(qs3, vn3, bc(qk), ALU.mult, [VNR, RWR], [RQR])
            vtt(o3, o3, qs3, ALU.add, [ORR, RQR], [ORR])
            for half in range(2):
                mm_ap(PS[:16, 6 + half, :512], PSR[6 + half], [(OH[:, t, :], OR_[:, half * 512:(half + 1) * 512])],
                      r=[OHR, ORR], first=(t == 0), last=(t == NS - 1))
            b = psum()
            mm(b, 8, [(ONESF[0:1, :], eg)], r=[CONST, RWR])
            vcopy(EGB[:], PS[:, b, :8], [PSR[b]], [EGBR])
            for half in range(2):
                b = psum()
                for hh in range(4):
                    h = half * 4 + hh
                    mm_ap(PS[:, b, hh * 128:(hh + 1) * 128], PSR[b], [(KV[:, h * 128:(h + 1) * 128], VNr[:, h * 128:(h + 1) * 128])],
                          r=[KVR, VNR])
                for hh in range(4):
                    h = half * 4 + hh
                    vstt(SN[:, h, :], S0[:, h, :], EGB[:, h:h + 1], PS[:, b, hh * 128:(hh + 1) * 128], ALU.mult, ALU.add,
                         [S0R, EGBR, PSR[b]], [SNR[h]])
            dma("sp", s_gdn_S[t], SN[:].rearrange("p a b -> p (a b)"), r=SNR)
        for half in range(2):
            evac(OG[:16, half * 512:(half + 1) * 512], PS[:16, 6 + half, :512], [PSR[6 + half]], [OGR])
        post(16, (0, 16), WG, WGR, 0)
        dma("sp", s_gdn_conv.rearrange("p (a bc) -> p a bc", a=24), XPS[:, :, 1:4, :].rearrange("p a b c -> p a (b c)"), r=[XPSR])

    def sched_mlp(i, nxt=None):
        for g in range(4):
            wpool.add(blk(w_up[i], 0, g * 1024))
            wpool.add(blk(w_down[i], g * 1024, 0))
            if nxt is not None:
                sched_mod(nxt, MOD_SPLIT[g])

    def do_mlp(i, nxt=None):
        mk = mem.mark()
        UA = mem.alloc([128, KT, NT], BF16, "UA")
        UAR = [Reg(f"ua{t}") for t in range(5)]
        H1 = mem.alloc([128, KT, 512], BF16, "H1")
        H1R = Reg("h1")
        scratch = std_scratch()
        SQ, SQR, RS, RSR, TMP, TMPR = scratch
        normed = [0]

        def ensure_norm(t):
            while normed[0] <= min(t, len(TILES) - 1):
                c0_, n_, _, ri_ = TILES[normed[0]]
                norm_mod(3, 4, TILES[normed[0]], UA[:, :, c0_:c0_ + n_], UAR[ri_], scratch)
                normed[0] += 1

        ensure_norm(0)
        for g in range(4):
            WU, WUR = wpool.get()
            WD, WDR = wpool.get()
            for ti, T in enumerate(TILES):
                c0, n, sample, ri = T
                ensure_norm(ti + 1)
                for m in range(KT):
                    b = psum()
                    mm(b, n, [(WU[:, kk, m * 128:(m + 1) * 128], UA[:, kk, c0:c0 + n]) for kk in range(KT)], r=[WUR, UAR[ri]])
                    s = m % 2
                    act(TMP[s][:, :n], PS[:, b, :n], AF.Relu, [PSR[b]], [TMPR[s]])
                    vtt(H1[:, m, :n], TMP[s][:, :n], TMP[s][:, :n], ALU.mult, [TMPR[s]], [H1R])
                for m in range(KT):
                    b = psum()
                    mm(b, n, [(WD[:, kk, m * 128:(m + 1) * 128], H1[:, kk, :n]) for kk in range(KT)], r=[WDR, H1R])
                    resid_update(5, T, m, b, RS, RSR)
            wpool.rel(2)
            if nxt is not None:
                do_mod(nxt, MOD_SPLIT[g])
        k.barrier()
        mem.release(mk)

    def do_final():
        mk = mem.mark()
        SQ, SQR, RS, RSR, TMP, TMPR = std_scratch()
        for T in TILES:
            c0, n, sample, ri = T
            b = sumsq(T, SQ, SQR)
            rstd(b, n, 1.0 / D, RS, RSR)
            for kk in range(KT):
                xs = X[:, kk, c0:c0 + n]
                vstt(xs, xs, FNG[:, kk:kk + 1], RS[:, :n], ALU.mult, ALU.mult, [XR[kk][ri], RSR, SMALL], [XR[kk][ri]])
        mem.release(mk)

    def write_x():
        for kk in range(KT):
            dma("sp", yT[kk * 128:(kk + 1) * 128, :], X[:, kk, :], r=XR[kk])

    nxt_of = {layers[q]: (layers[q + 1] if q + 1 < len(layers) else None) for q in range(len(layers))}
    nomlp = bool(os.environ.get("DBG_NOMLP"))
    for q, i in enumerate(layers):
        kind, j = LAYERS[i]
        if q == 0 or nomlp:
            sched_mod(i)
        if kind == "rg":
            sched_rg(j)
        elif kind == "ssd":
            sched_ssd()
        elif kind == "gdn":
            sched_gdn()
        if not nomlp:
            sched_mlp(i, nxt_of[i])
    for q, i in enumerate(layers):
        kind, j = LAYERS[i]
        if q == 0 or nomlp:
            do_mod(i)
        mod_cur["i"] = i % 2
        if kind == "rg":
            do_rg(i, j)
        elif kind == "ssd":
            do_ssd(i)
        elif kind == "gdn":
            do_gdn(i)
        if not nomlp:
            do_mlp(i, nxt_of[i])
    if final:
        do_final()
    write_x()
    k.finish()
    _NC_CACHE["k"] = k
    k.emit()
    return nc


def _fm(v):
    v = np.asarray(v, np.float32)
    return np.ascontiguousarray(v.reshape(-1, 128).T)


def _conv_state(st, ntile):
    tok = st.shape[0]
    st = st.reshape(tok, 3, ntile, 128).transpose(3, 2, 1, 0)
    return np.ascontiguousarray(st.reshape(128, -1))


def make_in_maps(inp, x_override=None, layers=(0, 1, 2, 3)):
    have = {LAYERS[i][0] for i in layers}
    maps = []
    x_prompt = inp["x_prompt"]
    x_sample = inp["x_sample"][:, 0, :]
    bm = np.concatenate([_fm(inp["b_mod"][i]) for i in range(4)], axis=1)
    shared = {
        "w_mod": np.ascontiguousarray(inp["w_mod"]), "b_mod_t": np.ascontiguousarray(bm),
        "w_up": np.ascontiguousarray(inp["w_mlp_up"]), "w_down": np.ascontiguousarray(inp["w_mlp_down"]),
        "fng": _fm(inp["final_norm_g"]),
    }
    if "rg" in have:
        rgv = np.zeros((128, 2, 8, 8), np.float32)
        for j in range(2):
            for q in range(4):
                rgv[:, j, q, :] = _fm(inp["rg_conv_w"][j, q])
            rgv[:, j, 4, :] = _fm(inp["rg_conv_b"][j])
            rgv[:, j, 5, :] = _fm(inp["rg_gate_b"][j, 0])
            rgv[:, j, 6, :] = _fm(inp["rg_gate_b"][j, 1])
            rgv[:, j, 7, :] = _fm(inp["rg_lambda"][j])
        shared.update({"rg_w_in": np.ascontiguousarray(inp["rg_w_in"]), "rg_w_out": np.ascontiguousarray(inp["rg_w_out"]),
                       "rg_gate_w": np.ascontiguousarray(inp["rg_gate_w"]), "rg_vec": np.ascontiguousarray(rgv.reshape(128, -1))})
    if "ssd" in have:
        sv = np.zeros((128, 5, 24), np.float32)
        for q in range(4):
            sv[:, q, :] = _fm(inp["ssd_conv_w"][0, q])
        sv[:, 4, :] = _fm(inp["ssd_conv_b"][0])
        hv = np.stack([inp["ssd_A_log"][0], inp["ssd_dt_bias"][0], inp["ssd_D"][0]], 1).astype(np.float32)
        rows = np.concatenate([np.tile(inp["ssd_A_log"][0][None, :], (128, 1)), np.tile(inp["ssd_D"][0][None, :], (128, 1))], 1)
        cols = np.concatenate([_fm(inp["ssd_norm_g"][0]), _fm(np.repeat(inp["ssd_D"][0], 64))], 1)
        shared.update({"ssd_w_in": np.ascontiguousarray(inp["ssd_w_in"][0]), "ssd_w_out": np.ascontiguousarray(inp["ssd_w_out"][0]),
                       "ssd_vec": np.ascontiguousarray(sv.reshape(128, -1)), "ssd_hvec": np.ascontiguousarray(hv),
                       "ssd_rows": np.ascontiguousarray(rows.astype(np.float32)), "ssd_cols": np.ascontiguousarray(cols)})
    if "gdn" in have:
        gv = np.zeros((128, 4, 24), np.float32)
        for q in range(4):
            gv[:, q, :] = _fm(inp["gdn_conv_w"][0, q])
        rows = np.concatenate([inp["gdn_A_log"][0], inp["gdn_dt_bias"][0], inp["gdn_norm_g"][0]]).astype(np.float32)
        ii, jj = np.meshgrid(np.arange(128), np.arange(128), indexing="ij")
        msk = np.stack([((ii >> (lv + 1)) == (jj >> (lv + 1))) & (((ii >> lv) & 1) == 1) & (((jj >> lv) & 1) == 0)
                        for lv in range(7)], 1).astype(np.float32)
        shared.update({"gdn_w_in": np.ascontiguousarray(inp["gdn_w_in"][0]), "gdn_w_out": np.ascontiguousarray(inp["gdn_w_out"][0]),
                       "gdn_vec": np.ascontiguousarray(gv.reshape(128, -1)),
                       "gdn_rows": np.ascontiguousarray(np.tile(rows[None, :], (128, 1))),
                       "gdn_masks": np.ascontiguousarray(msk.reshape(128, -1))})
    for c in range(NCORES):
        s0 = c * NS
        m = dict(shared)
        if x_override is not None:
            xp_, xs_ = x_override
            m["xT"] = np.ascontiguousarray(np.concatenate([xp_[c].T, xs_[s0:s0 + NS].T], axis=1).astype(np.float32))
        else:
            m["xT"] = np.ascontiguousarray(np.concatenate([x_prompt[c].T, x_sample[s0:s0 + NS].T], axis=1))
        m["cT"] = np.ascontiguousarray(np.concatenate([inp["c_prompt"][c][:, None], inp["c_sample"][s0:s0 + NS].T], axis=1))
        if "rg" in have:
            m["rg_conv_in"] = np.stack([_conv_state(inp["state_rglru_conv"][j, s0:s0 + NS], 8) for j in range(2)], 0)
            hh = inp["state_rglru_h"][:, s0:s0 + NS].reshape(2, NS, 8, 128).transpose(0, 3, 2, 1)
            m["rg_h_in"] = np.ascontiguousarray(hh.reshape(2, 128, -1))
        if "ssd" in have:
            m["ssd_conv_in"] = _conv_state(inp["state_ssd_conv"][0, s0:s0 + NS], 24)
            m["ssd_h_in"] = np.ascontiguousarray(inp["state_ssd_h"][0, s0:s0 + NS].reshape(NS, 2048, 128))
        if "gdn" in have:
            m["gdn_conv_in"] = _conv_state(inp["state_gdn_conv"][0, s0:s0 + NS], 24)
            m["gdn_S_in"] = np.ascontiguousarray(inp["state_gdn_S"][0, s0:s0 + NS].transpose(0, 2, 1, 3).reshape(NS, 128, 1024))
        maps.append(m)
    return maps


_NC_CACHE = {}


def kernel(**inputs):
    inp = {k_: np.asarray(v) for k_, v in inputs.items()}
    if "nc" not in _NC_CACHE:
        _NC_CACHE["nc"] = build_program()
    nc = _NC_CACHE["nc"]
    maps = make_in_maps(inp)
    res = run_bass_kernel_spmd(nc, maps, core_ids=list(range(NCORES)))
    return assemble(res.results)


def _conv_out_p(a, ntile):
    return a.reshape(128, ntile, 3).transpose(2, 1, 0).reshape(3, ntile * 128)


def _conv_out_s(a, ntile):
    return a.reshape(128, ntile, 3, NS).transpose(3, 2, 1, 0).reshape(NS, 3, ntile * 128)


def assemble(rs):
    B = len(rs)
    f = lambda a: np.ascontiguousarray(a, dtype=np.float32)
    y_prompt = np.stack([rs[c]["yT"][:, :LP].T for c in range(B)], 0)
    y_sample = np.concatenate([rs[c]["yT"][:, LP:].T for c in range(B)], 0)[:, None, :]
    p_rg_conv = np.stack([np.stack([_conv_out_p(rs[c]["p_rg_conv"][j], 8) for j in range(2)], 0) for c in range(B)], 1)
    p_rg_h = np.stack([rs[c]["p_rg_h"].reshape(2, 128, 8).transpose(0, 2, 1).reshape(2, 1024) for c in range(B)], 1)
    s_rg_conv = np.concatenate([np.stack([_conv_out_s(rs[c]["s_rg_conv"][j], 8) for j in range(2)], 0) for c in range(B)], 1)
    s_rg_h = np.concatenate([rs[c]["s_rg_h"].reshape(2, 128, 8, NS).transpose(0, 3, 2, 1).reshape(2, NS, 1024) for c in range(B)], 1)
    p_gdn_conv = np.stack([_conv_out_p(rs[c]["p_gdn_conv"], 24) for c in range(B)], 0)[None]
    s_gdn_conv = np.concatenate([_conv_out_s(rs[c]["s_gdn_conv"], 24) for c in range(B)], 0)[None]
    p_gdn_S = np.stack([rs[c]["p_gdn_S"].reshape(128, 8, 128).transpose(1, 0, 2) for c in range(B)], 0)[None]
    s_gdn_S = np.concatenate([rs[c]["s_gdn_S"].reshape(NS, 128, 8, 128).transpose(0, 2, 1, 3) for c in range(B)], 0)[None]
    p_ssd_conv = np.stack([_conv_out_p(rs[c]["p_ssd_conv"], 24) for c in range(B)], 0)[None]
    s_ssd_conv = np.concatenate([_conv_out_s(rs[c]["s_ssd_conv"], 24) for c in range(B)], 0)[None]
    p_ssd_h = np.stack([rs[c]["p_ssd_hT"].reshape(128, 32, 64).transpose(1, 2, 0) for c in range(B)], 0)[None]
    s_ssd_h = np.concatenate([rs[c]["s_ssd_h"].reshape(NS, 32, 64, 128) for c in range(B)], 0)[None]
    return (f(y_prompt), f(y_sample), f(p_rg_conv), f(p_rg_h), f(p_gdn_conv), f(p_gdn_S), f(p_ssd_conv), f(p_ssd_h),
            f(s_rg_conv), f(s_rg_h), f(s_gdn_conv), f(s_gdn_S), f(s_ssd_conv), f(s_ssd_h))
```

```python
import os
import math
import numpy as np
import concourse.bass as bass
import concourse.mybir as mybir
from concourse.bass_utils import run_bass_kernel_spmd

F32 = mybir.dt.float32
BF16 = mybir.dt.bfloat16
AF = mybir.ActivationFunctionType
ALU = mybir.AluOpType
AX = mybir.AxisListType

NCORES = 8
D = 1024
KT = 8
LP = 2048
NS = 16
NT = LP + NS
EPS = 1e-6


class Reg:
    __slots__ = ("name", "lw", "rd", "excl")

    def __init__(self, name, excl=False):
        self.name = name
        self.lw = None
        self.rd = {}
        self.excl = excl


class KB:
    ENGS = ("pe", "dve", "act", "pool", "sp")
    QUEUES = ("sp", "pool")

    def __init__(self, nc):
        self.nc = nc
        self.prog = {e: [] for e in self.ENGS}
        self.sems = []
        self.cur = {}
        self.cnt = {}
        for e in self.ENGS:
            self._new_sem(e)
        self.seen = {e: {} for e in self.ENGS}
        self.dslots = {}
        self.dnext = {}
        for q, n in (("sp", 12), ("pool", 6)):
            self.dslots[q] = [[self._alloc_sem(f"d_{q}{i}"), 0] for i in range(n)]
            self.dnext[q] = 0
        self.nops = 0
        self.needed = set()

    def _alloc_sem(self, name):
        self.sems.append(self.nc.alloc_semaphore(name))
        return len(self.sems) - 1

    def _new_sem(self, e):
        self.cur[e] = self._alloc_sem(f"e_{e}{len(self.sems)}")
        self.cnt[e] = 0

    def _deps(self, e, r, w, pe_skip=True):
        deps = []
        for x in r:
            if x.lw is not None:
                deps.append(x.lw)
            if x.excl:
                for (oe, t) in x.rd.items():
                    if oe != e:
                        deps.append(t)
        for x in w:
            if x.lw is not None and x.lw[0] != e:
                deps.append(x.lw)
            for (oe, t) in x.rd.items():
                if oe != e:
                    deps.append(t)
        waits = []
        sn = self.seen[e]
        for (oe, sk, val) in deps:
            if oe == e and e == "pe":
                continue
            if sn.get(sk, 0) >= val:
                continue
            sn[sk] = val
            waits.append((sk, val))
            if oe is not None:
                self.needed.add((sk, val))
        return waits

    def op(self, e, fn, r=(), w=()):
        waits = self._deps(e, r, w)
        if self.cnt[e] >= 30000:
            self._new_sem(e)
        self.cnt[e] += 1
        tok = (e, self.cur[e], self.cnt[e])
        self.prog[e].append((waits, fn, (self.cur[e], 1)))
        for x in r:
            x.rd[e] = tok
        for x in w:
            x.lw = tok
            x.rd = {}
        self.nops += 1
        return tok

    def dma(self, q, fns, r=(), w=()):
        if not isinstance(fns, (list, tuple)):
            fns = [fns]
        waits = self._deps(q + "_dma", r, w) if False else None
        deps = []
        for x in r:
            if x.lw is not None:
                deps.append(x.lw)
        for x in w:
            if x.lw is not None:
                deps.append(x.lw)
            deps.extend(x.rd.values())
        slot = self.dslots[q][self.dnext[q]]
        self.dnext[q] = (self.dnext[q] + 1) % len(self.dslots[q])
        if slot[1] > 0:
            deps.append((None, slot[0], slot[1]))
        waits = []
        sn = self.seen[q]
        for (oe, sk, val) in deps:
            if sn.get(sk, 0) >= val:
                continue
            sn[sk] = val
            waits.append((sk, val))
            if oe is not None:
                self.needed.add((sk, val))
        for i, fn in enumerate(fns):
            slot[1] += 16
            self.prog[q].append((waits if i == 0 else [], fn, (slot[0], 16)))
        tok = (None, slot[0], slot[1])
        for x in r:
            x.rd["dma%d" % slot[0]] = tok
        for x in w:
            x.lw = tok
            x.rd = {}
        return tok

    def barrier(self):
        toks = []
        for e in self.ENGS:
            if self.cnt[e] > 0:
                toks.append((e, self.cur[e], self.cnt[e]))
        for q in self.QUEUES:
            for s in self.dslots[q]:
                if s[1] > 0:
                    toks.append((None, s[0], s[1]))
        for e in self.ENGS:
            waits = []
            sn = self.seen[e]
            for (oe, sk, val) in toks:
                if oe == e:
                    continue
                if sn.get(sk, 0) >= val:
                    continue
                sn[sk] = val
                waits.append((sk, val))
                if oe is not None:
                    self.needed.add((sk, val))
            if waits:
                self.prog[e].append((waits, None, None))

    def finish(self):
        self.barrier()

    def emit(self):
        nc = self.nc
        sems = self.sems
        prog = self.prog

        eng_sems = set()
        for e in self.ENGS:
            for (waits, fn, inc) in prog[e]:
                if fn is not None and inc[1] == 1:
                    eng_sems.add(inc[0])
        rank = {}
        by_sk = {}
        for (sk, idx) in self.needed:
            by_sk.setdefault(sk, []).append(idx)
        for sk, lst_ in by_sk.items():
            rank[sk] = {idx: r + 1 for r, idx in enumerate(sorted(lst_))}
        needed = self.needed
        opidx = {}

        def run(eng, lst):
            for (waits, fn, inc) in lst:
                for (sk, val) in waits:
                    if sk in eng_sems:
                        eng.wait_ge(sems[sk], rank[sk][val])
                    else:
                        eng.wait_ge(sems[sk], val)
                if fn is not None:
                    ins = fn(eng)
                    if inc[1] == 1:
                        opidx[inc[0]] = opidx.get(inc[0], 0) + 1
                        if (inc[0], opidx[inc[0]]) in needed:
                            ins.then_inc(sems[inc[0]], 1)
                    else:
                        ins.then_inc(sems[inc[0]], inc[1])

        with nc.Block() as block:
            @block.sync
            def _(e):
                run(e, prog["sp"])

            @block.gpsimd
            def _(e):
                run(e, prog["pool"])

            @block.vector
            def _(e):
                run(e, prog["dve"])

            @block.scalar
            def _(e):
                run(e, prog["act"])

            @block.tensor
            def _(e):
                run(e, prog["pe"])


class Mem:
    def __init__(self, nc):
        self.nc = nc
        self.off = (nc.sbuf_base + 63) // 64 * 64
        self.top = nc.sbuf_top
        self.n = 0

    def alloc(self, shape, dtype, name=None):
        nbytes = int(np.prod(shape[1:])) * (4 if dtype == F32 else 2)
        nbytes = (nbytes + 63) // 64 * 64
        off = self.off
        assert off + nbytes <= self.top, f"SBUF overflow {name} {off + nbytes} > {self.top}"
        self.off += nbytes
        self.n += 1
        return self.nc.alloc_sbuf_tensor_at(f"{name or 't'}_{self.n}", list(shape), dtype, offset=off)

    def mark(self):
        return self.off

    def release(self, m):
        self.off = m


def mk_tiles(step):
    t = [(c0, step, False, c0 // 512) for c0 in range(0, LP, step)]
    t.append((LP, NS, True, 4))
    return t


TILES = mk_tiles(512)
TILES256 = mk_tiles(256)
TILES128 = mk_tiles(128)
LAYERS = [("rg", 0), ("gdn", 0), ("ssd", 0), ("rg", 1)]


def build_program(layers=(0, 1, 2, 3), final=True):
    nc = bass.Bass("TRN2", target_bir_lowering=False)
    k = KB(nc)
    mem = Mem(nc)
    have = {LAYERS[i][0] for i in layers}

    def din(name, shape):
        return nc.dram_tensor(name, list(shape), F32, kind="ExternalInput").ap()

    def dout(name, shape):
        return nc.dram_tensor(name, list(shape), F32, kind="ExternalOutput").ap()

    xT = din("xT", [D, NT])
    cT = din("cT", [D, 17])
    w_mod = din("w_mod", [4, D, 6 * D])
    b_mod = din("b_mod_t", [128, 4 * 48])
    w_up = din("w_up", [4, D, 4 * D])
    w_down = din("w_down", [4, 4 * D, D])
    fng = din("fng", [128, 8])
    yT = dout("yT", [D, NT])
    if "rg" in have:
        rg_w_in = din("rg_w_in", [2, D, 2 * D])
        rg_w_out = din("rg_w_out", [2, D, D])
        rg_gate_w = din("rg_gate_w", [2, 2, 4, 256, 256])
        rg_vec = din("rg_vec", [128, 2 * 8 * 8])
        rg_conv_in = din("rg_conv_in", [2, 128, 8 * 3 * 16])
        rg_h_in = din("rg_h_in", [2, 128, 8 * 16])
        p_rg_conv = dout("p_rg_conv", [2, 128, 8 * 3])
        s_rg_conv = dout("s_rg_conv", [2, 128, 8 * 3 * 16])
        p_rg_h = dout("p_rg_h", [2, 128, 8])
        s_rg_h = dout("s_rg_h", [2, 128, 8 * 16])
    if "ssd" in have:
        ssd_w_in = din("ssd_w_in", [D, 5152])
        ssd_w_out = din("ssd_w_out", [2048, D])
        ssd_vec = din("ssd_vec", [128, 5 * 24])
        ssd_hvec = din("ssd_hvec", [32, 3])
        ssd_rows = din("ssd_rows", [128, 64])
        ssd_cols = din("ssd_cols", [128, 32])
        ssd_conv_in = din("ssd_conv_in", [128, 24 * 3 * 16])
        ssd_h_in = din("ssd_h_in", [NS, 2048, 128])
        p_ssd_conv = dout("p_ssd_conv", [128, 24 * 3])
        s_ssd_conv = dout("s_ssd_conv", [128, 24 * 3 * 16])
        p_ssd_hT = dout("p_ssd_hT", [128, 2048])
        s_ssd_h = dout("s_ssd_h", [NS, 2048, 128])
    if "gdn" in have:
        gdn_w_in = din("gdn_w_in", [D, 4112])
        gdn_w_out = din("gdn_w_out", [D, D])
        gdn_vec = din("gdn_vec", [128, 4 * 24])
        gdn_rows = din("gdn_rows", [128, 16 + 128])
        gdn_masks = din("gdn_masks", [128, 7 * 128])
        gdn_conv_in = din("gdn_conv_in", [128, 24 * 3 * 16])
        gdn_S_in = din("gdn_S_in", [NS, 128, 8 * 128])
        p_gdn_conv = dout("p_gdn_conv", [128, 24 * 3])
        s_gdn_conv = dout("s_gdn_conv", [128, 24 * 3 * 16])
        p_gdn_S = dout("p_gdn_S", [128, 8 * 128])
        s_gdn_S = dout("s_gdn_S", [NS, 128, 8 * 128])

    X = mem.alloc([128, KT, NT], F32, "X")
    XR = [[Reg(f"X{kk}_{t}") for t in range(5)] for kk in range(KT)]
    MODS = [mem.alloc([128, 48, 17], F32, "MOD") for _ in range(2)]
    MODRS = [Reg("MOD0"), Reg("MOD1")]
    mod_cur = {"i": 0}
    ONESB = mem.alloc([128, 128], BF16, "onesb")
    ONESF = mem.alloc([128, 128], F32, "onesf")
    IDF = mem.alloc([128, 128], F32, "idf")
    TRI = mem.alloc([128, 128], F32, "tri")
    CONST = Reg("const")
    KC = mem.alloc([128, 2], F32, "kc")
    BMOD = mem.alloc([128, 4 * 48], F32, "bmod")
    FNG = mem.alloc([128, 8], F32, "fng")
    CT_ = mem.alloc([128, KT, 17], F32, "ct")
    SC = mem.alloc([128, KT, 17], BF16, "sc")
    CTR = Reg("ct")
    SMALL = Reg("small")
    WP = [mem.alloc([128, 8, 1024], BF16, f"wp{i}") for i in range(3)]
    WPR = [Reg(f"wp{i}") for i in range(3)]
    PS = nc.alloc_psum_tensor("ps", [128, 8, 512], F32)
    PSR = [Reg(f"ps{i}", excl=True) for i in range(8)]
    ps_state = {"next": 0, "lim": 8}

    def psum():
        b = ps_state.get("base", 0) + ps_state["next"] % ps_state["lim"]
        ps_state["next"] += 1
        return b

    class QSlot:
        def __init__(self, bank, q):
            self.ap = PS[:, bank, q * 128:(q + 1) * 128]
            self.reg = PSR[bank]

    QSLOTS = [[QSlot(bank, q) for q in range(4)] for bank in range(4)]
    qs_state = [0, 0, 0, 0]

    def psq(lane):
        sl = QSLOTS[lane][qs_state[lane] % 4]
        qs_state[lane] += 1
        return sl

    class WPool:
        def __init__(self):
            self.sched = []
            self.issued = 0
            self.taken = 0
            self.released = 0

        def add(self, src_ap):
            self.sched.append(src_ap)

        def _issue(self, i):
            b = i % 3
            src = self.sched[i]
            k.dma("pool", lambda e, b=b, src=src: e.dma_start(out=WP[b][:], in_=src), w=[WPR[b]])

        def get(self):
            i = self.taken
            self.pump()
            assert self.issued > i, "weight pool deadlock: release blocks before getting more"
            self.taken += 1
            return WP[i % 3], WPR[i % 3]

        def pump(self):
            while self.issued < len(self.sched) and self.issued < self.released + 3:
                self._issue(self.issued)
                self.issued += 1

        def rel(self, n=1):
            self.released += n
            assert self.released <= self.taken
            self.pump()

    wpool = WPool()

    def blk(w_ap2d, r0, c0):
        return w_ap2d[r0:r0 + 1024, c0:c0 + 1024].rearrange("(k p) n -> p k n", p=128)

    def act(out, in_, func, r, w, **kw):
        k.op("act", lambda e: e.activation(out=out, in_=in_, func=func, **kw), r=r, w=w)

    def vtt(out, a, b_, op, r, w, eng="dve"):
        k.op(eng, lambda e: e.tensor_tensor(out=out, in0=a, in1=b_, op=op), r=r, w=w)

    def vts(out, a, s1, s2, op0, op1, r, w, eng="dve"):
        if s2 is None:
            k.op(eng, lambda e: e.tensor_scalar(out=out, in0=a, scalar1=s1, scalar2=None, op0=op0), r=r, w=w)
        else:
            k.op(eng, lambda e: e.tensor_scalar(out=out, in0=a, scalar1=s1, scalar2=s2, op0=op0, op1=op1), r=r, w=w)

    def vstt(out, a, s, b_, op0, op1, r, w, eng="dve"):
        k.op(eng, lambda e: e.scalar_tensor_tensor(out=out, in0=a, scalar=s, in1=b_, op0=op0, op1=op1), r=r, w=w)

    def vcopy(out, a, r, w, eng="dve"):
        k.op(eng, lambda e: e.tensor_copy(out=out, in_=a), r=r, w=w)

    def vmemset(out, val, w, eng="dve"):
        k.op(eng, lambda e: e.memset(out, val), w=w)

    def vscan(out, d0, d1, init, r, w):
        k.op("dve", lambda e: e.tensor_tensor_scan(out=out, data0=d0, data1=d1, initial=init, op0=ALU.mult, op1=ALU.add), r=r, w=w)

    def vrecip(out, a, r, w):
        k.op("dve", lambda e: e.reciprocal(out=out, in_=a), r=r, w=w)

    def sigmoid_L(out, in_, r, w, scale=1.0, nbias=None):
        kw = {"scale": -scale}
        if nbias is not None:
            kw["bias"] = nbias
        act(out, in_, AF.Exp, r, w, **kw)
        act(out, out, AF.Ln, w, w, bias=1.0, scale=1.0)
        act(out, out, AF.Exp, w, w, scale=-1.0)

    def rstd_L(out, in_, r, w, inv, rows=128, lnpost=None):
        act(out, in_, AF.Ln, r, w, scale=inv, bias=KC[:rows, 0:1])
        if lnpost is None:
            act(out, out, AF.Exp, w, w, scale=-0.5)
        else:
            act(out, out, AF.Exp, w, w, scale=-0.5, bias=lnpost)

    def vreduce(out, a, r, w):
        k.op("dve", lambda e: e.tensor_reduce(out=out, in_=a, axis=AX.X, op=ALU.add), r=r, w=w)

    def dma(q, out, in_, r=(), w=()):
        k.dma(q, lambda e: e.dma_start(out=out, in_=in_), r=r, w=w)

    def mm_ap(out_ap, preg, pairs, r, first=True, last=True):
        np_ = len(pairs)
        for i_, (l, rh) in enumerate(pairs):
            k.op("pe", (lambda l, rh, st, sp: (lambda e: e.matmul(out_ap, lhsT=l, rhs=rh, start=st, stop=sp)))(
                l, rh, first and i_ == 0, last and i_ == np_ - 1), r=r, w=[preg])

    def mm(b, n, pairs, r, rows=128):
        mm_ap(PS[:rows, b, :n], PSR[b], pairs, r)

    def transpose(b, in_ap, in_parts, in_free, r):
        k.op("pe", lambda e: e.transpose(PS[:in_free, b, :in_parts], in_ap, IDF[:in_parts, :in_parts]),
             r=list(r) + [CONST], w=[PSR[b]])

    def transpose_q(slot, in_ap, r):
        k.op("pe", lambda e: e.transpose(slot.ap, in_ap, IDF[:]), r=list(r) + [CONST], w=[slot.reg])

    def run_interleaved(gens):
        live = list(gens)
        while live:
            nxt = []
            for gn in live:
                try:
                    next(gn)
                    nxt.append(gn)
                except StopIteration:
                    pass
            live = nxt

    evac_flip = [0]

    def evac(out, in_, r, w):
        evac_flip[0] ^= 1
        if evac_flip[0]:
            act(out, in_, AF.Copy, r, w)
        else:
            vcopy(out, in_, r, w)

    vmemset(ONESB[:], 1.0, [CONST], eng="pool")
    vmemset(ONESF[:], 1.0, [CONST], eng="pool")
    vmemset(IDF[:], 0.0, [CONST], eng="pool")
    k.op("pool", lambda e: e.affine_select(out=IDF[:], in_=IDF[:], pattern=[[-1, 128]], compare_op=ALU.not_equal,
                                           fill=1.0, base=0, channel_multiplier=1), r=[CONST], w=[CONST])
    k.op("pool", lambda e: e.affine_select(out=TRI[:], in_=ONESF[:], pattern=[[1, 128]], compare_op=ALU.is_ge,
                                           fill=0.0, base=0, channel_multiplier=-1), r=[CONST], w=[CONST])
    vmemset(KC[:, 0:1], EPS, [CONST], eng="pool")
    vmemset(KC[:, 1:2], math.log(128.0 ** -0.5), [CONST], eng="pool")
    dma("sp", BMOD[:], b_mod, w=[SMALL])
    dma("sp", FNG[:], fng, w=[SMALL])
    for kk in range(KT):
        dma("sp", X[:, kk, :], xT[kk * 128:(kk + 1) * 128, :], w=XR[kk])
    dma("sp", CT_[:], cT.rearrange("(k p) n -> p k n", p=128), w=[CTR])
    SGC = mem.alloc([128, KT, 17], F32, "sgc")
    sigmoid_L(SGC[:], CT_[:], [CTR], [CTR])
    vtt(SC[:], CT_[:], SGC[:], ALU.mult, [CTR], [CTR])

    def modcol(v, kk):
        return MODS[mod_cur["i"]][:, v * 8 + kk, 0:1]

    def modmat(v, kk):
        return MODS[mod_cur["i"]][:, v * 8 + kk, 1:17]

    class _ModReg:
        def __getattr__(self, a):
            return getattr(MODRS[mod_cur["i"]], a)

        def __setattr__(self, a, v):
            setattr(MODRS[mod_cur["i"]], a, v)

    MODR = _ModReg()

    def sumsq(T, SQ, SQR):
        c0, n, sample, ri = T
        b = psum()
        for kk in range(KT):
            s = kk % 2
            act(SQ[s][:, :n], X[:, kk, c0:c0 + n], AF.Square, [XR[kk][ri]], [SQR[s]])
            mm_ap(PS[:, b, :n], PSR[b], [(ONESB[:], SQ[s][:, :n])], [SQR[s], CONST], first=(kk == 0), last=(kk == KT - 1))
        return b

    def rstd(b, n, inv, RS, RSR, rows=128):
        rstd_L(RS[:rows, :n], PS[:rows, b, :n], [PSR[b], CONST], [RSR], inv, rows=rows)

    def norm_mod(v_sh, v_sc, T, U, UR, scratch):
        c0, n, sample, ri = T
        SQ, SQR, RS, RSR, TMP, TMPR = scratch
        b = sumsq(T, SQ, SQR)
        rstd(b, n, 1.0 / D, RS, RSR)
        for kk in range(KT):
            s = kk % 2
            vtt(TMP[s][:, :n], X[:, kk, c0:c0 + n], RS[:, :n], ALU.mult, [XR[kk][ri], RSR], [TMPR[s]])
            if not sample:
                act(U[:, kk, :n], TMP[s][:, :n], AF.Identity, [TMPR[s], MODR], [UR],
                    scale=modcol(v_sc, kk), bias=modcol(v_sh, kk))
            else:
                vtt(TMP[s][:, :n], TMP[s][:, :n], modmat(v_sc, kk), ALU.mult, [TMPR[s], MODR], [TMPR[s]])
                vtt(U[:, kk, :n], TMP[s][:, :n], modmat(v_sh, kk), ALU.add, [TMPR[s], MODR], [UR])

    def resid_update(v_g, T, m, b, T2, T2R):
        c0, n, sample, ri = T
        xs = X[:, m, c0:c0 + n]
        if not sample:
            vstt(xs, PS[:, b, :n], modcol(v_g, m), xs, ALU.mult, ALU.add, [PSR[b], MODR, XR[m][ri]], [XR[m][ri]])
        else:
            vtt(T2[:, :n], PS[:, b, :n], modmat(v_g, m), ALU.mult, [PSR[b], MODR], [T2R])
            vtt(xs, xs, T2[:, :n], ALU.add, [T2R, XR[m][ri]], [XR[m][ri]])

    def std_scratch(w=512, nsq=2):
        SQ = [mem.alloc([128, w], BF16, "sq") for _ in range(nsq)] * (2 // nsq)
        SQR = [Reg("sq") for _ in range(nsq)] * (2 // nsq)
        RS = mem.alloc([128, w], F32, "rs")
        RSR = Reg("rs")
        TMP = [mem.alloc([128, w], F32, "tmp") for _ in range(2)]
        TMPR = [Reg("tmp") for _ in range(2)]
        return (SQ, SQR, RS, RSR, TMP, TMPR)

    def conv_tile(b, T, m, XP, XPR, CBUF, CBR, XPS, XPSR, wv, bv, out, outr, act_first=False):
        c0, n, sample, ri = T
        if not sample:
            s = m % len(XP)
            xp = XP[s]
            act(xp[:, 3:3 + n], PS[:, b, :n], AF.Copy, [PSR[b]], [XPR[s]])
            if act_first:
                act(xp[:, 0:3], CBUF[:, m, :], AF.Copy, [CBR], [XPR[s]])
            else:
                vcopy(xp[:, 0:3], CBUF[:, m, :], [CBR], [XPR[s]])
            vcopy(CBUF[:, m, :], xp[:, n:n + 3], [XPR[s]], [CBR])
            srcs = [xp[:, kq:kq + n] for kq in range(4)]
            rr = [XPR[s]]
        else:
            act(XPS[:, m, 3, :], PS[:, b, :n], AF.Copy, [PSR[b]], [XPSR])
            srcs = [XPS[:, m, kq, :] for kq in range(4)]
            rr = [XPSR]
        if act_first:
            if bv is not None:
                act(out, srcs[0], AF.Identity, rr + [SMALL], [outr], scale=wv(0, m), bias=bv(m))
            else:
                act(out, srcs[0], AF.Identity, rr + [SMALL], [outr], scale=wv(0, m))
        elif bv is not None:
            vts(out, srcs[0], wv(0, m), bv(m), ALU.mult, ALU.add, rr + [SMALL], [outr])
        else:
            vts(out, srcs[0], wv(0, m), None, ALU.mult, None, rr + [SMALL], [outr])
        for kq in range(1, 4):
            vstt(out, srcs[kq], wv(kq, m), out, ALU.mult, ALU.add, rr + [SMALL, outr], [outr])

    MOD_SPLIT = [(0, 1), (2, 3), (4,), (5,)]

    def sched_mod(i, js=range(6)):
        for j in js:
            wpool.add(blk(w_mod[i], 0, j * 1024))

    def do_mod(i, js=range(6), slot=None):
        slot = (i % 2) if slot is None else slot
        MODt, MODtr = MODS[slot], MODRS[slot]
        for j in js:
            W, WR = wpool.get()
            for m in range(8):
                b = psum()
                mm(b, 17, [(W[:, kk, m * 128:(m + 1) * 128], SC[:, kk, :]) for kk in range(KT)], r=[WR, CTR])
                col = i * 48 + j * 8 + m
                act(MODt[:, j * 8 + m, :], PS[:, b, :17], AF.Identity, [PSR[b], SMALL], [MODtr],
                    bias=BMOD[:, col:col + 1], scale=1.0)
            wpool.rel()
            if j in (1, 4):
                sl = MODt[:, j * 8:(j + 1) * 8, :]
                vts(sl, sl, 1.0, None, ALU.add, None, [MODtr], [MODtr])

    def sched_rg(j):
        for ti in range(5):
            wpool.add(blk(rg_w_in[j], 0, 0))
            wpool.add(blk(rg_w_in[j], 0, 1024))
            wpool.add(blk(rg_w_out[j], 0, 0))

    def do_rg(i, j):
        mk = mem.mark()
        RGV = mem.alloc([128, 10, 8], F32, "rgv")
        LS8 = mem.alloc([128, 8], F32, "ls8")
        GW = mem.alloc([128, 2, 4, 2, 256], BF16, "gw")
        GWR = Reg("gw")
        U = mem.alloc([128, KT, 512], BF16, "U")
        UR = Reg("U")
        HY = mem.alloc([128, KT, 512], BF16, "HY")
        HYR = Reg("HY")
        scratch = std_scratch()
        XP = [mem.alloc([128, 3 + 512], F32, "xp") for _ in range(2)]
        XPR = [Reg("xp") for _ in range(2)]
        XPS = mem.alloc([128, KT, 4, 16], F32, "xps")
        XPSR = Reg("xps")
        H0 = mem.alloc([128, KT, 16], F32, "h0")
        H0R = Reg("h0")
        HS = mem.alloc([128, KT, 16], F32, "hs")
        HSR = Reg("hs")
        CBUF = mem.alloc([128, KT, 3], F32, "cbuf")
        CBR = Reg("cbuf")
        HLAST = mem.alloc([128, KT], F32, "hlast")
        HLR = Reg("hlast")
        XC = [mem.alloc([128, 512], F32, "xc") for _ in range(2)]
        XCR = [Reg("xc") for _ in range(2)]
        XCB = [mem.alloc([128, 512], BF16, "xcb") for _ in range(2)]
        XCBR = [Reg("xcb") for _ in range(2)]
        names = ["R", "I", "A", "S", "BX", "HT", "YB"]
        TT2 = [{nm: mem.alloc([128, 512], F32, nm) for nm in names} for _ in range(2)]
        TR2 = [{nm: Reg(nm) for nm in names} for _ in range(2)]
        TT, TR = TT2[0], TR2[0]

        dma("sp", RGV[:, 0:8, :].rearrange("p b c -> p (b c)"), rg_vec[:, j * 64:(j + 1) * 64], w=[SMALL])
        vts(RGV[:, 8:10, :], RGV[:, 5:7, :], -1.0, None, ALU.mult, None, [SMALL], [SMALL])
        for g in range(2):
            dma("pool", GW[:, g], rg_gate_w[j, g].rearrange("n (jt p) o -> p n jt o", p=128), w=[GWR])
        dma("sp", XPS[:, :, 0:3, :].rearrange("p a b c -> p a (b c)"), rg_conv_in[j].rearrange("p (a bc) -> p a bc", a=8), w=[XPSR])
        dma("sp", H0[:].rearrange("p a b -> p (a b)"), rg_h_in[j], w=[H0R])
        vmemset(CBUF[:], 0.0, [CBR])
        vmemset(HLAST[:], 0.0, [HLR])
        ls = LS8[:, :]
        act(ls, RGV[:, 7, :], AF.Exp, [SMALL], [SMALL], scale=-1.0)
        act(ls, ls, AF.Ln, [SMALL], [SMALL], bias=1.0, scale=1.0)
        vts(ls, ls, -8.0, None, ALU.mult, None, [SMALL], [SMALL])

        def vec(v, m):
            return RGV[:, v, m:m + 1]

        XC4 = [XC, [mem.alloc([128, 512], F32, "xc") for _ in range(2)]]
        XCR4 = [XCR, [Reg("xc") for _ in range(2)]]
        XCB4 = [XCB, [mem.alloc([128, 512], BF16, "xcb") for _ in range(2)]]
        XCBR4 = [XCBR, [Reg("xcb") for _ in range(2)]]
        norm_mod(0, 1, TILES[0], U, UR, scratch)
        for ti, T in enumerate(TILES):
            c0, n, sample, ri = T
            W0, W0R = wpool.get()
            W1, W1R = wpool.get()

            def conv_chain(nb):
                XCn, XCRn, XCBn, XCBRn = XC4[nb % 2], XCR4[nb % 2], XCB4[nb % 2], XCBR4[nb % 2]
                for jt in range(2):
                    m = 2 * nb + jt
                    b = psum()
                    mm(b, n, [(W1[:, kk, m * 128:(m + 1) * 128], U[:, kk, :n]) for kk in range(KT)], r=[W1R, UR])
                    yield
                    conv_tile(b, T, m, XP, XPR, CBUF, CBR, XPS, XPSR, vec, lambda m_: vec(4, m_), XCn[jt][:, :n], XCRn[jt])
                    yield
                    act(XCBn[jt][:, :n], XCn[jt][:, :n], AF.Copy, [XCRn[jt]], [XCBRn[jt]])
                    yield

            def gate_chain(nb, oh):
                XCn, XCRn, XCBn, XCBRn = XC4[nb % 2], XCR4[nb % 2], XCB4[nb % 2], XCBR4[nb % 2]
                mo = 2 * nb + oh
                TT, TR = TT2[mo % 2], TR2[mo % 2]
                Rt, It, At, St, BXt, HTt, YBt = [TT[nm][:, :n] for nm in names]
                bg = []
                for g, nm in ((0, "R"), (1, "I")):
                    b = psum()
                    mm(b, n, [(GW[:, g, nb, jt, oh * 128:(oh + 1) * 128], XCBn[jt][:, :n]) for jt in range(2)],
                       r=[GWR, XCBRn[0], XCBRn[1]])
                    bg.append(b)
                by = psum()
                mm(by, n, [(W0[:, kk, mo * 128:(mo + 1) * 128], U[:, kk, :n]) for kk in range(KT)], r=[W0R, UR])
                yield
                for g, nm in ((0, "R"), (1, "I")):
                    b = bg[g]
                    act(TT[nm][:, :n], PS[:, b, :n], AF.Exp, [PSR[b], SMALL], [TR[nm]], scale=-1.0, bias=vec(8 + g, mo))
                    yield
                act(YBt, PS[:, by, :n], AF.Copy, [PSR[by]], [TR["YB"]])
                act(St, PS[:, by, :n], AF.Square, [PSR[by]], [TR["S"]])
                yield
                for nm in ("R", "I"):
                    act(TT[nm][:, :n], TT[nm][:, :n], AF.Ln, [TR[nm]], [TR[nm]], bias=1.0, scale=1.0)
                    yield
                vts(St, St, 0.044715, 1.0, ALU.mult, ALU.add, [TR["S"]], [TR["S"]])
                yield
                for nm in ("R", "I"):
                    act(TT[nm][:, :n], TT[nm][:, :n], AF.Exp, [TR[nm]], [TR[nm]], scale=-1.0)
                    yield
                vtt(St, St, YBt, ALU.mult, [TR["S"], TR["YB"]], [TR["S"]])
                yield
                act(At, Rt, AF.Exp, [TR["R"], SMALL], [TR["A"]], scale=LS8[:, mo:mo + 1])
                sigmoid_L(St, St, [TR["S"]], [TR["S"]], scale=1.5957691216057308)
                yield
                vtt(YBt, YBt, St, ALU.mult, [TR["YB"], TR["S"]], [TR["YB"]])
                vstt(St, At, -1.0, At, ALU.mult, ALU.mult, [TR["A"]], [TR["S"]])
                yield
                act(St, St, AF.Ln, [TR["S"]], [TR["S"]], bias=1.0, scale=1.0)
                yield
                act(St, St, AF.Exp, [TR["S"]], [TR["S"]], scale=0.5)
                yield
                vtt(BXt, St, It, ALU.mult, [TR["S"], TR["I"]], [TR["BX"]])
                yield
                vtt(BXt, BXt, XCn[oh][:, :n], ALU.mult, [TR["BX"], XCRn[oh]], [TR["BX"]])
                yield
                if not sample:
                    vscan(HTt, At, BXt, HLAST[:, mo:mo + 1], [TR["A"], TR["BX"], HLR], [TR["HT"]])
                    yield
                    vcopy(HLAST[:, mo:mo + 1], TT["HT"][:, n - 1:n], [TR["HT"]], [HLR])
                else:
                    vtt(HTt, At, H0[:, mo, :], ALU.mult, [TR["A"], H0R], [TR["HT"]])
                    yield
                    vtt(HTt, HTt, BXt, ALU.add, [TR["HT"], TR["BX"]], [TR["HT"]])
                    yield
                    act(HS[:, mo, :], HTt, AF.Copy, [TR["HT"]], [HSR])
                yield
                vtt(HY[:, mo, :n], HTt, YBt, ALU.mult, [TR["HT"], TR["YB"]], [HYR])

            run_interleaved([conv_chain(0)])
            for nb in range(4):
                gens = [gate_chain(nb, 0), gate_chain(nb, 1)]
                if nb < 3:
                    gens.append(conv_chain(nb + 1))
                run_interleaved(gens)
            wpool.rel(2)
            if ti + 1 < len(TILES):
                norm_mod(0, 1, TILES[ti + 1], U, UR, scratch)
            W2, W2R = wpool.get()
            for m in range(KT):
                b = psum()
                mm(b, n, [(W2[:, kk, m * 128:(m + 1) * 128], HY[:, kk, :n]) for kk in range(KT)], r=[W2R, HYR])
                resid_update(2, T, m, b, TT2[0]["R"], TR2[0]["R"])
            wpool.rel()
            if c0 + n == LP:
                dma("sp", p_rg_conv[j], CBUF[:].rearrange("p a b -> p (a b)"), r=[CBR])
                dma("sp", p_rg_h[j], HLAST[:], r=[HLR])
            if sample:
                dma("sp", s_rg_conv[j].rearrange("p (a bc) -> p a bc", a=8), XPS[:, :, 1:4, :].rearrange("p a b c -> p a (b c)"), r=[XPSR])
                dma("sp", s_rg_h[j], HS[:].rearrange("p a b -> p (a b)"), r=[HSR])
        k.barrier()
        mem.release(mk)

    def sched_ssd():
        for _ in TILES256:
            for c in range(3):
                wpool.add(blk(ssd_w_in, 0, 2048 + c * 1024))
            for c in range(2):
                wpool.add(blk(ssd_w_in, 0, c * 1024))
            for c in range(2):
                wpool.add(blk(ssd_w_out, c * 1024, 0))

    def do_ssd(i):
        mk = mem.mark()
        NW = 256
        SSV = mem.alloc([128, 5, 24], F32, "ssv")
        SSH = mem.alloc([32, 4], F32, "ssh")
        ROWS = mem.alloc([128, 64], F32, "rows")
        COLS = mem.alloc([128, 32], F32, "cols")
        WDT = mem.alloc([128, KT, 32], BF16, "wdt")
        WDTR = Reg("wdt")
        U = mem.alloc([128, KT, NW], BF16, "U")
        UR = Reg("U")
        scratch = std_scratch(NW, nsq=1)
        CBUF = mem.alloc([128, 24, 3], F32, "cbuf")
        CBR = Reg("cbuf")
        XS = mem.alloc([128, 16, NW], F32, "XS")
        XSR = [Reg(f"xs{m}") for m in range(16)]
        BT = mem.alloc([128, 4, NW], F32, "BT")
        BTR = Reg("bt")
        CTm = mem.alloc([128, 4, NW], F32, "CT")
        CTR2 = Reg("ctm")
        DTF = mem.alloc([32, NW], F32, "dtf")
        DTFR = Reg("dtf")
        XTOK = mem.alloc([128, 2048], F32, "xtok")
        XTR = Reg("xtok")
        DIAG = mem.alloc([128, 4, 128], F32, "diag")
        DIAGR = Reg("diag")
        GBS = mem.alloc([128, 4, 128], F32, "gbs")
        GBSR = Reg("gbs")
        CBM = mem.alloc([128, 4, 128], F32 if False else F32, "cbm")
        CBMR = Reg("cbm")
        HT = mem.alloc([128, 2048], F32, "HT")
        HTR = [Reg(f"ht{h}") for h in range(32)]
        YG = mem.alloc([128, 2048], F32, "YG")
        YGR = [Reg(f"yg{g}") for g in range(4)]
        ZS = DIAG[:].rearrange("p a b -> p (a b)")
        ZSR = DIAGR
        SS = mem.alloc([128, 8], F32, "ss")
        SSR = Reg("ss")
        YT = mem.alloc([128, 16, NW], BF16, "YT")
        YTR = Reg("yt")
        mp = mem.mark()
        XP = [mem.alloc([128, 3 + NW], F32, "xp")]
        XPR = [Reg("xp")]
        BTOK = mem.alloc([128, 512], BF16, "btok")
        BTKR = Reg("btok")
        SM = mem.alloc([128, 6, 32], F32, "sm")
        SMR = [Reg(f"sm{q}") for q in range(6)]
        ME4 = [mem.alloc([128, 4, 128], F32, "me4")] * 2
        ME4R = [Reg("me4")] * 2
        MT4 = [mem.alloc([128, 4, 128], BF16, "mt4")] * 2
        MT4R = [Reg("mt4")] * 2
        XDT4 = [mem.alloc([128, 4, 64], BF16, "xdt4")] * 2
        XDT4R = [Reg("xdt4")] * 2
        CD4 = mem.alloc([128, 4, 128], BF16, "cd4")
        CD4R = Reg("cd4")
        BD4 = mem.alloc([128, 4, 128], BF16, "bd4")
        BD4R = Reg("bd4")
        HTB = mem.alloc([128, 2048], BF16, "htb")
        HTBR = [Reg(f"htb{h}") for h in range(8)]
        XPS, XPSR = None, None

        dma("sp", SSV[:].rearrange("p a b -> p (a b)"), ssd_vec, w=[SMALL])
        dma("sp", SSH[:, 0:3], ssd_hvec, w=[SMALL])
        dma("sp", ROWS[:], ssd_rows, w=[SMALL])
        dma("sp", COLS[:], ssd_cols, w=[SMALL])
        dma("pool", WDT[:], ssd_w_in[:, 5120:5152].rearrange("(k p) n -> p k n", p=128), w=[WDTR])
        act(ROWS[:, 0:32], ROWS[:, 0:32], AF.Exp, [SMALL], [SMALL])
        vts(ROWS[:, 0:32], ROWS[:, 0:32], -1.0, None, ALU.mult, None, [SMALL], [SMALL])
        act(SSH[:, 3:4], SSH[:, 0:1], AF.Exp, [SMALL], [SMALL])
        vts(SSH[:, 3:4], SSH[:, 3:4], -1.0, None, ALU.mult, None, [SMALL], [SMALL])
        ANEG = ROWS[:, 0:32]
        DB = ROWS[:, 32:64]
        vmemset(CBUF[:], 0.0, [CBR])
        vmemset(HT[:], 0.0, HTR)
        vmemset(HTB[:], 0.0, HTBR)
        DTT, ATOK, ACS, DS, CD, TM = [SM[:, q, :] for q in range(6)]

        def wv(kq, m):
            return SSV[:, kq, m:m + 1]

        def bv(m):
            return SSV[:, 4, m:m + 1]

        def post_group(rows, g, ucols, WZ, WZR, have_o, obank):
            yg = YG[:rows, g * 512:(g + 1) * 512]
            if have_o:
                vtt(yg.rearrange("p (h q) -> p h q", q=64), XTOK[:rows, g * 512:(g + 1) * 512].rearrange("p (h q) -> p h q", q=64),
                    DB[:rows, g * 8:(g + 1) * 8].unsqueeze(2).to_broadcast([rows, 8, 64]), ALU.mult, [XTR, SMALL], [YGR[g]])
                vtt(yg, yg, PS[:rows, obank, :512], ALU.add, [YGR[g], PSR[obank]], [YGR[g]])
            bz = psum()
            W, WR = WZ[g // 2], WZR[g // 2]
            mm(bz, 512, [(U[:, kk, ucols[0]:ucols[1]], W[:, kk, (g % 2) * 512:(g % 2) * 512 + 512]) for kk in range(KT)],
               r=[UR, WR], rows=rows)
            sigmoid_L(ZS[:rows, :], PS[:rows, bz, :512], [PSR[bz]], [ZSR])
            vtt(yg, yg, PS[:rows, bz, :512], ALU.mult, [YGR[g], PSR[bz]], [YGR[g]])
            vtt(yg, yg, ZS[:rows, :], ALU.mult, [YGR[g], ZSR], [YGR[g]])
            vtt(ZS[:rows, :], yg, yg, ALU.mult, [YGR[g], ZSR], [ZSR])
            vreduce(SS[:rows, g:g + 1], ZS[:rows, :], [ZSR], [SSR])
            rstd_L(SS[:rows, g:g + 1], SS[:rows, g:g + 1], [SSR, CONST], [SSR], 1.0 / 512, rows=rows)
            vts(yg, yg, SS[:rows, g:g + 1], None, ALU.mult, None, [YGR[g], SSR], [YGR[g]])

        def to_feature_major(rows, ocol):
            for jt in range(16):
                b = psum()
                transpose(b, YG[:rows, jt * 128:(jt + 1) * 128], rows, 128, [YGR[jt // 4]])
                act(YT[:, jt, ocol:ocol + rows], PS[:, b, :rows], AF.Copy, [PSR[b], SMALL], [YTR], scale=COLS[:, jt:jt + 1])

        ps_state["lim"] = 4
        norm_mod(0, 1, TILES256[0], U, UR, scratch)
        for ti, T in enumerate(TILES256):
            c0, n, sample, ri = T
            if sample:
                k.barrier()
                mem.release(mp)
                XPS = mem.alloc([128, 24, 4, 16], F32, "xps")
                XPSR = Reg("xps")
                dma("sp", XPS[:, :, 0:3, :].rearrange("p a b c -> p a (b c)"), ssd_conv_in.rearrange("p (a bc) -> p a bc", a=24), w=[XPSR])
            for cb in range(3):
                W, WR = wpool.get()
                for mm_ in range(8):
                    m = cb * 8 + mm_
                    b = psum()
                    mm(b, n, [(W[:, kk, mm_ * 128:(mm_ + 1) * 128], U[:, kk, :n]) for kk in range(KT)], r=[WR, UR])
                    if m < 16:
                        dst, dr = XS[:, m, :n], XSR[m]
                    elif m < 20:
                        dst, dr = BT[:, m - 16, :n], BTR
                    else:
                        dst, dr = CTm[:, m - 20, :n], CTR2
                    conv_tile(b, T, m, XP, XPR, CBUF, CBR, XPS, XPSR, wv, bv, dst, dr, act_first=True)
                    sg = scratch[4][m % 2][:, :n]
                    sgr = scratch[5][m % 2]
                    sigmoid_L(sg, dst, [dr], [sgr])
                    vtt(dst, dst, sg, ALU.mult, [dr, sgr], [dr])
                wpool.rel()
            b = psum()
            mm(b, n, [(WDT[:, kk, :], U[:, kk, :n]) for kk in range(KT)], r=[WDTR, UR], rows=32)
            act(DTF[:, :n], PS[:32, b, :n], AF.Exp, [PSR[b], SMALL], [DTFR], bias=SSH[:, 1:2], scale=1.0)
            act(DTF[:, :n], DTF[:, :n], AF.Ln, [DTFR], [DTFR], bias=1.0, scale=1.0)
            WZ0, WZ0R = wpool.get()
            WZ1, WZ1R = wpool.get()
            WZ, WZR = [WZ0, WZ1], [WZ0R, WZ1R]
            if not sample:
                for ch in range(n // 128):
                    o = ch * 128
                    for jt in range(16):
                        b = psum()
                        transpose(b, XS[:, jt, o:o + 128], 128, 128, [XSR[jt]])
                        act(XTOK[:, jt * 128:(jt + 1) * 128], PS[:, b, :128], AF.Copy, [PSR[b]], [XTR])
                    for g in range(4):
                        b = psum()
                        transpose(b, BT[:, g, o:o + 128], 128, 128, [BTR])
                        evac(BTOK[:, g * 128:(g + 1) * 128], PS[:, b, :128], [PSR[b]], [BTKR])
                    b = psum()
                    transpose(b, DTF[:, o:o + 128], 32, 128, [DTFR])
                    vcopy(DTT, PS[:, b, :32], [PSR[b]], [SMR[0]])
                    vtt(ATOK, DTT, ANEG, ALU.mult, [SMR[0], SMALL], [SMR[1]])
                    b = psum()
                    mm(b, 32, [(TRI[:], ATOK)], r=[CONST, SMR[1]])
                    vcopy(ACS, PS[:, b, :32], [PSR[b]], [SMR[2]])
                    b = psum()
                    mm(b, 32, [(ONESF[:], ATOK)], r=[CONST, SMR[1]])
                    vtt(TM, PS[:, b, :32], ACS, ALU.subtract, [PSR[b], SMR[2]], [SMR[5]])
                    act(DS, TM, AF.Exp, [SMR[5]], [SMR[3]])
                    act(CD, PS[:, b, :32], AF.Exp, [PSR[b]], [SMR[4]])
                    for g in range(4):
                        b = psum()
                        mm(b, 128, [(BT[:, g, o:o + 128], CTm[:, g, o:o + 128])], r=[BTR, CTR2])
                        vtt(CBM[:, g, :], PS[:, b, :128], TRI[:], ALU.mult, [PSR[b], CONST], [CBMR])
                    def bc(ap2, w):
                        return ap2.unsqueeze(2).to_broadcast([128, 4, w])

                    def emit_gb(hg_):
                        hsl_ = slice(hg_ * 4, hg_ * 4 + 4)
                        vtt(DIAG[:], IDF[:].unsqueeze(1).to_broadcast([128, 4, 128]), bc(ACS[:, hsl_], 128), ALU.mult,
                            [CONST, SMR[2]], [DIAGR])
                        bq = psum()
                        mm(bq, 512, [(ONESF[:], DIAG[:].rearrange("p a b -> p (a b)"))], r=[CONST, DIAGR])
                        return bq

                    bgb = emit_gb(0)
                    act(GBS[:].rearrange("p a b -> p (a b)"), PS[:, bgb, :512], AF.Copy, [PSR[bgb]], [GBSR])
                    for hg in range(8):
                        g = hg // 2
                        s4 = hg % 2
                        hsl = slice(hg * 4, hg * 4 + 4)
                        mt, mtr = MT4[s4], MT4R[s4]
                        me, mer = ME4[s4], ME4R[s4]
                        xd, xdr = XDT4[s4], XDT4R[s4]
                        act(me[:], GBS[:], AF.Exp, [GBSR], [mer])
                        vtt(CD4[:], me[:], CTm[:, g, o:o + 128].unsqueeze(1).to_broadcast([128, 4, 128]), ALU.mult,
                            [mer, CTR2], [CD4R])
                        vtt(me[:], GBS[:], bc(ACS[:, hsl], 128), ALU.subtract, [GBSR, SMR[2]], [mer])
                        act(me[:], me[:], AF.Relu, [mer], [mer], scale=-1.0)
                        act(me[:], me[:], AF.Exp, [mer], [mer], scale=-1.0)
                        if hg + 1 < 8:
                            bgb = emit_gb(hg + 1)
                        vtt(xd[:], XTOK[:, hg * 256:(hg + 1) * 256].rearrange("p (h q) -> p h q", q=64), bc(DTT[:, hsl], 64),
                            ALU.mult, [XTR, SMR[0]], [xdr])
                        vtt(BD4[:], BTOK[:, g * 128:(g + 1) * 128].unsqueeze(1).to_broadcast([128, 4, 128]), bc(DS[:, hsl], 128),
                            ALU.mult, [BTKR, SMR[3]], [BD4R])
                        vtt(mt[:], me[:], CBM[:, g, :].unsqueeze(1).to_broadcast([128, 4, 128]), ALU.mult, [mer, CBMR], [mtr])
                        if hg + 1 < 8:
                            act(GBS[:].rearrange("p a b -> p (a b)"), PS[:, bgb, :512], AF.Copy, [PSR[bgb]], [GBSR])
                        ob = 4 + g
                        b2 = psum()
                        for hh in range(4):
                            h = hg * 4 + hh
                            oc = (h % 8) * 64
                            mm_ap(PS[:, ob, oc:oc + 64], PSR[ob], [(mt[:, hh, :], xd[:, hh, :]), (CD4[:, hh, :], HTB[:, h * 64:(h + 1) * 64])],
                                  r=[mtr, xdr, CD4R, HTBR[hg]])
                        for hh in range(4):
                            mm_ap(PS[:, b2, hh * 64:(hh + 1) * 64], PSR[b2], [(BD4[:, hh, :], xd[:, hh, :])], r=[BD4R, xdr])
                        ht4 = HT[:, hg * 256:(hg + 1) * 256].rearrange("p (h q) -> p h q", q=64)
                        hrs = [HTR[hg * 4 + hh] for hh in range(4)]
                        vtt(ht4, ht4, bc(CD[:, hsl], 64), ALU.mult, hrs + [SMR[4]], hrs)
                        vtt(ht4, ht4, PS[:, b2, :256].rearrange("p (h q) -> p h q", q=64), ALU.add, hrs + [PSR[b2]], hrs)
                        act(HTB[:, hg * 256:(hg + 1) * 256], HT[:, hg * 256:(hg + 1) * 256], AF.Copy, hrs, [HTBR[hg]])
                    for g in range(4):
                        post_group(128, g, (o, o + 128), WZ, WZR, True, 4 + g)
                    to_feature_major(128, o)
            else:
                do_ssd_sample(locals())
            wpool.rel(2)
            if ti + 1 < len(TILES256):
                norm_mod(0, 1, TILES256[ti + 1], U, UR, scratch)
            O0, O0R = wpool.get()
            O1, O1R = wpool.get()
            for m in range(KT):
                b = psum()
                mm(b, n, [(O0[:, kk, m * 128:(m + 1) * 128], YT[:, kk, :n]) for kk in range(KT)] +
                   [(O1[:, kk, m * 128:(m + 1) * 128], YT[:, 8 + kk, :n]) for kk in range(KT)], r=[O0R, O1R, YTR])
                resid_update(2, T, m, b, scratch[2], scratch[3])
            wpool.rel(2)
            if c0 + n == LP:
                dma("sp", p_ssd_conv, CBUF[:].rearrange("p a b -> p (a b)"), r=[CBR])
                dma("sp", p_ssd_hT, HT[:], r=HTR)
        ps_state["lim"] = 8
        k.barrier()
        mem.release(mk)

    def do_ssd_sample(L):
        (XS, XSR, BT, BTR, CTm, CTR2, DTF, DTFR, SSH, COLS, XPS, XPSR, YG, YGR, HT, HTR, XTOK, XTR,
         DIAG, DIAGR, GBS, GBSR, CBM, CBMR, WZ, WZR, post_group, to_feature_major) = [L[q] for q in (
            "XS", "XSR", "BT", "BTR", "CTm", "CTR2", "DTF", "DTFR", "SSH", "COLS", "XPS", "XPSR", "YG", "YGR", "HT", "HTR",
            "XTOK", "XTR", "DIAG", "DIAGR", "GBS", "GBSR", "CBM", "CBMR", "WZ", "WZR", "post_group",
            "to_feature_major")]
        mk = mem.mark()
        EXPM = YG[:32, :]
        EXR = YGR[0]
        ADT = mem.alloc([32, 16], F32, "adt")
        ADR = Reg("adt")
        DECX = mem.alloc([128, 16, 16], F32, "decx")
        DTX = mem.alloc([128, 16, 16], F32, "dtx")
        DXR = Reg("dx")
        YS = mem.alloc([128, 16, 16], F32, "ys")
        YSR = Reg("ys")
        HB = [XTOK[:].rearrange("p (a b) -> p a b", a=16), XS[:, :, 128:256]]
        HBR = [XTR, Reg("hb1")]
        TMPB = YG[:].rearrange("p (a b) -> p a b", a=16)
        BBC = GBS
        CBC = CBM
        vmemset(EXPM, 1.0, YGR, eng="pool")
        k.op("pool", lambda e: e.affine_select(out=EXPM, in_=EXPM, pattern=[[1, 2048]], compare_op=ALU.is_ge,
                                               fill=0.0, base=0, channel_multiplier=-64), r=YGR, w=YGR)
        k.op("pool", lambda e: e.affine_select(out=EXPM, in_=EXPM, pattern=[[-1, 2048]], compare_op=ALU.is_ge,
                                               fill=0.0, base=63, channel_multiplier=64), r=YGR, w=YGR)
        vts(ADT[:], DTF[:, :16], SSH[:, 3:4], None, ALU.mult, None, [DTFR, SMALL], [ADR])
        for jt in range(16):
            b = psum()
            mm(b, 16, [(EXPM[:, jt * 128:(jt + 1) * 128], ADT[:])], r=YGR + [ADR])
            act(DECX[:, jt, :], PS[:, b, :16], AF.Exp, [PSR[b]], [DXR])
            b = psum()
            mm(b, 16, [(EXPM[:, jt * 128:(jt + 1) * 128], DTF[:, :16])], r=YGR + [DTFR])
            vtt(DTX[:, jt, :], PS[:, b, :16], XS[:, jt, :16], ALU.mult, [PSR[b], XSR[jt]], [DXR])

        def load_state(t):
            dma("sp", HB[t % 2], ssd_h_in[t].rearrange("(a p) n -> p a n", p=128), w=[HBR[t % 2]])

        load_state(0)
        for t in range(NS):
            if t + 1 < NS:
                load_state(t + 1)
            H0, H0R_ = HB[t % 2], HBR[t % 2]
            for (src, srcr, dst, dstr) in ((BT, BTR, BBC, GBSR), (CTm, CTR2, CBC, CBMR)):
                vtt(DIAG[:], IDF[:].unsqueeze(1).to_broadcast([128, 4, 128]), src[:, :, t:t + 1].to_broadcast([128, 4, 128]),
                    ALU.mult, [CONST, srcr], [DIAGR])
                b = psum()
                mm(b, 512, [(ONESF[:], DIAG[:].rearrange("p a b -> p (a b)"))], r=[CONST, DIAGR])
                act(dst[:].rearrange("p a b -> p (a b)"), PS[:, b, :512], AF.Copy, [PSR[b]], [dstr])
            for jt in range(16):
                g = jt // 4
                vts(TMPB[:, jt, :], BBC[:, g, :], DTX[:, jt, t:t + 1], None, ALU.mult, None, [GBSR, DXR], YGR)
                vstt(H0[:, jt, :], H0[:, jt, :], DECX[:, jt, t:t + 1], TMPB[:, jt, :], ALU.mult, ALU.add, [H0R_, DXR] + YGR, [H0R_])
                vtt(TMPB[:, jt, :], H0[:, jt, :], CBC[:, g, :], ALU.mult, [H0R_, CBMR], YGR)
            vreduce(YS[:, :, t:t + 1].rearrange("p a b -> p (a b)"), TMPB, YGR, [YSR])
            dma("sp", s_ssd_h[t].rearrange("(a p) n -> p a n", p=128), H0, r=[H0R_])
        for jt in range(16):
            vstt(YS[:, jt, :], XS[:, jt, :16], COLS[:, 16 + jt:17 + jt], YS[:, jt, :], ALU.mult, ALU.add, [XSR[jt], SMALL, YSR], [YSR])
        for jt in range(16):
            b = psum()
            transpose(b, YS[:, jt, :], 128, 16, [YSR])
            evac(YG[:16, jt * 128:(jt + 1) * 128], PS[:16, b, :128], [PSR[b]], [YGR[jt // 4]])
        for g in range(4):
            post_group(16, g, (0, 16), WZ, WZR, False, None)
        to_feature_major(16, 0)
        dma("sp", s_ssd_conv.rearrange("p (a bc) -> p a bc", a=24), XPS[:, :, 1:4, :].rearrange("p a b c -> p a (b c)"), r=[XPSR])

    def sched_gdn():
        for _ in TILES128:
            for c in range(3):
                wpool.add(blk(gdn_w_in, 0, c * 1024))
            wpool.add(blk(gdn_w_in, 0, 3072))
            wpool.add(blk(gdn_w_out, 0, 0))

    def do_gdn(i):
        mk = mem.mark()
        NW = 128
        GV = mem.alloc([128, 4, 24], F32, "gv")
        ROWS = mem.alloc([128, 16 + 128], F32, "rows")
        WAB = mem.alloc([128, KT, 16], BF16, "wab")
        WABR = Reg("wab")
        STL = mem.alloc([128, 128], F32, "stl")
        U = mem.alloc([128, KT, NW], BF16, "U")
        UR = Reg("U")
        scratch = std_scratch(NW)
        SQ, SQR, RS, RSR, TMP, TMPR = scratch
        CBUF = mem.alloc([128, 24, 3], F32, "cbuf")
        CBR = Reg("cbuf")
        QKV = mem.alloc([128, 24, NW], F32, "qkv")
        QR = [Reg(f"qkv{m}") for m in range(24)]
        S = mem.alloc([128, 8, 128], F32, "S")
        SR = [Reg(f"S{h}") for h in range(8)]
        OG = mem.alloc([128, 1024], F32, "og")
        OGR = Reg("og")
        ZS = mem.alloc([128, 512], F32, "zs")
        ZSR = Reg("zs")
        SG8 = mem.alloc([128, 8, NW], F32, "sg8")
        SG8R = Reg("sg8")
        YT = mem.alloc([128, 8, NW], BF16, "YT")
        YTR = Reg("yt")
        SM = mem.alloc([128, 10, 8], F32, "sm")
        SMR = Reg("smg")
        AB, BETA, NBETA, GTOK, GC, DSg, CDg, BEG, SSg, TM8 = [SM[:, q, :] for q in range(10)]
        DG = mem.alloc([128, 16, 128], F32, "dg")
        DIAG = DG[:, 0:8, :]
        DIAGR = Reg("diag")
        GBS = DG[:, 8:16, :]
        GBSR = Reg("gbs")
        MSK = mem.alloc([128, 7, 128], F32, "msk")
        MSKR = Reg("msk")
        dma("sp", MSK[:].rearrange("p a b -> p (a b)"), gdn_masks, w=[MSKR])
        mp = mem.mark()
        XP = [mem.alloc([128, 3 + NW], F32, "xp") for _ in range(2)]
        XPR = [Reg("xp") for _ in range(2)]
        NLANE = 2
        slot_names = ["KT", "VT", "D0", "DM", "DTM", "ZT", "DD", "TT", "RB1"]
        LB = [{q: mem.alloc([128, 4, 128], F32, q) for q in slot_names} for _ in range(NLANE)]
        LR = [{q: Reg(q) for q in slot_names} for _ in range(NLANE)]

        dma("sp", GV[:].rearrange("p a b -> p (a b)"), gdn_vec, w=[SMALL])
        dma("sp", ROWS[:], gdn_rows, w=[SMALL])
        dma("pool", WAB[:], gdn_w_in[:, 4096:4112].rearrange("(k p) n -> p k n", p=128), w=[WABR])
        act(ROWS[:, 0:8], ROWS[:, 0:8], AF.Exp, [SMALL], [SMALL])
        vts(ROWS[:, 0:8], ROWS[:, 0:8], -1.0, None, ALU.mult, None, [SMALL], [SMALL])
        ANEG = ROWS[:, 0:8]
        DTB = ROWS[:, 8:16]
        NGR = ROWS[:, 16:144]
        vts(STL[:], TRI[:], -1.0, 1.0, ALU.mult, ALU.add, [CONST], [SMALL])
        vmemset(CBUF[:], 0.0, [CBR])
        vmemset(S[:], 0.0, SR)

        def wv(kq, m):
            return GV[:, kq, m:m + 1]

        def tok_gates(rows, ucols):
            b = psum()
            mm(b, 16, [(U[:, kk, ucols[0]:ucols[1]], WAB[:, kk, :]) for kk in range(KT)], r=[UR, WABR], rows=rows)
            sigmoid_L(BETA[:rows], PS[:rows, b, 8:16], [PSR[b]], [SMR])
            vts(NBETA[:rows], BETA[:rows], -1.0, None, ALU.mult, None, [SMR], [SMR])
            vtt(AB[:rows], PS[:rows, b, 0:8], DTB[:rows], ALU.add, [PSR[b], SMALL], [SMR])
            act(AB[:rows], AB[:rows], AF.Exp, [SMR], [SMR])
            act(AB[:rows], AB[:rows], AF.Ln, [SMR], [SMR], bias=1.0, scale=1.0)
            vtt(GTOK[:rows], AB[:rows], ANEG[:rows], ALU.mult, [SMR, SMALL], [SMR])

        def post(rows, ucols, WG, WGR, ocol):
            og3 = OG[:rows, :].rearrange("p (h d) -> p h d", d=128)
            for half in range(2):
                zz = ZS[:rows, :]
                sl = OG[:rows, half * 512:(half + 1) * 512]
                vtt(zz, sl, sl, ALU.mult, [OGR, ZSR], [ZSR])
                vreduce(SSg[:rows, half * 4:(half + 1) * 4], zz.rearrange("p (h d) -> p h d", d=128), [ZSR], [SMR])
            rstd_L(SSg[:rows], SSg[:rows], [SMR, CONST], [SMR], 1.0 / 128, rows=rows)
            vtt(og3, og3, SSg[:rows].unsqueeze(2).to_broadcast([rows, 8, 128]), ALU.mult, [OGR, SMR], [OGR])
            vtt(og3, og3, NGR[:rows].unsqueeze(1).to_broadcast([rows, 8, 128]), ALU.mult, [OGR, SMALL], [OGR])
            for half in range(2):
                bz = psum()
                mm(bz, 512, [(U[:, kk, ucols[0]:ucols[1]], WG[:, kk, half * 512:(half + 1) * 512]) for kk in range(KT)],
                   r=[UR, WGR], rows=rows)
                sigmoid_L(ZS[:rows, :], PS[:rows, bz, :512], [PSR[bz]], [ZSR])
                sl = OG[:rows, half * 512:(half + 1) * 512]
                vtt(sl, sl, PS[:rows, bz, :512], ALU.mult, [OGR, PSR[bz]], [OGR])
                vtt(sl, sl, ZS[:rows, :], ALU.mult, [OGR, ZSR], [OGR])
            for h in range(8):
                b = psum()
                transpose(b, OG[:rows, h * 128:(h + 1) * 128], rows, 128, [OGR])
                evac(YT[:, h, ocol:ocol + rows], PS[:, b, :rows], [PSR[b]], [YTR])

        ps_state["lim"] = 6
        ps_state["base"] = 0
        norm_mod(0, 1, TILES128[0], U, UR, scratch)
        for ti, T in enumerate(TILES128):
            c0, n, sample, ri = T
            if sample:
                k.barrier()
                mem.release(mp)
                XPS = mem.alloc([128, 24, 4, 16], F32, "xps")
                XPSR = Reg("xps")
                dma("sp", XPS[:, :, 0:3, :].rearrange("p a b c -> p a (b c)"), gdn_conv_in.rearrange("p (a bc) -> p a bc", a=24), w=[XPSR])
            else:
                XPS, XPSR = None, None
            for cb in range(3):
                W, WR = wpool.get()
                for mm_ in range(8):
                    m = cb * 8 + mm_
                    b = psum()
                    mm(b, n, [(W[:, kk, mm_ * 128:(mm_ + 1) * 128], U[:, kk, :n]) for kk in range(KT)], r=[WR, UR])
                    conv_tile(b, T, m, XP, XPR, CBUF, CBR, XPS, XPSR, wv, None, QKV[:, m, :n], QR[m], act_first=True)
                wpool.rel()
                blkv = QKV[:, cb * 8:(cb + 1) * 8, :n]
                brs = QR[cb * 8:(cb + 1) * 8]
                sg = SG8[:, :, :n]
                sigmoid_L(sg, blkv, brs, [SG8R])
                vtt(blkv, blkv, sg, ALU.mult, brs + [SG8R], brs)
                if cb < 2:
                    act(sg, blkv, AF.Square, brs, [SG8R])
                    if not sample:
                        bb = []
                        for half in range(2):
                            b2 = psum()
                            mm(b2, 512, [(ONESF[:], SG8[:, half * 4:(half + 1) * 4, :].rearrange("p a b -> p (a b)"))], r=[CONST, SG8R])
                            bb.append(b2)
                        for half in range(2):
                            rstd_L(SG8[:, half * 4:(half + 1) * 4, :].rearrange("p a b -> p (a b)"), PS[:, bb[half], :512],
                                   [PSR[bb[half]], CONST], [SG8R], 1.0, lnpost=(KC[:, 1:2] if cb == 0 else None))
                    else:
                        b2 = psum()
                        for mm_ in range(8):
                            mm_ap(PS[:, b2, mm_ * 16:(mm_ + 1) * 16], PSR[b2], [(ONESF[:], SG8[:, mm_, :16])], r=[CONST, SG8R])
                        rstd_L(sg, PS[:, b2, :128].rearrange("p (a b) -> p a b", a=8), [PSR[b2], CONST], [SG8R], 1.0,
                               lnpost=(KC[:, 1:2] if cb == 0 else None))
                    vtt(blkv, blkv, sg, ALU.mult, brs + [SG8R], brs)
            WG, WGR = wpool.get()
            if not sample:
                for ch in range(n // 128):
                    o = ch * 128
                    tok_gates(128, (o, o + 128))
                    b = psum()
                    mm(b, 8, [(TRI[:], GTOK)], r=[CONST, SMR])
                    vcopy(GC, PS[:, b, :8], [PSR[b]], [SMR])
                    b = psum()
                    mm(b, 8, [(ONESF[:], GTOK)], r=[CONST, SMR])
                    vtt(TM8, PS[:, b, :8], GC, ALU.subtract, [PSR[b], SMR], [SMR])
                    act(DSg, TM8, AF.Exp, [SMR], [SMR])
                    act(CDg, PS[:, b, :8], AF.Exp, [PSR[b]], [SMR])
                    act(TM8, GC, AF.Exp, [SMR], [SMR])
                    vtt(BEG, TM8, BETA, ALU.mult, [SMR], [SMR])
                    for h in range(8):
                        vts(DIAG[:, h, :], IDF[:], GC[:, h:h + 1], None, ALU.mult, None, [CONST, SMR], [DIAGR])
                    for half in range(2):
                        b = psum()
                        mm(b, 512, [(ONESF[:], DG[:, half * 4:(half + 1) * 4, :].rearrange("p a b -> p (a b)"))], r=[CONST, DIAGR])
                        act(DG[:, 8 + half * 4:8 + (half + 1) * 4, :].rearrange("p a b -> p (a b)"), PS[:, b, :512], AF.Copy, [PSR[b]], [GBSR])
                    def bc4(ap2, w=128):
                        return ap2.unsqueeze(2).to_broadcast([128, 4, w])

                    def cb4(ap2):
                        return ap2.unsqueeze(1).to_broadcast([128, 4, 128])

                    def flat(t):
                        return t[:].rearrange("p a b -> p (a b)")

                    def pe4(b, fn, r):
                        for hh in range(4):
                            mm_ap(PS[:, b, hh * 128:(hh + 1) * 128], PSR[b], fn(hh), r=r)

                    def tr4(b, src4, r):
                        for hh in range(4):
                            k.op("pe", (lambda hh: (lambda e: e.transpose(PS[:, b, hh * 128:(hh + 1) * 128], src4(hh), IDF[:])))(hh),
                                 r=list(r) + [CONST], w=[PSR[b]])

                    def group_chain(gq, B, R):
                        hsl = slice(gq * 4, gq * 4 + 4)
                        hs = [gq * 4 + hh for hh in range(4)]
                        rq = [QR[h] for h in hs]
                        rk = [QR[8 + h] for h in hs]
                        rv = [QR[16 + h] for h in hs]
                        sr = [SR[h] for h in hs]
                        qT4 = QKV[:, gq * 4:gq * 4 + 4, o:o + 128]
                        S4 = S[:, hsl, :]
                        ps4 = lambda b: PS[:, b, :512].rearrange("p (a b) -> p a b", a=4)
                        b = psum()
                        tr4(b, lambda hh: QKV[:, 8 + hs[hh], o:o + 128], rk)
                        evac(flat(B["KT"]), PS[:, b, :512], [PSR[b]], [R["KT"]])
                        b = psum()
                        tr4(b, lambda hh: QKV[:, 16 + hs[hh], o:o + 128], rv)
                        evac(flat(B["VT"]), PS[:, b, :512], [PSR[b]], [R["VT"]])
                        vtt(B["D0"][:], GBS[:, hsl, :], bc4(GC[:, hsl]), ALU.subtract, [GBSR, SMR], [R["D0"]])
                        yield
                        vts(flat(B["DM"]), flat(B["D0"]), 0.0, None, ALU.max, None, [R["D0"]], [R["DM"]])
                        vts(flat(B["DTM"]), flat(B["D0"]), 0.0, None, ALU.min, None, [R["D0"]], [R["DTM"]])
                        bk = psum()
                        pe4(bk, lambda hh: [(QKV[:, 8 + hs[hh], o:o + 128], QKV[:, 8 + hs[hh], o:o + 128])], rk)
                        yield
                        act(flat(B["DM"]), flat(B["DM"]), AF.Exp, [R["DM"]], [R["DM"]], scale=-1.0)
                        act(flat(B["DTM"]), flat(B["DTM"]), AF.Exp, [R["DTM"]], [R["DTM"]])
                        yield
                        vtt(B["DM"][:], B["DM"][:], cb4(STL[:]), ALU.mult, [R["DM"], SMALL], [R["DM"]])
                        vtt(B["DTM"][:], B["DTM"][:], cb4(TRI[:]), ALU.mult, [R["DTM"], CONST], [R["DTM"]])
                        yield
                        vtt(B["DM"][:], ps4(bk), B["DM"][:], ALU.mult, [PSR[bk], R["DM"]], [R["DM"]])
                        vtt(B["DM"][:], B["DM"][:], bc4(NBETA[:, hsl]), ALU.mult, [R["DM"], SMR], [R["DM"]])
                        NA, NAR = B["DM"], R["DM"]
                        yield
                        first = True
                        for lv in range(7):
                            vtt(B["D0"][:], NA[:], cb4(MSK[:, lv, :]), ALU.mult, [NAR, MSKR], [R["D0"]])
                            yield
                            b = psum()
                            if first:
                                pe4(b, lambda hh: [(B["D0"][:, hh, :], IDF[:])], [R["D0"], CONST])
                            else:
                                pe4(b, lambda hh: [(B["D0"][:, hh, :], B["TT"][:, hh, :])], [R["D0"], R["TT"]])
                            yield
                            evac(flat(B["ZT"]), PS[:, b, :512], [PSR[b]], [R["ZT"]])
                            yield
                            b = psum()
                            if first:
                                pe4(b, lambda hh: [(IDF[:], B["ZT"][:, hh, :])], [CONST, R["ZT"]])
                            else:
                                pe4(b, lambda hh: [(B["DD"][:, hh, :], B["ZT"][:, hh, :])], [R["DD"], R["ZT"]])
                            yield
                            if first:
                                vtt(B["TT"][:], cb4(IDF[:]), ps4(b), ALU.add, [CONST, PSR[b]], [R["TT"]])
                            else:
                                vtt(B["TT"][:], B["TT"][:], ps4(b), ALU.add, [R["TT"], PSR[b]], [R["TT"]])
                            first = False
                            yield
                            if lv < 6:
                                b = psum()
                                tr4(b, lambda hh: B["TT"][:, hh, :], [R["TT"]])
                                yield
                                evac(flat(B["DD"]), PS[:, b, :512], [PSR[b]], [R["DD"]])
                                yield
                        vtt(B["VT"][:], B["VT"][:], bc4(BETA[:, hsl]), ALU.mult, [R["VT"], SMR], [R["VT"]])
                        vtt(B["RB1"][:], B["KT"][:], bc4(BEG[:, hsl]), ALU.mult, [R["KT"], SMR], [R["RB1"]])
                        vtt(B["KT"][:], B["KT"][:], bc4(DSg[:, hsl]), ALU.mult, [R["KT"], SMR], [R["KT"]])
                        yield
                        b = psum()
                        pe4(b, lambda hh: [(B["RB1"][:, hh, :], B["TT"][:, hh, :])], [R["RB1"], R["TT"]])
                        yield
                        act(flat(B["ZT"]), PS[:, b, :512], AF.Copy, [PSR[b]], [R["ZT"]], scale=-1.0)
                        act(flat(B["D0"]), GBS[:, hsl, :].rearrange("p a b -> p (a b)"), AF.Exp, [GBSR], [R["D0"]])
                        yield
                        b = psum()
                        pe4(b, lambda hh: [(B["TT"][:, hh, :], B["VT"][:, hh, :]), (B["ZT"][:, hh, :], S[:, hs[hh], :])],
                            [R["TT"], R["VT"], R["ZT"]] + sr)
                        vtt(B["D0"][:], qT4, B["D0"][:], ALU.mult, rq + [R["D0"]], [R["D0"]])
                        yield
                        evac(flat(B["DD"]), PS[:, b, :512], [PSR[b]], [R["DD"]])
                        b2 = psum()
                        pe4(b2, lambda hh: [(QKV[:, 8 + hs[hh], o:o + 128], QKV[:, hs[hh], o:o + 128])], rk + rq)
                        yield
                        vtt(B["DTM"][:], ps4(b2), B["DTM"][:], ALU.mult, [PSR[b2], R["DTM"]], [R["DTM"]])
                        yield
                        ob = 6 + gq
                        pe4(ob, lambda hh: [(B["D0"][:, hh, :], S[:, hs[hh], :]), (B["DTM"][:, hh, :], B["DD"][:, hh, :])],
                            [R["D0"], R["DTM"], R["DD"]] + sr)
                        b = psum()
                        pe4(b, lambda hh: [(B["KT"][:, hh, :], B["DD"][:, hh, :])], [R["KT"], R["DD"]])
                        yield
                        vtt(S4, S4, bc4(CDg[:, hsl]), ALU.mult, sr + [SMR], sr)
                        vtt(S4, S4, ps4(b), ALU.add, sr + [PSR[b]], sr)

                    gens = [group_chain(gq, LB[gq], LR[gq]) for gq in range(2)]
                    for _ in range(3):
                        next(gens[0])
                    live = list(gens)
                    while live:
                        nxt = []
                        for gn in live:
                            try:
                                next(gn)
                                nxt.append(gn)
                            except StopIteration:
                                pass
                        live = nxt
                    for half in range(2):
                        evac(OG[:, half * 512:(half + 1) * 512], PS[:, 6 + half, :512], [PSR[6 + half]], [OGR])
                    if os.environ.get("DBG_GDN") and c0 == 0 and ch == 0:
                        dbg2 = dout("dbg_og", [128, 1024])
                        dma("sp", dbg2, OG[:], r=[OGR])
                    post(128, (o, o + 128), WG, WGR, o)
                    if os.environ.get("DBG_GDN") and c0 == 0 and ch == 0:
                        dbg3 = dout("dbg_og2", [128, 1024])
                        dma("sp", dbg3, OG[:], r=[OGR])
            else:
                gdn_sample(locals())
            wpool.rel()
            if ti + 1 < len(TILES128):
                norm_mod(0, 1, TILES128[ti + 1], U, UR, scratch)
            WO, WOR = wpool.get()
            for m in range(KT):
                b = psum()
                mm(b, n, [(WO[:, kk, m * 128:(m + 1) * 128], YT[:, kk, :n]) for kk in range(KT)], r=[WOR, YTR])
                resid_update(2, T, m, b, RS, RSR)
            wpool.rel()
            if c0 + n == LP:
                dma("sp", p_gdn_conv, CBUF[:].rearrange("p a b -> p (a b)"), r=[CBR])
                dma("sp", p_gdn_S, S[:].rearrange("p a b -> p (a b)"), r=SR)
        ps_state["lim"] = 8
        ps_state["base"] = 0
        k.barrier()
        mem.release(mk)

    def gdn_sample(L):
        (QKV, QR, S, SR, OG, OGR, U, UR, WAB, WABR, ROWS, XPS, XPSR, WG, WGR, post, DG, YT) = [L[q] for q in (
            "QKV", "QR", "S", "SR", "OG", "OGR", "U", "UR", "WAB", "WABR", "ROWS", "XPS", "XPSR", "WG", "WGR", "post", "DG", "YT")]
        ANEG, DTB = ROWS[:, 0:8], ROWS[:, 8:16]
        S0B = [mem.alloc([128, 8, 128], F32, "s0") for _ in range(2)]
        S0RB = [Reg("s0a"), Reg("s0b")]
        SNB = [S, mem.alloc([128, 8, 128], F32, "sn1")]
        SNRB = [SR, [Reg(f"sn1_{h}") for h in range(8)]]
        RW = mem.alloc([1, 12, 8], F32, "rw")
        RWR = Reg("rw")
        KV = DG[0:1].rearrange("p a b -> p (a b)")
        KVR = Reg("kv")
        RQ = mem.alloc([1, 1024], F32, "rq")
        RQR = Reg("rq")
        VNr = mem.alloc([1, 1024], F32, "vnr")
        VNR = Reg("vnr")
        OR_ = mem.alloc([1, 1024], F32, "orow")
        ORR = Reg("orow")
        EGB = mem.alloc([128, 8], F32, "egb")
        EGBR = Reg("egb")
        OH = mem.alloc([1, 16, 16], F32, "oh")
        OHR = Reg("oh")
        vmemset(OH[:], 0.0, [OHR])
        for t in range(NS):
            vmemset(OH[:, t, t:t + 1], 1.0, [OHR])
        ab, beta, g_, eg, neg, qk = [RW[:, q, :] for q in (0, 2, 3, 4, 5, 6)]
        ab16 = RW[:, 0:2, :].rearrange("p a b -> p (a b)")
        def load_S0(t_):
            dma("sp", S0B[t_ % 2][:].rearrange("p a b -> p (a b)"), gdn_S_in[t_], w=[S0RB[t_ % 2]])

        load_S0(0)
        for t in range(NS):
            if t + 1 < NS:
                load_S0(t + 1)
            S0, S0R = S0B[t % 2], S0RB[t % 2]
            SN, SNR = SNB[t % 2], SNRB[t % 2]
            b = psum()
            mm(b, 16, [(U[:, kk, t:t + 1], WAB[:, kk, :]) for kk in range(KT)], r=[UR, WABR], rows=1)
            vcopy(ab16, PS[:1, b, :16], [PSR[b]], [RWR])
            sigmoid_L(beta, RW[:, 1, :], [RWR], [RWR])
            vtt(g_, RW[:, 0, :], DTB[:1], ALU.add, [RWR, SMALL], [RWR])
            act(g_, g_, AF.Exp, [RWR], [RWR])
            act(g_, g_, AF.Ln, [RWR], [RWR], bias=1.0, scale=1.0)
            vtt(g_, g_, ANEG[:1], ALU.mult, [RWR, SMALL], [RWR])
            act(eg, g_, AF.Exp, [RWR], [RWR])
            vts(neg, eg, -1.0, None, ALU.mult, None, [RWR], [RWR])
            for half in range(4):
                b = psum()
                for hh in range(4):
                    m = 8 + half * 4 + hh
                    mm_ap(PS[:1, b, hh * 128:(hh + 1) * 128], PSR[b], [(QKV[:, m, t:t + 1], IDF[:])], r=[QR[m], CONST])
                evac(KV[:, half * 512:(half + 1) * 512], PS[:1, b, :512], [PSR[b]], [KVR])
            def rows_times_S0(mbase):
                for half in range(2):
                    b = psum()
                    for hh in range(4):
                        h = half * 4 + hh
                        mm_ap(PS[:1, b, hh * 128:(hh + 1) * 128], PSR[b], [(QKV[:, mbase + h, t:t + 1], S0[:, h, :])],
                              r=[QR[mbase + h], S0R])
                    evac(RQ[:, half * 512:half * 512 + 512], PS[:1, b, :512], [PSR[b]], [RQR])
            rows_times_S0(8)
            b = psum()
            for h in range(8):
                mm_ap(PS[:1, b, h:h + 1], PSR[b], [(QKV[:, h, t:t + 1], QKV[:, 8 + h, t:t + 1])], r=[QR[h], QR[8 + h]])
            vcopy(qk, PS[:1, b, :8], [PSR[b]], [RWR])
            r3 = RQ[:, 0:1024].rearrange("p (h d) -> p h d", d=128)
            qs3 = r3
            v3 = KV[:, 1024:2048].rearrange("p (h d) -> p h d", d=128)
            vn3 = VNr[:].rearrange("p (h d) -> p h d", d=128)
            o3 = OR_[:].rearrange("p (h d) -> p h d", d=128)

            def bc(row):
                return row.unsqueeze(2).to_broadcast([1, 8, 128])
            vtt(vn3, r3, bc(neg), ALU.mult, [RQR, RWR], [VNR])
            vtt(vn3, vn3, v3, ALU.add, [VNR, KVR], [VNR])
            vtt(vn3, vn3, bc(beta), ALU.mult, [VNR, RWR], [VNR])
            rows_times_S0(0)
            vtt(o3, qs3, bc(eg), ALU.mult, [RQR, RWR], [ORR])
            vtt(qs3, vn3, bc(qk), ALU.mult, [VNR, RWR], [RQR])
            vtt(o3, o3, qs3, ALU.add, [ORR, RQR], [ORR])
            for half in range(2):
                mm_ap(PS[:16, 6 + half, :512], PSR[6 + half], [(OH[:, t, :], OR_[:, half * 512:(half + 1) * 512])],
                      r=[OHR, ORR], first=(t == 0), last=(t == NS - 1))
            b = psum()
            mm(b, 8, [(ONESF[0:1, :], eg)], r=[CONST, RWR])
            vcopy(EGB[:], PS[:, b, :8], [PSR[b]], [EGBR])
            for half in range(2):
                b = psum()
                for hh in range(4):
                    h = half * 4 + hh
                    mm_ap(PS[:, b, hh * 128:(hh + 1) * 128], PSR[b], [(KV[:, h * 128:(h + 1) * 128], VNr[:, h * 128:(h + 1) * 128])],
                          r=[KVR, VNR])
                for hh in range(4):
                    h = half * 4 + hh
                    vstt(SN[:, h, :], S0[:, h, :], EGB[:, h:h + 1], PS[:, b, hh * 128:(hh + 1) * 128], ALU.mult, ALU.add,
                         [S0R, EGBR, PSR[b]], [SNR[h]])
            dma("sp", s_gdn_S[t], SN[:].rearrange("p a b -> p (a b)"), r=SNR)
        for half in range(2):
            evac(OG[:16, half * 512:(half + 1) * 512], PS[:16, 6 + half, :512], [PSR[6 + half]], [OGR])
        post(16, (0, 16), WG, WGR, 0)
        dma("sp", s_gdn_conv.rearrange("p (a bc) -> p a bc", a=24), XPS[:, :, 1:4, :].rearrange("p a b c -> p a (b c)"), r=[XPSR])

    def sched_mlp(i, nxt=None):
        for g in range(4):
            wpool.add(blk(w_up[i], 0, g * 1024))
            wpool.add(blk(w_down[i], g * 1024, 0))
            if nxt is not None:
                sched_mod(nxt, MOD_SPLIT[g])

    def do_mlp(i, nxt=None):
        mk = mem.mark()
        UA = mem.alloc([128, KT, NT], BF16, "UA")
        UAR = [Reg(f"ua{t}") for t in range(5)]
        H1 = mem.alloc([128, KT, 512], BF16, "H1")
        H1R = Reg("h1")
        scratch = std_scratch()
        SQ, SQR, RS, RSR, TMP, TMPR = scratch
        normed = [0]

        def ensure_norm(t):
            while normed[0] <= min(t, len(TILES) - 1):
                c0_, n_, _, ri_ = TILES[normed[0]]
                norm_mod(3, 4, TILES[normed[0]], UA[:, :, c0_:c0_ + n_], UAR[ri_], scratch)
                normed[0] += 1

        ensure_norm(0)
        for g in range(4):
            WU, WUR = wpool.get()
            WD, WDR = wpool.get()
            for ti, T in enumerate(TILES):
                c0, n, sample, ri = T
                ensure_norm(ti + 1)
                for m in range(KT):
                    b = psum()
                    mm(b, n, [(WU[:, kk, m * 128:(m + 1) * 128], UA[:, kk, c0:c0 + n]) for kk in range(KT)], r=[WUR, UAR[ri]])
                    s = m % 2
                    act(TMP[s][:, :n], PS[:, b, :n], AF.Relu, [PSR[b]], [TMPR[s]])
                    vtt(H1[:, m, :n], TMP[s][:, :n], TMP[s][:, :n], ALU.mult, [TMPR[s]], [H1R])
                for m in range(KT):
                    b = psum()
                    mm(b, n, [(WD[:, kk, m * 128:(m + 1) * 128], H1[:, kk, :n]) for kk in range(KT)], r=[WDR, H1R])
                    resid_update(5, T, m, b, RS, RSR)
            wpool.rel(2)
            if nxt is not None:
                do_mod(nxt, MOD_SPLIT[g])
        k.barrier()
        mem.release(mk)

    def do_final():
        mk = mem.mark()
        SQ, SQR, RS, RSR, TMP, TMPR = std_scratch()
        for T in TILES:
            c0, n, sample, ri = T
            b = sumsq(T, SQ, SQR)
            rstd(b, n, 1.0 / D, RS, RSR)
            for kk in range(KT):
                xs = X[:, kk, c0:c0 + n]
                vstt(xs, xs, FNG[:, kk:kk + 1], RS[:, :n], ALU.mult, ALU.mult, [XR[kk][ri], RSR, SMALL], [XR[kk][ri]])
        mem.release(mk)

    def write_x():
        for kk in range(KT):
            dma("sp", yT[kk * 128:(kk + 1) * 128, :], X[:, kk, :], r=XR[kk])

    nxt_of = {layers[q]: (layers[q + 1] if q + 1 < len(layers) else None) for q in range(len(layers))}
    nomlp = bool(os.environ.get("DBG_NOMLP"))
    for q, i in enumerate(layers):
        kind, j = LAYERS[i]
        if q == 0 or nomlp:
            sched_mod(i)
        if kind == "rg":
            sched_rg(j)
        elif kind == "ssd":
            sched_ssd()
        elif kind == "gdn":
            sched_gdn()
        if not nomlp:
            sched_mlp(i, nxt_of[i])
    for q, i in enumerate(layers):
        kind, j = LAYERS[i]
        if q == 0 or nomlp:
            do_mod(i)
        mod_cur["i"] = i % 2
        if kind == "rg":
            do_rg(i, j)
        elif kind == "ssd":
            do_ssd(i)
        elif kind == "gdn":
            do_gdn(i)
        if not nomlp:
            do_mlp(i, nxt_of[i])
    if final:
        do_final()
    write_x()
    k.finish()
    _NC_CACHE["k"] = k
    k.emit()
    return nc


def _fm(v):
    v = np.asarray(v, np.float32)
    return np.ascontiguousarray(v.reshape(-1, 128).T)


def _conv_state(st, ntile):
    tok = st.shape[0]
    st = st.reshape(tok, 3, ntile, 128).transpose(3, 2, 1, 0)
    return np.ascontiguousarray(st.reshape(128, -1))


def make_in_maps(inp, x_override=None, layers=(0, 1, 2, 3)):
    have = {LAYERS[i][0] for i in layers}
    maps = []
    x_prompt = inp["x_prompt"]
    x_sample = inp["x_sample"][:, 0, :]
    bm = np.concatenate([_fm(inp["b_mod"][i]) for i in range(4)], axis=1)
    shared = {
        "w_mod": np.ascontiguousarray(inp["w_mod"]), "b_mod_t": np.ascontiguousarray(bm),
        "w_up": np.ascontiguousarray(inp["w_mlp_up"]), "w_down": np.ascontiguousarray(inp["w_mlp_down"]),
        "fng": _fm(inp["final_norm_g"]),
    }
    if "rg" in have:
        rgv = np.zeros((128, 2, 8, 8), np.float32)
        for j in range(2):
            for q in range(4):
                rgv[:, j, q, :] = _fm(inp["rg_conv_w"][j, q])
            rgv[:, j, 4, :] = _fm(inp["rg_conv_b"][j])
            rgv[:, j, 5, :] = _fm(inp["rg_gate_b"][j, 0])
            rgv[:, j, 6, :] = _fm(inp["rg_gate_b"][j, 1])
            rgv[:, j, 7, :] = _fm(inp["rg_lambda"][j])
        shared.update({"rg_w_in": np.ascontiguousarray(inp["rg_w_in"]), "rg_w_out": np.ascontiguousarray(inp["rg_w_out"]),
                       "rg_gate_w": np.ascontiguousarray(inp["rg_gate_w"]), "rg_vec": np.ascontiguousarray(rgv.reshape(128, -1))})
    if "ssd" in have:
        sv = np.zeros((128, 5, 24), np.float32)
        for q in range(4):
            sv[:, q, :] = _fm(inp["ssd_conv_w"][0, q])
        sv[:, 4, :] = _fm(inp["ssd_conv_b"][0])
        hv = np.stack([inp["ssd_A_log"][0], inp["ssd_dt_bias"][0], inp["ssd_D"][0]], 1).astype(np.float32)
        rows = np.concatenate([np.tile(inp["ssd_A_log"][0][None, :], (128, 1)), np.tile(inp["ssd_D"][0][None, :], (128, 1))], 1)
        cols = np.concatenate([_fm(inp["ssd_norm_g"][0]), _fm(np.repeat(inp["ssd_D"][0], 64))], 1)
        shared.update({"ssd_w_in": np.ascontiguousarray(inp["ssd_w_in"][0]), "ssd_w_out": np.ascontiguousarray(inp["ssd_w_out"][0]),
                       "ssd_vec": np.ascontiguousarray(sv.reshape(128, -1)), "ssd_hvec": np.ascontiguousarray(hv),
                       "ssd_rows": np.ascontiguousarray(rows.astype(np.float32)), "ssd_cols": np.ascontiguousarray(cols)})
    if "gdn" in have:
        gv = np.zeros((128, 4, 24), np.float32)
        for q in range(4):
            gv[:, q, :] = _fm(inp["gdn_conv_w"][0, q])
        rows = np.concatenate([inp["gdn_A_log"][0], inp["gdn_dt_bias"][0], inp["gdn_norm_g"][0]]).astype(np.float32)
        ii, jj = np.meshgrid(np.arange(128), np.arange(128), indexing="ij")
        msk = np.stack([((ii >> (lv + 1)) == (jj >> (lv + 1))) & (((ii >> lv) & 1) == 1) & (((jj >> lv) & 1) == 0)
                        for lv in range(7)], 1).astype(np.float32)
        shared.update({"gdn_w_in": np.ascontiguousarray(inp["gdn_w_in"][0]), "gdn_w_out": np.ascontiguousarray(inp["gdn_w_out"][0]),
                       "gdn_vec": np.ascontiguousarray(gv.reshape(128, -1)),
                       "gdn_rows": np.ascontiguousarray(np.tile(rows[None, :], (128, 1))),
                       "gdn_masks": np.ascontiguousarray(msk.reshape(128, -1))})
    for c in range(NCORES):
        s0 = c * NS
        m = dict(shared)
        if x_override is not None:
            xp_, xs_ = x_override
            m["xT"] = np.ascontiguousarray(np.concatenate([xp_[c].T, xs_[s0:s0 + NS].T], axis=1).astype(np.float32))
        else:
            m["xT"] = np.ascontiguousarray(np.concatenate([x_prompt[c].T, x_sample[s0:s0 + NS].T], axis=1))
        m["cT"] = np.ascontiguousarray(np.concatenate([inp["c_prompt"][c][:, None], inp["c_sample"][s0:s0 + NS].T], axis=1))
        if "rg" in have:
            m["rg_conv_in"] = np.stack([_conv_state(inp["state_rglru_conv"][j, s0:s0 + NS], 8) for j in range(2)], 0)
            hh = inp["state_rglru_h"][:, s0:s0 + NS].reshape(2, NS, 8, 128).transpose(0, 3, 2, 1)
            m["rg_h_in"] = np.ascontiguousarray(hh.reshape(2, 128, -1))
        if "ssd" in have:
            m["ssd_conv_in"] = _conv_state(inp["state_ssd_conv"][0, s0:s0 + NS], 24)
            m["ssd_h_in"] = np.ascontiguousarray(inp["state_ssd_h"][0, s0:s0 + NS].reshape(NS, 2048, 128))
        if "gdn" in have:
            m["gdn_conv_in"] = _conv_state(inp["state_gdn_conv"][0, s0:s0 + NS], 24)
            m["gdn_S_in"] = np.ascontiguousarray(inp["state_gdn_S"][0, s0:s0 + NS].transpose(0, 2, 1, 3).reshape(NS, 128, 1024))
        maps.append(m)
    return maps


_NC_CACHE = {}


def kernel(**inputs):
    inp = {k_: np.asarray(v) for k_, v in inputs.items()}
    if "nc" not in _NC_CACHE:
        _NC_CACHE["nc"] = build_program()
    nc = _NC_CACHE["nc"]
    maps = make_in_maps(inp)
    res = run_bass_kernel_spmd(nc, maps, core_ids=list(range(NCORES)))
    return assemble(res.results)


def _conv_out_p(a, ntile):
    return a.reshape(128, ntile, 3).transpose(2, 1, 0).reshape(3, ntile * 128)


def _conv_out_s(a, ntile):
    return a.reshape(128, ntile, 3, NS).transpose(3, 2, 1, 0).reshape(NS, 3, ntile * 128)


def assemble(rs):
    B = len(rs)
    f = lambda a: np.ascontiguousarray(a, dtype=np.float32)
    y_prompt = np.stack([rs[c]["yT"][:, :LP].T for c in range(B)], 0)
    y_sample = np.concatenate([rs[c]["yT"][:, LP:].T for c in range(B)], 0)[:, None, :]
    p_rg_conv = np.stack([np.stack([_conv_out_p(rs[c]["p_rg_conv"][j], 8) for j in range(2)], 0) for c in range(B)], 1)
    p_rg_h = np.stack([rs[c]["p_rg_h"].reshape(2, 128, 8).transpose(0, 2, 1).reshape(2, 1024) for c in range(B)], 1)
    s_rg_conv = np.concatenate([np.stack([_conv_out_s(rs[c]["s_rg_conv"][j], 8) for j in range(2)], 0) for c in range(B)], 1)
    s_rg_h = np.concatenate([rs[c]["s_rg_h"].reshape(2, 128, 8, NS).transpose(0, 3, 2, 1).reshape(2, NS, 1024) for c in range(B)], 1)
    p_gdn_conv = np.stack([_conv_out_p(rs[c]["p_gdn_conv"], 24) for c in range(B)], 0)[None]
    s_gdn_conv = np.concatenate([_conv_out_s(rs[c]["s_gdn_conv"], 24) for c in range(B)], 0)[None]
    p_gdn_S = np.stack([rs[c]["p_gdn_S"].reshape(128, 8, 128).transpose(1, 0, 2) for c in range(B)], 0)[None]
    s_gdn_S = np.concatenate([rs[c]["s_gdn_S"].reshape(NS, 128, 8, 128).transpose(0, 2, 1, 3) for c in range(B)], 0)[None]
    p_ssd_conv = np.stack([_conv_out_p(rs[c]["p_ssd_conv"], 24) for c in range(B)], 0)[None]
    s_ssd_conv = np.concatenate([_conv_out_s(rs[c]["s_ssd_conv"], 24) for c in range(B)], 0)[None]
    p_ssd_h = np.stack([rs[c]["p_ssd_hT"].reshape(128, 32, 64).transpose(1, 2, 0) for c in range(B)], 0)[None]
    s_ssd_h = np.concatenate([rs[c]["s_ssd_h"].reshape(NS, 32, 64, 128) for c in range(B)], 0)[None]
    return (f(y_prompt), f(y_sample), f(p_rg_conv), f(p_rg_h), f(p_gdn_conv), f(p_gdn_S), f(p_ssd_conv), f(p_ssd_h),
            f(s_rg_conv), f(s_rg_h), f(s_gdn_conv), f(s_gdn_S), f(s_ssd_conv), f(s_ssd_h))
```
